# Optimizing a Trainium2 kernel written in Bass

```python
import math
import jax, jax.numpy as jnp
from jax import lax
import numpy as np

D_MODEL = 2048
BATCH = 2
SEQ = 8192
DEPTH = 1

HEAD_DIM = 128
DIL_GROUPS = ((128, 1), (512, 4), (2048, 16))
N_DIL_GROUPS = 3
HEADS_PER_DIL = 4
N_HEADS_A = N_DIL_GROUPS * HEADS_PER_DIL
N_Q_B = 8
N_KV_B = 2
Q_PER_KV = N_Q_B // N_KV_B
GRID_W = 64
ROPE_THETA = 10000.0
ALIBI_MAX = 8.0
D_FF = 5632
N_MOD = 9
Q_BLOCK = 128
EPS = 1e-6

D_A = N_HEADS_A * HEAD_DIM
D_A_OUT = HEADS_PER_DIL * HEAD_DIM
D_QB = N_Q_B * HEAD_DIM
D_KVB = N_KV_B * HEAD_DIM
SPLIT_POINTS = (D_A, 2 * D_A, 3 * D_A,
                3 * D_A + D_QB, 3 * D_A + D_QB + D_KVB, 3 * D_A + D_QB + 2 * D_KVB,
                3 * D_A + D_QB + 2 * D_KVB + D_MODEL)
D_IN = 3 * D_A + D_QB + 2 * D_KVB + 2 * D_MODEL

kernel_name = "hybrid_dilated_axial_gqa_macaron_adaln"


def rms_norm(x, g):
    xf = x.astype(jnp.float32)
    y = xf * lax.rsqrt(jnp.mean(xf * xf, axis=-1, keepdims=True) + EPS)
    return (y * g.astype(jnp.float32)).astype(x.dtype)


def modulate(x, shift, scale):
    return x * (1.0 + scale[:, None, :]) + shift[:, None, :]


def swiglu(x, w1, w3, w2):
    return (jax.nn.silu(x @ w1) * (x @ w3)) @ w2


def axial_rope_tables(s):
    rows = s // GRID_W
    row = jnp.repeat(jnp.arange(rows), GRID_W).astype(jnp.float32)
    col = jnp.tile(jnp.arange(GRID_W), rows).astype(jnp.float32)
    half = HEAD_DIM // 2
    inv_freq = ROPE_THETA ** (-jnp.arange(0, half, 2, dtype=jnp.float32) / half)
    ang = jnp.concatenate([row[:, None] * inv_freq, col[:, None] * inv_freq], axis=-1)
    return jnp.cos(ang), jnp.sin(ang)


def apply_rope(x, cos, sin):
    xf = x.astype(jnp.float32)
    half = HEAD_DIM // 2
    x1, x2 = xf[..., :half], xf[..., half:]
    c = cos[None, :, None, :]
    s = sin[None, :, None, :]
    return jnp.concatenate([x1 * c - x2 * s, x1 * s + x2 * c], axis=-1).astype(x.dtype)


def dilated_attention(qa, ka, va, slopes):
    b, s = qa.shape[0], qa.shape[1]
    nblk = s // Q_BLOCK
    scale = HEAD_DIM ** -0.5
    k_groups = [ka[:, :, g] for g in range(N_DIL_GROUPS)]
    v_groups = [va[:, :, g] for g in range(N_DIL_GROUPS)]

    def block(i):
        start = i * Q_BLOCK
        t = start + jnp.arange(Q_BLOCK)
        q_blk = lax.dynamic_slice_in_dim(qa, start, Q_BLOCK, axis=1)
        outs, lses = [], []
        for g, (window, dil) in enumerate(DIL_GROUPS):
            half = window // (2 * dil)
            n_keys = 2 * half + 1
            offs = dil * jnp.arange(-half, half + 1)
            idx = t[:, None] + offs[None, :]
            valid = (idx >= 0) & (idx < s)
            flat = jnp.clip(idx, 0, s - 1).reshape(-1)
            k_sel = jnp.take(k_groups[g], flat, axis=1).reshape(b, Q_BLOCK, n_keys, HEADS_PER_DIL, HEAD_DIM)
            v_sel = jnp.take(v_groups[g], flat, axis=1).reshape(b, Q_BLOCK, n_keys, HEADS_PER_DIL, HEAD_DIM)
            scores = jnp.einsum('bqhd,bqjhd->bhqj', q_blk[:, :, g], k_sel,
                                preferred_element_type=jnp.float32) * scale
            scores = scores - slopes[g][None, :, None, None] * jnp.abs(offs).astype(jnp.float32)[None, None, None, :]
            scores = jnp.where(valid[None, None], scores, -jnp.inf)
            lse = jax.nn.logsumexp(scores, axis=-1)
            p = jnp.exp(scores - lse[..., None])
            o = jnp.einsum('bhqj,bqjhd->bqhd', p.astype(va.dtype), v_sel,
                           preferred_element_type=jnp.float32)
            outs.append(o)
            lses.append(lse)
        lse_all = jnp.stack(lses, axis=0)
        wts = jax.nn.softmax(lse_all, axis=0)
        wts = jnp.transpose(wts, (0, 1, 3, 2))[..., None]
        o = jnp.sum(wts * jnp.stack(outs, axis=0), axis=0)
        return o.astype(qa.dtype)

    o = lax.map(block, jnp.arange(nblk))
    return jnp.transpose(o, (1, 0, 2, 3, 4)).reshape(b, s, D_A_OUT)


def gqa_attention(qb, kb, vb):
    b, s = qb.shape[0], qb.shape[1]
    nblk = s // Q_BLOCK
    scale = HEAD_DIM ** -0.5
    q_blocks = jnp.moveaxis(qb.reshape(b, nblk, Q_BLOCK, N_KV_B, Q_PER_KV, HEAD_DIM), 1, 0)

    def block(q_blk):
        scores = jnp.einsum('bqkgd,bskd->bkgqs', q_blk, kb,
                            preferred_element_type=jnp.float32) * scale
        p = jax.nn.softmax(scores, axis=-1)
        return jnp.einsum('bkgqs,bskd->bqkgd', p.astype(vb.dtype), vb)

    o = lax.map(block, q_blocks)
    return jnp.moveaxis(o, 0, 1).reshape(b, s, D_QB)


def token_mixing(u, w_in, q_norm_a, k_norm_a, q_norm_b, k_norm_b,
                 w_branch_a, w_branch_b, w_out, cos, sin, slopes):
    b, s, _ = u.shape
    proj = u @ w_in
    qa, ka, va, qb, kb, vb, ga, gb = jnp.split(proj, SPLIT_POINTS, axis=-1)
    qa = rms_norm(qa.reshape(b, s, N_DIL_GROUPS, HEADS_PER_DIL, HEAD_DIM), q_norm_a)
    ka = rms_norm(ka.reshape(b, s, N_DIL_GROUPS, HEADS_PER_DIL, HEAD_DIM), k_norm_a)
    va = va.reshape(b, s, N_DIL_GROUPS, HEADS_PER_DIL, HEAD_DIM)
    out_a = dilated_attention(qa, ka, va, slopes)
    qb = apply_rope(rms_norm(qb.reshape(b, s, N_Q_B, HEAD_DIM), q_norm_b), cos, sin)
    kb = apply_rope(rms_norm(kb.reshape(b, s, N_KV_B, HEAD_DIM), k_norm_b), cos, sin)
    vb = vb.reshape(b, s, N_KV_B, HEAD_DIM)
    out_b = gqa_attention(qb.reshape(b, s, N_KV_B, Q_PER_KV, HEAD_DIM), kb, vb)
    merged = jax.nn.sigmoid(ga) * (out_a @ w_branch_a) + jax.nn.sigmoid(gb) * (out_b @ w_branch_b)
    return merged @ w_out


def setup_inputs(seed: int = 0) -> dict:
    key = jax.random.key(seed)
    ks = jax.random.split(key, 24)
    f32 = jnp.float32

    def dense(k, shape, fan_in, mult=1.0):
        return jax.random.normal(k, shape, f32) * (mult * fan_in ** -0.5)

    def gain(k, shape):
        return 1.0 + 0.02 * jax.random.normal(k, shape, f32)

    L = DEPTH
    return {
        "x": jax.random.normal(ks[0], (BATCH, SEQ, D_MODEL), f32),
        "c": jax.random.normal(ks[1], (BATCH, D_MODEL), f32),
        "w_ada": dense(ks[2], (L, D_MODEL, N_MOD * D_MODEL), D_MODEL, 0.5),
        "b_ada": 0.01 * jax.random.normal(ks[3], (L, N_MOD * D_MODEL), f32),
        "norm_ffn1": gain(ks[4], (L, D_MODEL)),
        "w1_ffn1": dense(ks[5], (L, D_MODEL, D_FF), D_MODEL),
        "w3_ffn1": dense(ks[6], (L, D_MODEL, D_FF), D_MODEL),
        "w2_ffn1": dense(ks[7], (L, D_FF, D_MODEL), D_FF),
        "norm_mix": gain(ks[8], (L, D_MODEL)),
        "w_in": dense(ks[9], (L, D_MODEL, D_IN), D_MODEL),
        "q_norm_a": gain(ks[10], (L, HEAD_DIM)),
        "k_norm_a": gain(ks[11], (L, HEAD_DIM)),
        "q_norm_b": gain(ks[12], (L, HEAD_DIM)),
        "k_norm_b": gain(ks[13], (L, HEAD_DIM)),
        "w_branch_a": dense(ks[14], (L, D_A_OUT, D_MODEL), D_A_OUT),
        "w_branch_b": dense(ks[15], (L, D_QB, D_MODEL), D_QB),
        "w_out": dense(ks[16], (L, D_MODEL, D_MODEL), D_MODEL),
        "norm_ffn2": gain(ks[17], (L, D_MODEL)),
        "w1_ffn2": dense(ks[18], (L, D_MODEL, D_FF), D_MODEL),
        "w3_ffn2": dense(ks[19], (L, D_MODEL, D_FF), D_MODEL),
        "w2_ffn2": dense(ks[20], (L, D_FF, D_MODEL), D_FF),
        "norm_final": gain(ks[21], (D_MODEL,)),
    }


def reference(x, c, w_ada, b_ada, norm_ffn1, w1_ffn1, w3_ffn1, w2_ffn1, norm_mix, w_in,
              q_norm_a, k_norm_a, q_norm_b, k_norm_b, w_branch_a, w_branch_b, w_out,
              norm_ffn2, w1_ffn2, w3_ffn2, w2_ffn2, norm_final):
    s = x.shape[1]
    cos, sin = axial_rope_tables(s)
    slopes = jnp.exp2(-ALIBI_MAX * jnp.arange(1, N_HEADS_A + 1, dtype=jnp.float32) / N_HEADS_A)
    slopes = slopes.reshape(N_DIL_GROUPS, HEADS_PER_DIL)
    c_act = jax.nn.silu(c)
    h = x
    for l in range(DEPTH):
        mod = c_act @ w_ada[l] + b_ada[l]
        sh1, sc1, g1, sh2, sc2, g2, sh3, sc3, g3 = jnp.split(mod, N_MOD, axis=-1)
        u = modulate(rms_norm(h, norm_ffn1[l]), sh1, sc1)
        h = h + 0.5 * g1[:, None, :] * swiglu(u, w1_ffn1[l], w3_ffn1[l], w2_ffn1[l])
        u = modulate(rms_norm(h, norm_mix[l]), sh2, sc2)
        h = h + g2[:, None, :] * token_mixing(u, w_in[l], q_norm_a[l], k_norm_a[l], q_norm_b[l],
                                              k_norm_b[l], w_branch_a[l], w_branch_b[l], w_out[l],
                                              cos, sin, slopes)
        u = modulate(rms_norm(h, norm_ffn2[l]), sh3, sc3)
        h = h + 0.5 * g3[:, None, :] * swiglu(u, w1_ffn2[l], w3_ffn2[l], w2_ffn2[l])
    return rms_norm(h, norm_final)
```

```python
import math
import os
from contextlib import ExitStack

import numpy as np
import ml_dtypes

import concourse.bass as bass
import concourse.mybir as mybir
from concourse.bass_utils import run_bass_kernel_spmd

F32 = mybir.dt.float32
BF16 = mybir.dt.bfloat16
AF = mybir.ActivationFunctionType
ALU = mybir.AluOpType

NCORES = 8
D = 2048
S = 8192
TOK = 2048
T = 512
NT = TOK // T
DC = D // 128
DFF = 5632
NJ = DFF // 128
HD = 128
EPS = 1e-6
SCALE = HD ** -0.5
NEG = -30000.0

FM_COLS = 1536 + 1536 + 2048 + 512 + 2048 + 2048
TM_COLS = 1536 + 256
WIN_COLS = FM_COLS + TM_COLS

DMAXP = (4, 5, 11)
DMAX = (1, 2, 8)
EW = tuple((2 * d + 1) * 128 for d in DMAXP)
EOFF = (0, EW[0], EW[0] + EW[1])
ETOT = sum(EW)


class _Stop(Exception):
    pass


STOP = os.environ.get("KSTOP", "")


class Sem:
    def __init__(self, nc, name):
        self.s = nc.alloc_semaphore(name)
        self.n = 0
        self.waited = {}

    def inc(self, ins, by=1):
        ins.then_inc(self.s, by)
        self.n += by
        return self.n

    def wait(self, eng, val=None):
        v = self.n if val is None else val
        if v <= 0:
            return
        k = id(eng)
        if self.waited.get(k, 0) >= v:
            return
        eng.wait_ge(self.s, v)
        self.waited[k] = v


class G:
    nc = None
    PROG = {}
    SIG = {}
    DSEM = {}

    @staticmethod
    def reset(nc):
        G.nc = nc
        G.PROG = {}
        G.SIG = {}
        G.DSEM = {}

    @staticmethod
    def dsem(name):
        if name not in G.DSEM:
            G.DSEM[name] = Sem(G.nc, name + "_d")
        return G.DSEM[name]

    @staticmethod
    def signal(ins, dname):
        k = id(ins)
        if k in G.SIG:
            return G.SIG[k][1]
        txt = str(ins.ins)[:24]
        eng = txt.split()[0]
        if "DMA" in txt:
            sem = G.dsem(dname)
            ev = (sem, sem.inc(ins, 16))
        else:
            if eng not in G.PROG:
                G.PROG[eng] = Sem(G.nc, "prog_" + eng)
            sem = G.PROG[eng]
            ev = (sem, sem.inc(ins, 1))
        G.SIG[k] = (ins, ev)
        return ev


class Evs:
    def __init__(self, name):
        self.name = name
        self.m = {}

    def add(self, ins, by=None):
        sem, v = G.signal(ins, self.name)
        self.m[sem] = max(self.m.get(sem, 0), v)

    inc = add

    def wait(self, eng):
        for sem, v in self.m.items():
            sem.wait(eng, v)


class Cur:
    def __init__(self, bufs):
        self.__dict__["bufs"] = bufs
        self.__dict__["i"] = 0

    def sel(self, i):
        self.__dict__["i"] = i

    def __getattr__(self, name):
        return getattr(self.bufs[self.i], name)


class Buf:
    def __init__(self, nc, name, ap):
        self.ap = ap
        self.rd = Evs(name)
        self.fr = Evs(name)

    def acquire(self, eng):
        self.fr.wait(eng)
        self.rd.wait(eng)

    def produced(self, ins, by=None):
        self.rd.add(ins)

    def wait_ready(self, eng):
        self.rd.wait(eng)

    def consumed(self, ins, by=None):
        self.fr.add(ins)


def build_program():
    holder = {}
    try:
        _build_program(holder)
    except _Stop:
        pass
    return holder["nc"]


def _build_program(holder):
    nc = bass.Bass("TRN2", target_bir_lowering=False)
    G.reset(nc)
    holder["nc"] = nc
    PE, ACT, DVE, SP, GP = nc.tensor, nc.scalar, nc.vector, nc.sync, nc.gpsimd

    def din(name, shape, dt=F32):
        return nc.dram_tensor(name, list(shape), dt, kind="ExternalInput")

    def dscr(name, shape, dt):
        return nc.dram_tensor(name, list(shape), dt)

    xT = din("xT", [D, TOK])
    cT = din("cT", [128, DC])
    wada = din("wada", [D, 4608])
    bada = din("bada", [128, 36])
    gtab_d = din("gtab", [128, 4 * DC])
    qkg_d = din("qkg", [128, 8])
    w1a = din("w1a", [D, DFF]); w3a = din("w3a", [D, DFF]); w2a = din("w2a", [DFF, D])
    w1b = din("w1b", [D, DFF]); w3b = din("w3b", [D, DFF]); w2b = din("w2b", [DFF, D])
    win = din("win", [D, WIN_COLS])
    wpa = din("wpa", [512, D]); wpb = din("wpb", [1024, D]); wout = din("wout", [D, D])
    ropeC_d = din("ropeC", [128, TOK]); ropeS_d = din("ropeS", [128, TOK])
    etab_d = din("etab", [4, 128, ETOT], BF16)
    vtab_d = din("vtab", [128, 32])
    yT = nc.dram_tensor("yT", [D, TOK], F32, kind="ExternalOutput")

    modin = dscr("modin", [128, 36], F32)
    modout = dscr("modout", [512, 36], F32)
    h1T = dscr("h1T", [D, TOK], F32)
    qaT = dscr("qaT", [1536, TOK], BF16)
    kaTc = [dscr(f"kaT{c}", [256, TOK], BF16) for c in range(6)]
    kaGc = [dscr(f"kaG{c}", [1024, TOK], BF16) for c in range(6)]
    vaLc = [dscr(f"vaL{c}", [256, 1536], BF16) for c in range(8)]
    vaGc = [dscr(f"vaG{c}", [1024, 1536], BF16) for c in range(8)]
    qbT = dscr("qbT", [1024, TOK], BF16)
    kbTt = [dscr(f"kbT{t}", [256, T], BF16) for t in range(NT)]
    kbGt = [dscr(f"kbG{t}", [4 * 256, T], BF16) for t in range(NT)]
    vbLt = [dscr(f"vbL{t}", [T, 256], BF16) for t in range(NT)]
    vbGt = [dscr(f"vbG{t}", [4 * T, 256], BF16) for t in range(NT)]
    sgT = dscr("sgT", [2 * D, TOK], BF16)
    hKp = dscr("hKp", [1536, 1, 1024], BF16)
    hKn = dscr("hKn", [1536, 1, 1024], BF16)
    hVp = dscr("hVp", [1024, 1, 1536], BF16)
    hVn = dscr("hVn", [1024, 1, 1536], BF16)
    oaT = dscr("oaT", [512, TOK], BF16)
    obT = dscr("obT", [1024, TOK], BF16)

    def wview(w):
        return w.ap().rearrange("(kc p) n -> p kc n", p=128)

    with ExitStack() as es:
        def sb(name, shape, dt):
            return es.enter_context(nc.sbuf_tensor(name, list(shape), dt))

        ones = sb("ones", [128, 128], BF16)
        ones_f = sb("ones_f", [128, 128], F32)
        gtab = sb("gtabs", [128, 4 * DC], F32)
        qkg = sb("qkgs", [128, 8], F32)
        vtab = sb("vtabs", [128, 32], F32)
        modT = sb("modT", [128, 144], F32)
        tabA = sb("tabA", [128, 3 * DC], F32)
        tabG = sb("tabG", [128, 3 * DC], F32)
        NSLOT = 3
        slots = [Buf(nc, f"slot{i}", sb(f"slot{i}", [128, 8192], BF16)) for i in range(NSLOT)]
        psum_all = nc.alloc_psum_tensor("psum_all", [128, 4096], F32)
        banks = [Buf(nc, f"bank{i}", psum_all[:, i * 512:(i + 1) * 512]) for i in range(8)]

        ld = Evs("ld")
        ldr = Evs("ldr")
        ldb = Evs("ldb")
        st = Evs("st")
        cc = Sem(nc, "cc")
        ccb = Sem(nc, "ccb")
        cca = Sem(nc, "cca")
        groups = [[0, 1, 2, 3], [4, 5, 6, 7]]
        misc = Evs("misc")

        slot_ctr = [0]

        def wstage(parts):
            sl = slots[slot_ctr[0] % NSLOT]
            slot_ctr[0] += 1
            sl.acquire(GP)
            for vf, src in parts:
                sl.produced(GP.dma_start(out=vf(sl.ap), in_=src), 16)
            for d in list(deferred):
                d[0] -= 1
                if d[0] <= 0:
                    deferred.remove(d)
                    d[1]()
            return sl

        deferred = []

        def flush_deferred():
            for d in list(deferred):
                deferred.remove(d)
                d[1]()

        ld.inc(SP.dma_start(out=gtab[:, :], in_=gtab_d[:, :]), 16)
        ld.inc(SP.dma_start(out=qkg[:, :], in_=qkg_d[:, :]), 16)
        ld.inc(SP.dma_start(out=vtab[:, :], in_=vtab_d[:, :]), 16)
        misc.inc(DVE.memset(ones[:, :], 1.0))
        misc.inc(DVE.memset(ones_f[:, :], 1.0))
        for e in (PE, ACT, DVE):
            ld.wait(e)
            misc.wait(e)

        with ExitStack() as e0:
            cts = e0.enter_context(nc.sbuf_tensor("cts", [128, DC], F32))
            cact = e0.enter_context(nc.sbuf_tensor("cact", [128, DC], BF16))
            bad = e0.enter_context(nc.sbuf_tensor("bads", [128, 36], F32))
            modp = e0.enter_context(nc.sbuf_tensor("modp", [128, 36], F32))
            ld0 = Evs("ld0")
            ld0.inc(SP.dma_start(out=cts[:, :], in_=cT[:, :]), 16)
            ld0.inc(SP.dma_start(out=bad[:, :], in_=bada[:, :]), 16)
            ld0.wait(ACT)
            misc.inc(ACT.activation(out=cact[:, :], in_=cts[:, :], func=AF.Silu))
            misc.wait(PE)
            wv = wview(wada)
            bk = banks[7]
            bk.acquire(PE)
            last = None
            for s_ in range(9):
                sl = wstage([(lambda a: a[:, :].rearrange("p (k n) -> p k n", n=512),
                              wv[:, :, s_ * 512:(s_ + 1) * 512])])
                sl.wait_ready(PE)
                sv = sl.ap[:, :].rearrange("p (k n) -> p k n", n=512)
                for i in range(4):
                    col = s_ * 4 + i
                    for kc in range(DC):
                        last = PE.matmul(bk.ap[:, col:col + 1], lhsT=sv[:, kc, i * 128:(i + 1) * 128],
                                         rhs=cact[:, kc:kc + 1], start=(kc == 0), stop=(kc == DC - 1))
                sl.consumed(last)
            bk.produced(last)
            bk.wait_ready(DVE)
            ld0.wait(DVE)
            i_ = DVE.tensor_tensor(out=modp[:, :], in0=bk.ap[:, 0:36], in1=bad[:, :], op=ALU.add)
            bk.consumed(i_)
            misc.inc(i_)
            misc.wait(SP)
            st.inc(SP.dma_start(out=modin[:, :], in_=modp[:, :]), 16)
            st.wait(GP)
            cc.inc(GP.collective_compute("AllGather", ALU.bypass, replica_groups=[[0, 1, 2, 3], [4, 5, 6, 7]],
                                         ins=[modin.ap().opt()], outs=[modout.ap().opt()]))
            cc.wait(SP)
            ldm = Evs("ldm")
            ldm.inc(SP.dma_start(out=modT[:, :].rearrange("p (r c) -> p r c", c=36),
                                 in_=modout.ap().rearrange("(r p) c -> p r c", p=128)), 16)
            ldm.wait(DVE)
            for k in range(3):
                sc = modT[:, (3 * k + 1) * DC:(3 * k + 2) * DC]
                g = modT[:, (3 * k + 2) * DC:(3 * k + 3) * DC]
                misc.inc(DVE.scalar_tensor_tensor(out=tabA[:, k * DC:(k + 1) * DC], in0=sc, scalar=1.0,
                                                  in1=gtab[:, k * DC:(k + 1) * DC], op0=ALU.add, op1=ALU.mult))
                misc.inc(DVE.tensor_scalar(out=tabG[:, k * DC:(k + 1) * DC], in0=g,
                                           scalar1=(1.0 if k == 1 else 0.5), scalar2=None, op0=ALU.mult))
            for e in (ACT, DVE, PE):
                misc.wait(e)

        if STOP == "p0":
            raise _Stop()

        def tabB(k):
            return modT[:, (3 * k) * DC:(3 * k + 1) * DC]

        def run_token_phases(first):
            with ExitStack() as e1:
                def sbl(name, shape, dt):
                    return e1.enter_context(nc.sbuf_tensor(name + ("A" if first else "B"), list(shape), dt))

                xt = Buf(nc, "xt", sbl("xt", [128, DC, T], F32))
                class _Cur:
                    def __init__(self, bufs):
                        self.__dict__["bufs"] = bufs
                        self.__dict__["i"] = 0

                    def sel(self, i):
                        self.__dict__["i"] = i

                    def __getattr__(self, name):
                        return getattr(self.bufs[self.i], name)
                _u0 = Buf(nc, "u", sbl("u", [128, DC, T], BF16))
                _u1 = Buf(nc, "u2", sbl("u2", [128, DC, T], BF16)) if first else _u0
                u = _Cur([_u0, _u1])
                hid = Buf(nc, "hid", sbl("hid", [128, NJ, T], BF16))
                rstd = Buf(nc, "rstd", sbl("rstd", [128, T], F32))
                lnt = Buf(nc, "lnt", sbl("lnt", [128, T], F32))
                sqs = [Buf(nc, f"sq{i}", sbl(f"sq{i}", [128, T], BF16)) for i in range(4)]
                tmps = [Buf(nc, f"tmp{i}", sbl(f"tmp{i}", [128, T], F32)) for i in range(3)]
                sils = [Buf(nc, f"sil{i}", sbl(f"sil{i}", [128, T], F32)) for i in range(2)]
                osts = [Buf(nc, f"ost{i}", sbl(f"ost{i}", [128, T], BF16)) for i in range(4)]
                ost_c = [0]

                def rms_rstd(src_bank_or_none, src_chunks, nfeat, statbank, split=False):
                    n = len(src_chunks)
                    statbank.acquire(PE)
                    for i, (ap, wfn, cfn) in enumerate(src_chunks):
                        sq = sqs[i % 4]
                        if split and i % 2 == 1:
                            sq.acquire(DVE)
                            wfn(DVE)
                            a = DVE.tensor_tensor(out=sq.ap[:, :], in0=ap, in1=ap, op=ALU.mult)
                        else:
                            sq.acquire(ACT)
                            wfn(ACT)
                            a = ACT.activation(out=sq.ap[:, :], in_=ap, func=AF.Square)
                        sq.produced(a)
                        cfn(a)
                        sq.wait_ready(PE)
                        m = PE.matmul(statbank.ap[:, :], lhsT=ones[:, :], rhs=sq.ap[:, :],
                                      start=(i == 0), stop=(i == n - 1))
                        sq.consumed(m)
                    statbank.produced(m)
                    statbank.wait_ready(ACT)
                    lnt.acquire(ACT)
                    a = ACT.activation(out=lnt.ap[:, :], in_=statbank.ap[:, :], func=AF.Ln,
                                       scale=1.0 / nfeat, bias=EPS)
                    statbank.consumed(a)
                    lnt.produced(a)
                    lnt.wait_ready(ACT)
                    rstd.acquire(ACT)
                    a = ACT.activation(out=rstd.ap[:, :], in_=lnt.ap[:, :], func=AF.Exp, scale=-0.5)
                    lnt.consumed(a)
                    rstd.produced(a)

                uch = [[None] * DC, [None] * DC]

                def norm_mod(k):
                    chunks = [(xt.ap[:, dc, :], xt.wait_ready, xt.consumed) for dc in range(DC)]
                    rms_rstd(None, chunks, D, banks[6], split=True)
                    rstd.wait_ready(DVE)
                    xt.wait_ready(DVE)
                    u.acquire(ACT)
                    for dc in range(DC):
                        tmp = tmps[dc % 3]
                        tmp.acquire(DVE)
                        i_ = DVE.scalar_tensor_tensor(out=tmp.ap[:, :], in0=xt.ap[:, dc, :],
                                                      scalar=tabA[:, k * DC + dc:k * DC + dc + 1],
                                                      in1=rstd.ap[:, :], op0=ALU.mult, op1=ALU.mult)
                        tmp.produced(i_)
                        xt.consumed(i_)
                        if dc == DC - 1:
                            rstd.consumed(i_)
                        tmp.wait_ready(ACT)
                        a = ACT.activation(out=u.ap[:, dc, :], in_=tmp.ap[:, :], func=AF.Identity,
                                           bias=tabB(k)[:, dc:dc + 1], scale=1.0)
                        tmp.consumed(a)
                        u.produced(a)
                        ev = Evs("uch")
                        ev.add(a)
                        uch[u.i][dc] = ev

                def ffn(k, w1, w3, w2, do_norm=True):
                    if do_norm:
                        norm_mod(k)
                    w1v, w3v = wview(w1), wview(w3)
                    w2v = w2.ap().rearrange("(j p) n -> p j n", p=128)
                    hid.acquire(DVE)
                    for s_ in range(NJ // 2):
                        sl = wstage([
                            (lambda a: a[:, 0:4096].rearrange("p (k n) -> p k n", n=256), w1v[:, :, s_ * 256:(s_ + 1) * 256]),
                            (lambda a: a[:, 4096:8192].rearrange("p (k n) -> p k n", n=256), w3v[:, :, s_ * 256:(s_ + 1) * 256]),
                        ])
                        sl.wait_ready(PE)
                        v1 = sl.ap[:, 0:4096].rearrange("p (k n) -> p k n", n=256)
                        v3 = sl.ap[:, 4096:8192].rearrange("p (k n) -> p k n", n=256)
                        for jj in range(2):
                            j = 2 * s_ + jj
                            b1 = banks[j % 2]
                            b3 = banks[2 + j % 2]
                            b1.acquire(PE)
                            for kc in range(DC):
                                if j == 0:
                                    uch[u.i][kc].wait(PE)
                                m = PE.matmul(b1.ap[:, :], lhsT=v1[:, kc, jj * 128:(jj + 1) * 128], rhs=u.ap[:, kc, :],
                                              start=(kc == 0), stop=(kc == DC - 1))
                            b1.produced(m)
                            b3.acquire(PE)
                            for kc in range(DC):
                                m = PE.matmul(b3.ap[:, :], lhsT=v3[:, kc, jj * 128:(jj + 1) * 128], rhs=u.ap[:, kc, :],
                                              start=(kc == 0), stop=(kc == DC - 1))
                            b3.produced(m)
                            if jj == 1:
                                sl.consumed(m)
                            if j == NJ - 1:
                                u.consumed(m)
                            sil = sils[j % 2]
                            b1.wait_ready(ACT)
                            sil.acquire(ACT)
                            a = ACT.activation(out=sil.ap[:, :], in_=b1.ap[:, :], func=AF.Silu)
                            b1.consumed(a)
                            sil.produced(a)
                            sil.wait_ready(DVE)
                            b3.wait_ready(DVE)
                            d_ = DVE.tensor_tensor(out=hid.ap[:, j, :], in0=sil.ap[:, :], in1=b3.ap[:, :], op=ALU.mult)
                            sil.consumed(d_)
                            b3.consumed(d_)
                            hid.produced(d_)
                    hid.wait_ready(PE)
                    HJ = NJ // 2
                    for dp in range(DC // 2):
                        for half in range(2):
                            sl = wstage([(lambda a: a[:, 0:HJ * 256].rearrange("p (j n) -> p j n", n=256),
                                          w2v[:, half * HJ:(half + 1) * HJ, dp * 256:(dp + 1) * 256])])
                            sl.wait_ready(PE)
                            v2 = sl.ap[:, 0:HJ * 256].rearrange("p (j n) -> p j n", n=256)
                            for i in range(2):
                                db = dp * 2 + i
                                yb = banks[4 + (dp % 2) * 2 + i]
                                if half == 0:
                                    yb.acquire(PE)
                                for jj in range(HJ):
                                    j = half * HJ + jj
                                    m = PE.matmul(yb.ap[:, :], lhsT=v2[:, jj, i * 128:(i + 1) * 128], rhs=hid.ap[:, j, :],
                                                  start=(j == 0), stop=(j == NJ - 1))
                                if i == 1:
                                    sl.consumed(m)
                                if half == 1:
                                    yb.produced(m)
                                    if db == DC - 1:
                                        hid.consumed(m)
                                    yb.wait_ready(DVE)
                                    xt.acquire(DVE)
                                    d_ = DVE.scalar_tensor_tensor(out=xt.ap[:, db, :], in0=yb.ap[:, :],
                                                                  scalar=tabG[:, k * DC + db:k * DC + db + 1],
                                                                  in1=xt.ap[:, db, :], op0=ALU.mult, op1=ALU.add)
                                    yb.consumed(d_)
                                    xt.produced(d_)

                def store_stage(dst_ap, eng_producer_fn, extra=None):
                    o = osts[ost_c[0] % 4]
                    ost_c[0] += 1
                    eng, ins = eng_producer_fn(o)
                    o.produced(ins)
                    o.wait_ready(SP)
                    dm = SP.dma_start(out=dst_ap, in_=o.ap[:, :])
                    o.consumed(dm, 16)
                    st.inc(dm, 16)
                    if extra is not None:
                        extra.add(dm)

                if first:
                    ropeC = sbl("ropeC", [128, TOK], F32)
                    ropeS = sbl("ropeS", [128, TOK], F32)
                    ldr.inc(SP.dma_start(out=ropeC[:, :], in_=ropeC_d[:, :]), 16)
                    ldr.inc(SP.dma_start(out=ropeS[:, :], in_=ropeS_d[:, :]), 16)
                    winv = wview(win)
                    for t in range(NT):
                        tsl = slice(t * T, (t + 1) * T)
                        if t == 0:
                            xt.acquire(SP)
                            xt.produced(SP.dma_start(out=xt.ap[:, :, :],
                                                     in_=xT.ap().rearrange("(c p) t -> p c t", p=128)[:, :, tsl]), 16)
                        u.sel(0)
                        ffn(0, w1a, w3a, w2a, do_norm=(t == 0))
                        xt.wait_ready(SP)
                        dm = SP.dma_start(out=h1T.ap().rearrange("(c p) t -> p c t", p=128)[:, :, tsl], in_=xt.ap[:, :, :])
                        xt.consumed(dm, 16)
                        st.inc(dm, 16)
                        u.sel(1)
                        norm_mod(1)
                        if t + 1 < NT:
                            nsl = slice((t + 1) * T, (t + 2) * T)
                            xt.acquire(SP)
                            xt.produced(SP.dma_start(out=xt.ap[:, :, :],
                                                     in_=xT.ap().rearrange("(c p) t -> p c t", p=128)[:, :, nsl]), 16)
                        first_pb = [True]
                        ldr.wait(DVE)
                        hb = 0

                        def fm_stage(si):
                            return wstage([(lambda a: a[:, :].rearrange("p (k n) -> p k n", n=512),
                                            winv[:, :, si * 512:(si + 1) * 512])])

                        def proj_block(sl, bi, bank, last_of_slot):
                            sv = sl.ap[:, :].rearrange("p (k n) -> p k n", n=512)
                            bank.acquire(PE)
                            for kc in range(DC):
                                if first_pb[0]:
                                    uch[u.i][kc].wait(PE)
                                m = PE.matmul(bank.ap[:, :], lhsT=sv[:, kc, bi * 128:(bi + 1) * 128], rhs=u.ap[:, kc, :],
                                              start=(kc == 0), stop=(kc == DC - 1))
                            first_pb[0] = False
                            bank.produced(m)
                            if last_of_slot:
                                sl.consumed(m)
                            return m

                        ssb_c = [0]

                        def qk_post(qbank, pbank, gcol, gpcol, dst_ap, extra=None):
                            ssb = banks[4 + ssb_c[0] % 2] if pbank is None else banks[6 + ssb_c[0] % 2]
                            ssb_c[0] += 1
                            rms_rstd(None, [(qbank.ap[:, :], qbank.wait_ready, lambda a: None)], HD, ssb)
                            rstd.wait_ready(DVE)
                            qbank.wait_ready(DVE)
                            if pbank is None:
                                def prod(o):
                                    o.acquire(DVE)
                                    i_ = DVE.scalar_tensor_tensor(out=o.ap[:, :], in0=qbank.ap[:, :],
                                                                  scalar=qkg[:, gcol:gcol + 1], in1=rstd.ap[:, :],
                                                                  op0=ALU.mult, op1=ALU.mult)
                                    qbank.consumed(i_)
                                    rstd.consumed(i_)
                                    return DVE, i_
                                store_stage(dst_ap, prod)
                            else:
                                t1, t2 = tmps[0], tmps[1]
                                t1.acquire(DVE)
                                i_ = DVE.scalar_tensor_tensor(out=t1.ap[:, :], in0=qbank.ap[:, :],
                                                              scalar=qkg[:, gcol:gcol + 1], in1=ropeC[:, tsl],
                                                              op0=ALU.mult, op1=ALU.mult)
                                qbank.consumed(i_)
                                t1.produced(i_)
                                t2.acquire(DVE)
                                pbank.wait_ready(DVE)
                                i_ = DVE.scalar_tensor_tensor(out=t2.ap[:, :], in0=pbank.ap[:, :],
                                                              scalar=qkg[:, gpcol:gpcol + 1], in1=ropeS[:, tsl],
                                                              op0=ALU.mult, op1=ALU.mult)
                                pbank.consumed(i_)
                                t2.produced(i_)
                                t3 = tmps[2]
                                t3.acquire(DVE)
                                t1.wait_ready(DVE)
                                t2.wait_ready(DVE)
                                i_ = DVE.tensor_tensor(out=t3.ap[:, :], in0=t1.ap[:, :], in1=t2.ap[:, :], op=ALU.add)
                                t1.consumed(i_)
                                t2.consumed(i_)
                                t3.produced(i_)

                                def prod(o):
                                    o.acquire(DVE)
                                    t3.wait_ready(DVE)
                                    i2 = DVE.tensor_tensor(out=o.ap[:, :], in0=t3.ap[:, :], in1=rstd.ap[:, :], op=ALU.mult)
                                    t3.consumed(i2)
                                    rstd.consumed(i2)
                                    return DVE, i2
                                store_stage(dst_ap, prod, extra)

                        lastm = [None]
                        kvst = Evs("kvst")

                        def do_qk_plain(stages):
                            nonlocal hb
                            for si in stages:
                                sl = fm_stage(si)
                                sl.wait_ready(PE)
                                for bi in range(4):
                                    head = (si % 3) * 4 + bi
                                    bank = banks[hb % 4]
                                    hb += 1
                                    lastm[0] = proj_block(sl, bi, bank, bi == 3)
                                    if si < 3:
                                        dst = qaT.ap()[head * 128:(head + 1) * 128, tsl]
                                    else:
                                        dst = kaTc[head // 2].ap()[(head % 2) * 128:(head % 2 + 1) * 128, tsl]
                                    qk_post(bank, None, 0 if si < 3 else 1, None, dst)

                        def do_rope(stages):
                            nonlocal hb
                            for si in stages:
                                sl = fm_stage(si)
                                sl.wait_ready(PE)
                                for hh in range(2):
                                    qbank = banks[(hb) % 4]
                                    pbank = banks[(hb + 1) % 4]
                                    hb += 2
                                    proj_block(sl, 2 * hh, qbank, False)
                                    lastm[0] = proj_block(sl, 2 * hh + 1, pbank, hh == 1)
                                    if si < 10:
                                        head = (si - 6) * 2 + hh
                                        dst = qbT.ap()[head * 128:(head + 1) * 128, tsl]
                                        qk_post(qbank, pbank, 2, 4, dst)
                                    else:
                                        dst = kbTt[t].ap()[hh * 128:(hh + 1) * 128, :]
                                        qk_post(qbank, pbank, 3, 5, dst, kvst)

                        def do_gates():
                            nonlocal hb
                            for si in range(11, 19):
                                sl = fm_stage(si)
                                sl.wait_ready(PE)
                                for bi in range(4):
                                    bank = banks[hb % 4]
                                    hb += 1
                                    lastm[0] = proj_block(sl, bi, bank, bi == 3)
                                    row = (si - 11) * 4 + bi

                                    def prod(o, bank=bank):
                                        o.acquire(ACT)
                                        bank.wait_ready(ACT)
                                        a = ACT.activation(out=o.ap[:, :], in_=bank.ap[:, :], func=AF.Sigmoid)
                                        bank.consumed(a)
                                        return ACT, a
                                    store_stage(sgT.ap()[row * 128:(row + 1) * 128, tsl], prod)

                        def do_v(vis):
                            nonlocal hb
                            for vi in vis:
                                ncol = 512 if vi < 3 else 256
                                c0 = FM_COLS + vi * 512
                                sl = wstage([(lambda a, ncol=ncol: a[:, 0:DC * ncol].rearrange("p (k n) -> p k n", n=ncol),
                                              winv[:, :, c0:c0 + ncol])])
                                sl.wait_ready(PE)
                                sv = sl.ap[:, 0:DC * ncol].rearrange("p (k n) -> p k n", n=ncol)
                                for tb in range(4):
                                    bank = banks[hb % 4]
                                    hb += 1
                                    bank.acquire(PE)
                                    for kc in range(DC):
                                        m = PE.matmul(bank.ap[:, 0:ncol], lhsT=u.ap[:, kc, tb * 128:(tb + 1) * 128], rhs=sv[:, kc, :],
                                                      start=(kc == 0), stop=(kc == DC - 1))
                                    bank.produced(m)
                                    lastm[0] = m
                                    if tb == 3:
                                        sl.consumed(m)
                                    r0 = t * T + tb * 128
                                    if vi < 3:
                                        dst = vaLc[r0 // 256].ap()[r0 % 256:r0 % 256 + 128, vi * 512:(vi + 1) * 512]
                                    else:
                                        dst = vbLt[t].ap()[tb * 128:(tb + 1) * 128, :]
                                    o = osts[ost_c[0] % 4]
                                    ost_c[0] += 1
                                    o.acquire(DVE)
                                    bank.wait_ready(DVE)
                                    i_ = DVE.tensor_copy(out=o.ap[:, 0:ncol], in_=bank.ap[:, 0:ncol])
                                    bank.consumed(i_)
                                    o.produced(i_)
                                    o.wait_ready(SP)
                                    dm = SP.dma_start(out=dst, in_=o.ap[:, 0:ncol])
                                    o.consumed(dm, 16)
                                    st.inc(dm, 16)
                                    if vi == 3:
                                        kvst.add(dm)

                        do_rope([10])
                        do_v([3])

                        def trig(t=t, kvst=kvst):
                            kvst.wait(GP)
                            ccb.inc(GP.collective_compute("AllGather", ALU.bypass, replica_groups=groups,
                                                          ins=[kbTt[t].ap().opt()], outs=[kbGt[t].ap().opt()]))
                            ccb.inc(GP.collective_compute("AllGather", ALU.bypass, replica_groups=groups,
                                                          ins=[vbLt[t].ap().opt()], outs=[vbGt[t].ap().opt()]))
                        deferred.append([3, trig])
                        do_qk_plain([3, 4, 5])
                        do_v([0, 1, 2])
                        do_qk_plain([0, 1, 2])
                        if t + 1 < NT:
                            u.sel(0)
                            norm_mod(0)
                            u.sel(1)
                        do_rope([6, 7, 8, 9])
                        do_gates()
                        u.consumed(lastm[0])
                else:
                    oat = Buf(nc, "oat", sbl("oat", [128, 4, T], BF16))
                    obt = Buf(nc, "obt", sbl("obt", [128, 8, T], BF16))
                    sga = [Buf(nc, f"sga{i}", sbl(f"sga{i}", [128, 4, T], BF16)) for i in range(2)]
                    sgb = [Buf(nc, f"sgb{i}", sbl(f"sgb{i}", [128, 4, T], BF16)) for i in range(2)]
                    wpav = wpa.ap().rearrange("(s p) n -> p s n", p=128)
                    wpbv = wpb.ap().rearrange("(s p) n -> p s n", p=128)
                    woutv = wview(wout)
                    for t in range(NT):
                        tsl = slice(t * T, (t + 1) * T)
                        def load_ops(tt):
                            sl_ = slice(tt * T, (tt + 1) * T)
                            oat.acquire(SP)
                            oat.produced(SP.dma_start(out=oat.ap[:, :, :],
                                                      in_=oaT.ap().rearrange("(s p) t -> p s t", p=128)[:, :, sl_]), 16)
                            obt.acquire(SP)
                            obt.produced(SP.dma_start(out=obt.ap[:, :, :],
                                                      in_=obT.ap().rearrange("(s p) t -> p s t", p=128)[:, :, sl_]), 16)
                            for s0 in range(2):
                                sg_load(s0, sl_)

                        def sg_load(s0, sl_):
                            ga0 = sga[s0 % 2]
                            gb0 = sgb[s0 % 2]
                            ga0.acquire(SP)
                            ga0.produced(SP.dma_start(out=ga0.ap[:, :, :], in_=sgv[:, s0 * 4:(s0 + 1) * 4, sl_]), 16)
                            gb0.acquire(SP)
                            gb0.produced(SP.dma_start(out=gb0.ap[:, :, :], in_=sgv[:, DC + s0 * 4:DC + (s0 + 1) * 4, sl_]), 16)
                        sgv = sgT.ap().rearrange("(c p) t -> p c t", p=128)
                        if t == 0:
                            load_ops(0)
                        oat.wait_ready(PE)
                        obt.wait_ready(PE)
                        u.acquire(DVE)
                        for s_ in range(4):
                            ga_ = sga[s_ % 2]
                            gb_ = sgb[s_ % 2]
                            if s_ >= 2:
                                sg_load(s_, tsl)
                            sl = wstage([
                                (lambda a: a[:, 0:2048].rearrange("p (s n) -> p s n", n=512), wpav[:, :, s_ * 512:(s_ + 1) * 512]),
                                (lambda a: a[:, 2048:6144].rearrange("p (s n) -> p s n", n=512), wpbv[:, :, s_ * 512:(s_ + 1) * 512]),
                            ])
                            sl.wait_ready(PE)
                            va_ = sl.ap[:, 0:2048].rearrange("p (s n) -> p s n", n=512)
                            vb_ = sl.ap[:, 2048:6144].rearrange("p (s n) -> p s n", n=512)
                            for i in range(4):
                                db = s_ * 4 + i
                                ba = banks[db % 2]
                                bb = banks[2 + db % 2]
                                ba.acquire(PE)
                                for s2 in range(4):
                                    m = PE.matmul(ba.ap[:, :], lhsT=va_[:, s2, i * 128:(i + 1) * 128], rhs=oat.ap[:, s2, :],
                                                  start=(s2 == 0), stop=(s2 == 3))
                                ba.produced(m)
                                bb.acquire(PE)
                                for s2 in range(8):
                                    m = PE.matmul(bb.ap[:, :], lhsT=vb_[:, s2, i * 128:(i + 1) * 128], rhs=obt.ap[:, s2, :],
                                                  start=(s2 == 0), stop=(s2 == 7))
                                bb.produced(m)
                                if i == 3:
                                    sl.consumed(m)
                                    if s_ == 3:
                                        oat.consumed(m)
                                        obt.consumed(m)
                                t1, t2 = tmps[0], tmps[1]
                                ga_.wait_ready(DVE)
                                gb_.wait_ready(DVE)
                                t1.acquire(DVE)
                                ba.wait_ready(DVE)
                                i_ = DVE.tensor_tensor(out=t1.ap[:, :], in0=ba.ap[:, :], in1=ga_.ap[:, i, :], op=ALU.mult)
                                ba.consumed(i_)
                                t1.produced(i_)
                                t2.acquire(DVE)
                                bb.wait_ready(DVE)
                                i_ = DVE.tensor_tensor(out=t2.ap[:, :], in0=bb.ap[:, :], in1=gb_.ap[:, i, :], op=ALU.mult)
                                bb.consumed(i_)
                                t2.produced(i_)
                                if i == 3:
                                    ga_.consumed(i_)
                                    gb_.consumed(i_)
                                t1.wait_ready(DVE)
                                t2.wait_ready(DVE)
                                i_ = DVE.tensor_tensor(out=u.ap[:, db, :], in0=t1.ap[:, :], in1=t2.ap[:, :], op=ALU.add)
                                t1.consumed(i_)
                                t2.consumed(i_)
                                u.produced(i_)
                        xt.acquire(SP)
                        xt.produced(SP.dma_start(out=xt.ap[:, :, :],
                                                 in_=h1T.ap().rearrange("(c p) t -> p c t", p=128)[:, :, tsl]), 16)
                        if t + 1 < NT:
                            load_ops(t + 1)
                        u.wait_ready(PE)
                        for s_ in range(4):
                            sl = wstage([(lambda a: a[:, :].rearrange("p (k n) -> p k n", n=512),
                                          woutv[:, :, s_ * 512:(s_ + 1) * 512])])
                            sl.wait_ready(PE)
                            sv = sl.ap[:, :].rearrange("p (k n) -> p k n", n=512)
                            for i in range(4):
                                db = s_ * 4 + i
                                yb = banks[4 + db % 2]
                                yb.acquire(PE)
                                for kc in range(DC):
                                    m = PE.matmul(yb.ap[:, :], lhsT=sv[:, kc, i * 128:(i + 1) * 128], rhs=u.ap[:, kc, :],
                                                  start=(kc == 0), stop=(kc == DC - 1))
                                yb.produced(m)
                                if i == 3:
                                    sl.consumed(m)
                                    if s_ == 3:
                                        u.consumed(m)
                                yb.wait_ready(DVE)
                                xt.acquire(DVE)
                                d_ = DVE.scalar_tensor_tensor(out=xt.ap[:, db, :], in0=yb.ap[:, :],
                                                              scalar=tabG[:, DC + db:DC + db + 1],
                                                              in1=xt.ap[:, db, :], op0=ALU.mult, op1=ALU.add)
                                yb.consumed(d_)
                                xt.produced(d_)
                        ffn(2, w1b, w3b, w2b)
                        chunks = [(xt.ap[:, dc, :], xt.wait_ready, xt.consumed) for dc in range(DC)]
                        rms_rstd(None, chunks, D, banks[6], split=True)
                        rstd.wait_ready(DVE)
                        xt.acquire(DVE)
                        for dc in range(DC):
                            d_ = DVE.scalar_tensor_tensor(out=xt.ap[:, dc, :], in0=xt.ap[:, dc, :],
                                                          scalar=gtab[:, 3 * DC + dc:3 * DC + dc + 1],
                                                          in1=rstd.ap[:, :], op0=ALU.mult, op1=ALU.mult)
                            xt.produced(d_)
                        rstd.consumed(d_)
                        xt.wait_ready(SP)
                        dm = SP.dma_start(out=yT.ap().rearrange("(c p) t -> p c t", p=128)[:, :, tsl], in_=xt.ap[:, :, :])
                        xt.consumed(dm, 16)
                        st.inc(dm, 16)

        run_token_phases(True)
        if STOP == "ab":
            raise _Stop()

        st.wait(GP)
        flush_deferred()
        for e in (PE, ACT, DVE, SP):
            st.wait(e)

        if STOP == "kv":
            raise _Stop()
        def attn_tail(ob, db_, dst_ap, rden, osta):
            db_.wait_ready(ACT)
            rden.acquire(ACT)
            a_ = ACT.activation(out=rden.ap[:, :], in_=db_.ap[:, :], func=AF.Ln)
            db_.consumed(a_)
            lnev = Evs("lnev")
            lnev.add(a_)
            lnev.wait(ACT)
            a_ = ACT.activation(out=rden.ap[:, :], in_=rden.ap[:, :], func=AF.Exp, scale=-1.0)
            rden.produced(a_)
            rden.wait_ready(DVE)
            ob.wait_ready(DVE)
            osta.acquire(DVE)
            i_ = DVE.tensor_tensor(out=osta.ap[:, :], in0=ob.ap[:, :], in1=rden.ap[:, :], op=ALU.mult)
            ob.consumed(i_)
            rden.consumed(i_)
            osta.produced(i_)
            osta.wait_ready(SP)
            dm = SP.dma_start(out=dst_ap, in_=osta.ap[:, :])
            osta.consumed(dm, 16)
            st.inc(dm, 16)

        with ExitStack() as e2:
            def sbl(name, shape, dt):
                return e2.enter_context(nc.sbuf_tensor(name, list(shape), dt))
            kbt = sbl("kbt", [128, 2, S], BF16)
            vbt = sbl("vbt", [128, 64, 256], BF16)
            qbt = sbl("qbt", [128, 8, TOK], BF16)
            spairs = [Buf(nc, f"bank{2 * i}", psum_all[:, i * 1024:(i + 1) * 1024]) for i in range(2)]
            pps = [Buf(nc, ("sq%d" % i if i < 2 else "sil0"), sbl(f"pp{i}", [128, 2 * T], BF16)) for i in range(3)]
            rden = Buf(nc, "rstd", sbl("rdenb", [128, T], F32))
            ostb = [Buf(nc, f"ost{i}", sbl(f"ostb{i}", [128, T], BF16)) for i in range(2)]
            paccs = [Buf(nc, f"tmp{i}", sbl(f"pacc{i}", [128, T], F32)) for i in range(2)]
            accw = [[sbl(f"accw{i}{j}", [128, 2 * T], F32) for j in range(2)] for i in range(2)]
            ccb.wait(SP)
            for tt in range(NT):
                for kvh in range(2):
                    ldb.inc(SP.dma_start(
                        out=kbt[:, kvh, :].rearrange("p (r t c) -> p r t c", r=4, t=NT)[:, :, tt, :],
                        in_=kbGt[tt].ap().rearrange("(r h p) c -> h p r c", h=2, p=128)[kvh]), 16)
                for r in range(4):
                    ldb.inc(SP.dma_start(
                        out=vbt[:, r * 16 + tt * 4:r * 16 + tt * 4 + 4, :],
                        in_=vbGt[tt].ap()[r * T:(r + 1) * T, :].rearrange("(b p) c -> p b c", p=128)), 16)
            ldb.inc(SP.dma_start(out=qbt[:, 0:4, :], in_=qbT.ap().rearrange("(h p) t -> p h t", p=128)[:, 0:4, :]), 16)
            ldb1 = Evs("ldb1")
            ldb1.inc(SP.dma_start(out=qbt[:, 4:8, :], in_=qbT.ap().rearrange("(h p) t -> p h t", p=128)[:, 4:8, :]), 16)
            ldb.wait(GP)
            ldb1.wait(GP)
            for c in range(6):
                cca.inc(GP.collective_compute("AllGather", ALU.bypass, replica_groups=groups,
                                              ins=[kaTc[c].ap().opt()], outs=[kaGc[c].ap().opt()]))
            for c in range(8):
                cca.inc(GP.collective_compute("AllGather", ALU.bypass, replica_groups=groups,
                                              ins=[vaLc[c].ap().opt()], outs=[vaGc[c].ap().opt()]))

            ldb.wait(PE)
            def emit_halo():
                cca.wait(SP)
                cca.wait(GP)
                pid = GP.partition_id()
                prv = (pid + 3) % 4
                nxt = (pid + 1) % 4
                pid2 = SP.partition_id()
                prv2 = (pid2 + 3) % 4
                nxt2 = (pid2 + 1) % 4
                hal = Evs("hal")
                for c in range(6):
                    kx = kaGc[c].ap().rearrange("(r x) t -> x r t", r=4)
                    hal.inc(GP.dma_start(out=hKp.ap()[c * 256:(c + 1) * 256], in_=kx[:, bass.ds(prv, 1), 1024:2048]), 16)
                    hal.inc(GP.dma_start(out=hKn.ap()[c * 256:(c + 1) * 256], in_=kx[:, bass.ds(nxt, 1), 0:1024]), 16)
                hal2 = Evs("hal2")
                for c in range(4):
                    vp = vaGc[4 + c].ap().rearrange("(r t) c -> t r c", r=4)
                    vn = vaGc[c].ap().rearrange("(r t) c -> t r c", r=4)
                    hal2.inc(SP.dma_start(out=hVp.ap()[c * 256:(c + 1) * 256], in_=vp[:, bass.ds(prv2, 1), :]), 16)
                    hal2.inc(SP.dma_start(out=hVn.ap()[c * 256:(c + 1) * 256], in_=vn[:, bass.ds(nxt2, 1), :]), 16)

                return hal, hal2
            halo_evs = [None]
            pend_tail = [None]
            it = 0
            for hq in range(8):
                kv = hq // 4
                if hq == 4:
                    ldb1.wait(PE)
                if hq == 6:
                    halo_evs[0] = emit_halo()
                for qt in range(NT):
                    tsl = slice(qt * T, (qt + 1) * T)
                    ob = banks[4 + it % 2]
                    db_ = banks[6 + it % 2]
                    ob.acquire(PE)
                    db_.acquire(PE)
                    NKB = S // 128

                    NP = NKB // 2
                    pacc = paccs[it % 2]
                    accs = accw[it % 2]
                    chain = [Evs("chain0"), Evs("chain1")]

                    def s_mm(j):
                        sp = spairs[j % 2]
                        sp.acquire(PE)
                        for hh in range(2):
                            kb = 2 * j + hh
                            m = PE.matmul(sp.ap[:, hh * T:(hh + 1) * T], lhsT=kbt[:, kv, kb * 128:(kb + 1) * 128],
                                          rhs=qbt[:, hq, tsl], start=True, stop=True)
                        sp.produced(m)
                        p = pps[j % 3]
                        sp.wait_ready(ACT)
                        p.acquire(ACT)
                        a = ACT.activation(out=p.ap[:, :], in_=sp.ap[:, :], func=AF.Exp, scale=SCALE)
                        sp.consumed(a)
                        p.produced(a)

                    def pv_mm(j):
                        p = pps[j % 3]
                        p.wait_ready(PE)
                        for hh in range(2):
                            kb = 2 * j + hh
                            m = PE.matmul(ob.ap[:, :], lhsT=vbt[:, kb, kv * 128:(kv + 1) * 128], rhs=p.ap[:, hh * T:(hh + 1) * T],
                                          start=(kb == 0), stop=(kb == NKB - 1))
                        p.consumed(m)
                        p.wait_ready(DVE)
                        acc = accs[j % 2]
                        if j < 2:
                            if j == 0:
                                pacc.acquire(DVE)
                            d_ = DVE.tensor_copy(out=acc[:, :], in_=p.ap[:, :])
                        else:
                            chain[j % 2].wait(DVE)
                            d_ = DVE.tensor_tensor(out=acc[:, :], in0=acc[:, :], in1=p.ap[:, :], op=ALU.add)
                        p.consumed(d_)
                        chain[j % 2].add(d_)
                        if j == NP - 1:
                            chain[0].wait(DVE)
                            chain[1].wait(DVE)
                            d2 = DVE.tensor_tensor(out=accs[0][:, :], in0=accs[0][:, :], in1=accs[1][:, :], op=ALU.add)
                            fin = Evs("fin")
                            fin.add(d2)
                            fin.wait(DVE)
                            d3 = DVE.tensor_tensor(out=pacc.ap[:, :], in0=accs[0][:, 0:T], in1=accs[0][:, T:2 * T], op=ALU.add)
                            pacc.produced(d3)
                        return m
                    s_mm(0)
                    s_mm(1)
                    for j in range(NP):
                        if j + 2 < NP:
                            s_mm(j + 2)
                        m = pv_mm(j)
                        if j == 2 and pend_tail[0] is not None:
                            pend_tail[0]()
                            pend_tail[0] = None
                    ob.produced(m)
                    pacc.wait_ready(PE)
                    m = PE.matmul(db_.ap[:, :], lhsT=ones_f[:, :], rhs=pacc.ap[:, :], start=True, stop=True)
                    pacc.consumed(m)
                    db_.produced(m)
                    pend_tail[0] = (lambda ob=ob, db_=db_, dst=obT.ap()[hq * 128:(hq + 1) * 128, tsl], o_=ostb[it % 2]:
                                    attn_tail(ob, db_, dst, rden, o_))
                    it += 1
            pend_tail[0]()
            pend_tail[0] = None
            banks[7].acquire(PE)
            misc.inc(PE.matmul(banks[7].ap[:, 0:1], lhsT=ones[:, :], rhs=ones[:, 0:1], start=True, stop=True))
            for e in (SP, ACT, DVE):
                misc.wait(e)
                st.wait(e)
            st.wait(PE)

        if STOP == "mb":
            raise _Stop()
        with ExitStack() as e3:
            def sbl(name, shape, dt):
                return e3.enter_context(nc.sbuf_tensor(name, list(shape), dt))
            kwin = Cur([Buf(nc, f"kwin{i}", sbl(f"kwin{i}", [128, 3, 4096], BF16)) for i in range(2)])
            vwin = Cur([Buf(nc, f"vwin{i}", sbl(f"vwin{i}", [128, 32, 384], BF16)) for i in range(2)])
            qwin = Cur([Buf(nc, f"qwin{i}", sbl(f"qwin{i}", [128, 3, TOK], BF16)) for i in range(2)])
            etb = Buf(nc, "etb", sbl("etb", [128, ETOT], BF16))
            sx = [Buf(nc, (f"tmp{i}" if i < 3 else "lnt"), sbl(f"sx{i}", [128, T], F32)) for i in range(4)]
            ps = [Buf(nc, ("sq%d" % i if i < 2 else ("sil%d" % (i - 2) if i < 4 else "hid")), sbl(f"pa{i}", [128, T], BF16)) for i in range(5)]
            rden = Buf(nc, "rstd", sbl("rdena", [128, T], F32))
            osta = [Buf(nc, f"ost{i}", sbl(f"osta{i}", [128, T], BF16)) for i in range(2)]
            hal, hal2 = halo_evs[0]
            hal2.wait(SP)
            hal.wait(SP)
            qaTv = qaT.ap().rearrange("(h p) t -> h p t", p=128)
            hKpv = hKp.ap().rearrange("(h p) o t -> h p (o t)", p=128)
            hKnv = hKn.ap().rearrange("(h p) o t -> h p (o t)", p=128)
            hVpv = hVp.ap().rearrange("(b p) o c -> p b (o c)", p=128)
            hVnv = hVn.ap().rearrange("(b p) o c -> p b (o c)", p=128)
            it = 0
            lastd = [None]

            def load_slot(h):
                for b_ in (kwin, vwin, qwin):
                    b_.sel(h % 2)
                    b_.acquire(SP)
                for g in range(3):
                    hd = 4 * g + h
                    kwin.produced(SP.dma_start(out=kwin.ap[:, g, 1024:3072],
                                               in_=kaTc[hd // 2].ap()[(hd % 2) * 128:(hd % 2 + 1) * 128, :]), 16)
                    kwin.produced(SP.dma_start(out=kwin.ap[:, g, 0:1024], in_=hKpv[hd]), 16)
                    kwin.produced(SP.dma_start(out=kwin.ap[:, g, 3072:4096], in_=hKnv[hd]), 16)
                    for c in range(8):
                        vwin.produced(SP.dma_start(
                            out=vwin.ap[:, 8 + 2 * c:10 + 2 * c, g * 128:(g + 1) * 128],
                            in_=vaLc[c].ap()[:, hd * 128:(hd + 1) * 128].rearrange("(b p) c -> p b c", p=128)), 16)
                    vwin.produced(SP.dma_start(out=vwin.ap[:, 0:8, g * 128:(g + 1) * 128],
                                               in_=hVpv[:, :, hd * 128:(hd + 1) * 128]), 16)
                    vwin.produced(SP.dma_start(out=vwin.ap[:, 24:32, g * 128:(g + 1) * 128],
                                               in_=hVnv[:, :, hd * 128:(hd + 1) * 128]), 16)
                    qwin.produced(SP.dma_start(out=qwin.ap[:, g, :], in_=qaTv[hd]), 16)

            def compute_slot(h):
                nonlocal it
                for b_ in (kwin, vwin, qwin):
                    b_.sel(h % 2)
                for b_ in (kwin, vwin, qwin):
                    b_.wait_ready(PE)
                etb.wait_ready(DVE)
                for qt in range(NT):
                    tsl = slice(qt * T, (qt + 1) * T)
                    ob = banks[3 + it % 2]
                    db_ = banks[5 + it % 2]
                    ob.acquire(PE)
                    db_.acquire(PE)
                    work = []
                    for g in range(3):
                        for kb in range(4 * qt - DMAX[g], 4 * qt + 3 + DMAX[g] + 1):
                            work.append((g, kb))
                    nw = len(work)

                    def s_mm(i):
                        g, kb = work[i]
                        p_ = kb + 8
                        sbk = banks[(0, 1, 2, 7)[i % 4]]
                        sbk.acquire(PE)
                        m = PE.matmul(sbk.ap[:, :], lhsT=kwin.ap[:, g, p_ * 128:(p_ + 1) * 128], rhs=qwin.ap[:, g, tsl],
                                      start=True, stop=True)
                        sbk.produced(m)
                        x_ = sx[i % 4]
                        sbk.wait_ready(ACT)
                        x_.acquire(ACT)
                        a = ACT.activation(out=x_.ap[:, :], in_=sbk.ap[:, :], func=AF.Exp, scale=SCALE,
                                           bias=vtab[:, p_:p_ + 1])
                        sbk.consumed(a)
                        x_.produced(a)
                        p = ps[i % 5]
                        x_.wait_ready(DVE)
                        p.acquire(DVE)
                        d0 = kb - 4 * qt
                        c0 = EOFF[g] + (DMAXP[g] - d0) * 128
                        d_ = DVE.tensor_tensor(out=p.ap[:, :], in0=x_.ap[:, :], in1=etb.ap[:, c0:c0 + 512], op=ALU.mult)
                        x_.consumed(d_)
                        p.produced(d_)
                        lastd[0] = d_

                    def pv_mm(i):
                        g, kb = work[i]
                        p_ = kb + 8
                        p = ps[i % 5]
                        p.wait_ready(PE)
                        PE.matmul(ob.ap[:, :], lhsT=vwin.ap[:, p_, g * 128:(g + 1) * 128], rhs=p.ap[:, :],
                                  start=(i == 0), stop=(i == nw - 1))
                        m = PE.matmul(db_.ap[:, :], lhsT=ones[:, :], rhs=p.ap[:, :],
                                      start=(i == 0), stop=(i == nw - 1))
                        p.consumed(m)
                        return m
                    s_mm(0)
                    s_mm(1)
                    s_mm(2)
                    for i in range(nw):
                        if i + 3 < nw:
                            s_mm(i + 3)
                        m = pv_mm(i)
                        if i == 2 and pend_tail[0] is not None:
                            pend_tail[0]()
                            pend_tail[0] = None
                    ob.produced(m)
                    db_.produced(m)
                    if qt == NT - 1:
                        for b_ in (kwin, vwin, qwin):
                            b_.consumed(m)
                        etb.consumed(lastd[0])
                    pend_tail[0] = (lambda ob=ob, db_=db_, dst=oaT.ap()[h * 128:(h + 1) * 128, tsl], o_=osta[it % 2]:
                                    attn_tail(ob, db_, dst, rden, o_))
                    it += 1
            def load_etb(h):
                etb.acquire(SP)
                etb.produced(SP.dma_start(out=etb.ap[:, :], in_=etab_d.ap()[h]), 16)
            load_slot(0)
            load_etb(0)
            load_slot(1)
            for h in range(4):
                compute_slot(h)
                if h + 1 < 4:
                    load_etb(h + 1)
                if h + 2 < 4:
                    load_slot(h + 2)
            pend_tail[0]()
            pend_tail[0] = None
            banks[7].acquire(PE)
            misc.inc(PE.matmul(banks[7].ap[:, 0:1], lhsT=ones[:, :], rhs=ones[:, 0:1], start=True, stop=True))
            for e in (SP, ACT, DVE):
                misc.wait(e)
                st.wait(e)
            st.wait(PE)

        if STOP == "ma":
            raise _Stop()
        run_token_phases(False)
        st.wait(SP)
        for e in (PE, ACT, DVE, GP):
            st.wait(e)
    return nc


def _tab16(v):
    return np.ascontiguousarray(np.asarray(v, np.float32).reshape(DC, 128).T)


def _const_tables():
    half = 64
    inv_freq = (10000.0 ** (-np.arange(0, half, 2, dtype=np.float32) / half)).astype(np.float32)
    tpos = np.arange(S)
    row = (tpos // 64).astype(np.float32)
    col = (tpos % 64).astype(np.float32)
    ang = np.concatenate([row[:, None] * inv_freq, col[:, None] * inv_freq], axis=-1).astype(np.float32)
    cos = np.cos(ang).astype(np.float32)
    sin = np.sin(ang).astype(np.float32)
    C = np.concatenate([cos, cos], axis=1).T
    Sn = np.concatenate([-sin, sin], axis=1).T
    slopes = (2.0 ** (-8.0 * np.arange(1, 13, dtype=np.float32) / 12.0)).astype(np.float32).reshape(3, 4)
    dil = (1, 4, 16)
    et = np.zeros((4, 128, ETOT), np.float32)
    for h in range(4):
        for g in range(3):
            k = np.arange(128)[:, None]
            c = np.arange(EW[g])[None, :]
            diff = 128 * DMAXP[g] + k - c
            ok = (diff % dil[g] == 0) & (np.abs(diff) <= 64 * dil[g])
            val = np.exp(-slopes[g, h] * np.abs(diff).astype(np.float32)).astype(np.float32)
            et[h, :, EOFF[g]:EOFF[g] + EW[g]] = np.where(ok, val, 0.0)
    return np.ascontiguousarray(C), np.ascontiguousarray(Sn), et.astype(ml_dtypes.bfloat16)


def _win_layout(w_in):
    qa = np.arange(0, 1536)
    ka = np.arange(1536, 3072)
    va = np.arange(3072, 4608)
    qb0 = 4608
    kb0 = 4608 + 1024
    vb0 = kb0 + 256
    ga0 = vb0 + 256
    gb0 = ga0 + 2048

    def swap(base):
        return np.concatenate([np.arange(base + 64, base + 128), np.arange(base, base + 64)])
    cols = [qa, ka]
    for h in range(8):
        cols.append(np.arange(qb0 + h * 128, qb0 + (h + 1) * 128))
        cols.append(swap(qb0 + h * 128))
    for h in range(2):
        cols.append(np.arange(kb0 + h * 128, kb0 + (h + 1) * 128))
        cols.append(swap(kb0 + h * 128))
    cols.append(np.arange(ga0, ga0 + 2048))
    cols.append(np.arange(gb0, gb0 + 2048))
    cols.append(va)
    cols.append(np.arange(vb0, vb0 + 256))
    idx = np.concatenate(cols)
    assert idx.shape[0] == WIN_COLS
    return np.ascontiguousarray(w_in[:, idx])


_NC_CACHE = {}


def kernel(x, c, w_ada, b_ada, norm_ffn1, w1_ffn1, w3_ffn1, w2_ffn1, norm_mix, w_in,
           q_norm_a, k_norm_a, q_norm_b, k_norm_b, w_branch_a, w_branch_b, w_out,
           norm_ffn2, w1_ffn2, w3_ffn2, w2_ffn2, norm_final):
    f = lambda a: np.ascontiguousarray(np.asarray(a, dtype=np.float32))
    x = f(x); c = f(c)
    w_ada0 = f(w_ada)[0]; b_ada0 = f(b_ada)[0]
    C, Sn, et = _const_tables()
    gt = np.concatenate([_tab16(f(norm_ffn1)[0]), _tab16(f(norm_mix)[0]), _tab16(f(norm_ffn2)[0]),
                         _tab16(f(norm_final))], axis=1)
    gqa, gka, gqb, gkb = f(q_norm_a)[0], f(k_norm_a)[0], f(q_norm_b)[0], f(k_norm_b)[0]
    sw = lambda v: np.concatenate([v[64:], v[:64]])
    qk = np.stack([gqa, gka, gqb, gkb, sw(gqb), sw(gkb), gqa, gqa], axis=1)
    qk = np.ascontiguousarray(qk.astype(np.float32))
    winp = _win_layout(f(w_in)[0])
    shared = {
        "w1a": f(w1_ffn1)[0], "w3a": f(w3_ffn1)[0], "w2a": f(w2_ffn1)[0],
        "w1b": f(w1_ffn2)[0], "w3b": f(w3_ffn2)[0], "w2b": f(w2_ffn2)[0],
        "win": winp, "wpa": f(w_branch_a)[0], "wpb": f(w_branch_b)[0], "wout": f(w_out)[0],
        "gtab": np.ascontiguousarray(gt), "qkg": qk, "etab": et,
    }
    in_maps = []
    for r in range(NCORES):
        b, g = r // 4, r % 4
        t0 = g * TOK
        vt = np.zeros((128, 32), np.float32)
        for p in range(32):
            gb = g * 16 - 8 + p
            if gb < 0 or gb >= 64:
                vt[:, p] = NEG
        m = dict(shared)
        m.update({
            "xT": np.ascontiguousarray(x[b, t0:t0 + TOK, :].T),
            "cT": _tab16(c[b]),
            "wada": np.ascontiguousarray(w_ada0[:, g * 4608:(g + 1) * 4608]),
            "bada": np.ascontiguousarray(b_ada0[g * 4608:(g + 1) * 4608].reshape(36, 128).T),
            "ropeC": np.ascontiguousarray(C[:, t0:t0 + TOK]),
            "ropeS": np.ascontiguousarray(Sn[:, t0:t0 + TOK]),
            "vtab": vt,
        })
        in_maps.append(m)
    if "nc" not in _NC_CACHE:
        _NC_CACHE["nc"] = build_program()
    nc = _NC_CACHE["nc"]
    res = run_bass_kernel_spmd(nc, in_maps, core_ids=list(range(NCORES)))
    out = np.empty((2, S, D), np.float32)
    for r in range(NCORES):
        b, g = r // 4, r % 4
        out[b, g * TOK:(g + 1) * TOK, :] = res.results[r]["yT"].T
    return out


if __name__ == "__main__":
    import time
    t0 = time.time()
    nc = build_program()
    print("build ok", time.time() - t0)
```

```python
import math
import os
from contextlib import ExitStack

import numpy as np
import ml_dtypes

import concourse.bass as bass
import concourse.mybir as mybir
from concourse.bass_utils import run_bass_kernel_spmd

F32 = mybir.dt.float32
BF16 = mybir.dt.bfloat16
AF = mybir.ActivationFunctionType
ALU = mybir.AluOpType

NCORES = 8
D = 2048
S = 8192
TOK = 2048
T = 512
NT = TOK // T
DC = D // 128
DFF = 5632
NJ = DFF // 128
HD = 128
EPS = 1e-6
SCALE = HD ** -0.5
NEG = -30000.0

FM_COLS = 1536 + 1536 + 2048 + 512 + 2048 + 2048
TM_COLS = 1536 + 256
WIN_COLS = FM_COLS + TM_COLS

DMAXP = (4, 5, 11)
DMAX = (1, 2, 8)
EW = tuple((2 * d + 1) * 128 for d in DMAXP)
EOFF = (0, EW[0], EW[0] + EW[1])
ETOT = sum(EW)


class _Stop(Exception):
    pass


STOP = os.environ.get("KSTOP", "")


class Sem:
    def __init__(self, nc, name):
        self.s = nc.alloc_semaphore(name)
        self.n = 0
        self.waited = {}

    def inc(self, ins, by=1):
        ins.then_inc(self.s, by)
        self.n += by
        return self.n

    def wait(self, eng, val=None):
        v = self.n if val is None else val
        if v <= 0:
            return
        k = id(eng)
        if self.waited.get(k, 0) >= v:
            return
        eng.wait_ge(self.s, v)
        self.waited[k] = v


class G:
    nc = None
    PROG = {}
    SIG = {}
    DSEM = {}

    @staticmethod
    def reset(nc):
        G.nc = nc
        G.PROG = {}
        G.SIG = {}
        G.DSEM = {}

    @staticmethod
    def dsem(name):
        if name not in G.DSEM:
            G.DSEM[name] = Sem(G.nc, name + "_d")
        return G.DSEM[name]

    @staticmethod
    def signal(ins, dname):
        k = id(ins)
        if k in G.SIG:
            return G.SIG[k][1]
        txt = str(ins.ins)[:24]
        eng = txt.split()[0]
        if "DMA" in txt:
            sem = G.dsem(dname)
            ev = (sem, sem.inc(ins, 16))
        else:
            if eng not in G.PROG:
                G.PROG[eng] = Sem(G.nc, "prog_" + eng)
            sem = G.PROG[eng]
            ev = (sem, sem.inc(ins, 1))
        G.SIG[k] = (ins, ev)
        return ev


class Evs:
    def __init__(self, name):
        self.name = name
        self.m = {}

    def add(self, ins, by=None):
        sem, v = G.signal(ins, self.name)
        self.m[sem] = max(self.m.get(sem, 0), v)

    inc = add

    def wait(self, eng):
        for sem, v in self.m.items():
            sem.wait(eng, v)


class Cur:
    def __init__(self, bufs):
        self.__dict__["bufs"] = bufs
        self.__dict__["i"] = 0

    def sel(self, i):
        self.__dict__["i"] = i

    def __getattr__(self, name):
        return getattr(self.bufs[self.i], name)


class Buf:
    def __init__(self, nc, name, ap):
        self.ap = ap
        self.rd = Evs(name)
        self.fr = Evs(name)

    def acquire(self, eng):
        self.fr.wait(eng)
        self.rd.wait(eng)

    def produced(self, ins, by=None):
        self.rd.add(ins)

    def wait_ready(self, eng):
        self.rd.wait(eng)

    def consumed(self, ins, by=None):
        self.fr.add(ins)


def build_program():
    holder = {}
    try:
        _build_program(holder)
    except _Stop:
        pass
    return holder["nc"]


def _build_program(holder):
    nc = bass.Bass("TRN2", target_bir_lowering=False)
    G.reset(nc)
    holder["nc"] = nc
    PE, ACT, DVE, SP, GP = nc.tensor, nc.scalar, nc.vector, nc.sync, nc.gpsimd

    def din(name, shape, dt=F32):
        return nc.dram_tensor(name, list(shape), dt, kind="ExternalInput")

    def dscr(name, shape, dt):
        return nc.dram_tensor(name, list(shape), dt)

    xT = din("xT", [D, TOK])
    cT = din("cT", [128, DC])
    wada = din("wada", [D, 4608])
    bada = din("bada", [128, 36])
    gtab_d = din("gtab", [128, 4 * DC])
    qkg_d = din("qkg", [128, 8])
    w1a = din("w1a", [D, DFF]); w3a = din("w3a", [D, DFF]); w2a = din("w2a", [DFF, D])
    w1b = din("w1b", [D, DFF]); w3b = din("w3b", [D, DFF]); w2b = din("w2b", [DFF, D])
    win = din("win", [D, WIN_COLS])
    wpa = din("wpa", [512, D]); wpb = din("wpb", [1024, D]); wout = din("wout", [D, D])
    ropeC_d = din("ropeC", [128, TOK]); ropeS_d = din("ropeS", [128, TOK])
    etab_d = din("etab", [4, 128, ETOT], BF16)
    vtab_d = din("vtab", [128, 32])
    yT = nc.dram_tensor("yT", [D, TOK], F32, kind="ExternalOutput")

    modin = dscr("modin", [128, 36], F32)
    modout = dscr("modout", [512, 36], F32)
    h1T = dscr("h1T", [D, TOK], F32)
    qaT = dscr("qaT", [1536, TOK], BF16)
    kaTc = [dscr(f"kaT{c}", [256, TOK], BF16) for c in range(6)]
    kaGc = [dscr(f"kaG{c}", [1024, TOK], BF16) for c in range(6)]
    vaLc = [dscr(f"vaL{c}", [256, 1536], BF16) for c in range(8)]
    vaGc = [dscr(f"vaG{c}", [1024, 1536], BF16) for c in range(8)]
    qbT = dscr("qbT", [1024, TOK], BF16)
    kbTt = [dscr(f"kbT{t}", [256, T], BF16) for t in range(NT)]
    kbGt = [dscr(f"kbG{t}", [4 * 256, T], BF16) for t in range(NT)]
    vbLt = [dscr(f"vbL{t}", [T, 256], BF16) for t in range(NT)]
    vbGt = [dscr(f"vbG{t}", [4 * T, 256], BF16) for t in range(NT)]
    sgT = dscr("sgT", [2 * D, TOK], BF16)
    hKp = dscr("hKp", [1536, 1, 1024], BF16)
    hKn = dscr("hKn", [1536, 1, 1024], BF16)
    hVp = dscr("hVp", [1024, 1, 1536], BF16)
    hVn = dscr("hVn", [1024, 1, 1536], BF16)
    oaT = dscr("oaT", [512, TOK], BF16)
    obT = dscr("obT", [1024, TOK], BF16)

    def wview(w):
        return w.ap().rearrange("(kc p) n -> p kc n", p=128)

    with ExitStack() as es:
        def sb(name, shape, dt):
            return es.enter_context(nc.sbuf_tensor(name, list(shape), dt))

        ones = sb("ones", [128, 128], BF16)
        ones_f = sb("ones_f", [128, 128], F32)
        gtab = sb("gtabs", [128, 4 * DC], F32)
        qkg = sb("qkgs", [128, 8], F32)
        vtab = sb("vtabs", [128, 32], F32)
        modT = sb("modT", [128, 144], F32)
        tabA = sb("tabA", [128, 3 * DC], F32)
        tabG = sb("tabG", [128, 3 * DC], F32)
        NSLOT = 3
        slots = [Buf(nc, f"slot{i}", sb(f"slot{i}", [128, 8192], BF16)) for i in range(NSLOT)]
        psum_all = nc.alloc_psum_tensor("psum_all", [128, 4096], F32)
        banks = [Buf(nc, f"bank{i}", psum_all[:, i * 512:(i + 1) * 512]) for i in range(8)]

        ld = Evs("ld")
        ldr = Evs("ldr")
        ldb = Evs("ldb")
        st = Evs("st")
        cc = Sem(nc, "cc")
        ccb = Sem(nc, "ccb")
        cca = Sem(nc, "cca")
        groups = [[0, 1, 2, 3], [4, 5, 6, 7]]
        misc = Evs("misc")

        slot_ctr = [0]

        def wstage(parts):
            sl = slots[slot_ctr[0] % NSLOT]
            slot_ctr[0] += 1
            sl.acquire(GP)
            for vf, src in parts:
                sl.produced(GP.dma_start(out=vf(sl.ap), in_=src), 16)
            for d in list(deferred):
                d[0] -= 1
                if d[0] <= 0:
                    deferred.remove(d)
                    d[1]()
            return sl

        deferred = []

        def flush_deferred():
            for d in list(deferred):
                deferred.remove(d)
                d[1]()

        ld.inc(SP.dma_start(out=gtab[:, :], in_=gtab_d[:, :]), 16)
        ld.inc(SP.dma_start(out=qkg[:, :], in_=qkg_d[:, :]), 16)
        ld.inc(SP.dma_start(out=vtab[:, :], in_=vtab_d[:, :]), 16)
        misc.inc(DVE.memset(ones[:, :], 1.0))
        misc.inc(DVE.memset(ones_f[:, :], 1.0))
        for e in (PE, ACT, DVE):
            ld.wait(e)
            misc.wait(e)

        with ExitStack() as e0:
            cts = e0.enter_context(nc.sbuf_tensor("cts", [128, DC], F32))
            cact = e0.enter_context(nc.sbuf_tensor("cact", [128, DC], BF16))
            bad = e0.enter_context(nc.sbuf_tensor("bads", [128, 36], F32))
            modp = e0.enter_context(nc.sbuf_tensor("modp", [128, 36], F32))
            ld0 = Evs("ld0")
            ld0.inc(SP.dma_start(out=cts[:, :], in_=cT[:, :]), 16)
            ld0.inc(SP.dma_start(out=bad[:, :], in_=bada[:, :]), 16)
            ld0.wait(ACT)
            misc.inc(ACT.activation(out=cact[:, :], in_=cts[:, :], func=AF.Silu))
            misc.wait(PE)
            wv = wview(wada)
            bk = banks[7]
            bk.acquire(PE)
            last = None
            for s_ in range(9):
                sl = wstage([(lambda a: a[:, :].rearrange("p (k n) -> p k n", n=512),
                              wv[:, :, s_ * 512:(s_ + 1) * 512])])
                sl.wait_ready(PE)
                sv = sl.ap[:, :].rearrange("p (k n) -> p k n", n=512)
                for i in range(4):
                    col = s_ * 4 + i
                    for kc in range(DC):
                        last = PE.matmul(bk.ap[:, col:col + 1], lhsT=sv[:, kc, i * 128:(i + 1) * 128],
                                         rhs=cact[:, kc:kc + 1], start=(kc == 0), stop=(kc == DC - 1))
                sl.consumed(last)
            bk.produced(last)
            bk.wait_ready(DVE)
            ld0.wait(DVE)
            i_ = DVE.tensor_tensor(out=modp[:, :], in0=bk.ap[:, 0:36], in1=bad[:, :], op=ALU.add)
            bk.consumed(i_)
            misc.inc(i_)
            misc.wait(SP)
            st.inc(SP.dma_start(out=modin[:, :], in_=modp[:, :]), 16)
            st.wait(GP)
            cc.inc(GP.collective_compute("AllGather", ALU.bypass, replica_groups=[[0, 1, 2, 3], [4, 5, 6, 7]],
                                         ins=[modin.ap().opt()], outs=[modout.ap().opt()]))
            cc.wait(ACT)
            ldm = Evs("ldm")
            ldm.inc(ACT.dma_start(out=modT[:, :].rearrange("p (r c) -> p r c", c=36),
                                 in_=modout.ap().rearrange("(r p) c -> p r c", p=128)), 16)
            ldm.wait(DVE)
            for k in range(3):
                sc = modT[:, (3 * k + 1) * DC:(3 * k + 2) * DC]
                g = modT[:, (3 * k + 2) * DC:(3 * k + 3) * DC]
                misc.inc(DVE.scalar_tensor_tensor(out=tabA[:, k * DC:(k + 1) * DC], in0=sc, scalar=1.0,
                                                  in1=gtab[:, k * DC:(k + 1) * DC], op0=ALU.add, op1=ALU.mult))
                misc.inc(DVE.tensor_scalar(out=tabG[:, k * DC:(k + 1) * DC], in0=g,
                                           scalar1=(1.0 if k == 1 else 0.5), scalar2=None, op0=ALU.mult))
            for e in (ACT, DVE, PE):
                misc.wait(e)

        if STOP == "p0":
            raise _Stop()

        def tabB(k):
            return modT[:, (3 * k) * DC:(3 * k + 1) * DC]

        def run_token_phases(first):
            with ExitStack() as e1:
                def sbl(name, shape, dt):
                    return e1.enter_context(nc.sbuf_tensor(name + ("A" if first else "B"), list(shape), dt))

                xt = Buf(nc, "xt", sbl("xt", [128, DC, T], F32))
                class _Cur:
                    def __init__(self, bufs):
                        self.__dict__["bufs"] = bufs
                        self.__dict__["i"] = 0

                    def sel(self, i):
                        self.__dict__["i"] = i

                    def __getattr__(self, name):
                        return getattr(self.bufs[self.i], name)
                _u0 = Buf(nc, "u", sbl("u", [128, DC, T], BF16))
                _u1 = Buf(nc, "u2", sbl("u2", [128, DC, T], BF16)) if first else _u0
                u = _Cur([_u0, _u1])
                hid = Buf(nc, "hid", sbl("hid", [128, NJ, T], BF16))
                rstd = Buf(nc, "rstd", sbl("rstd", [128, T], F32))
                lnt = Buf(nc, "lnt", sbl("lnt", [128, T], F32))
                sqs = [Buf(nc, f"sq{i}", sbl(f"sq{i}", [128, T], BF16)) for i in range(4)]
                tmps = [Buf(nc, f"tmp{i}", sbl(f"tmp{i}", [128, T], F32)) for i in range(3)]
                sils = [Buf(nc, f"sil{i}", sbl(f"sil{i}", [128, T], F32)) for i in range(2)]
                osts = [Buf(nc, f"ost{i}", sbl(f"ost{i}", [128, T], BF16)) for i in range(4)]
                ost_c = [0]

                def rms_rstd(src_bank_or_none, src_chunks, nfeat, statbank, split=False):
                    n = len(src_chunks)
                    statbank.acquire(PE)
                    for i, (ap, wfn, cfn) in enumerate(src_chunks):
                        sq = sqs[i % 4]
                        if split and i % 2 == 1:
                            sq.acquire(DVE)
                            wfn(DVE)
                            a = DVE.tensor_tensor(out=sq.ap[:, :], in0=ap, in1=ap, op=ALU.mult)
                        else:
                            sq.acquire(ACT)
                            wfn(ACT)
                            a = ACT.activation(out=sq.ap[:, :], in_=ap, func=AF.Square)
                        sq.produced(a)
                        cfn(a)
                        sq.wait_ready(PE)
                        m = PE.matmul(statbank.ap[:, :], lhsT=ones[:, :], rhs=sq.ap[:, :],
                                      start=(i == 0), stop=(i == n - 1))
                        sq.consumed(m)
                    statbank.produced(m)
                    statbank.wait_ready(ACT)
                    lnt.acquire(ACT)
                    a = ACT.activation(out=lnt.ap[:, :], in_=statbank.ap[:, :], func=AF.Ln,
                                       scale=1.0 / nfeat, bias=EPS)
                    statbank.consumed(a)
                    lnt.produced(a)
                    lnt.wait_ready(ACT)
                    rstd.acquire(ACT)
                    a = ACT.activation(out=rstd.ap[:, :], in_=lnt.ap[:, :], func=AF.Exp, scale=-0.5)
                    lnt.consumed(a)
                    rstd.produced(a)

                uch = [[None] * DC, [None] * DC]

                def norm_mod(k):
                    chunks = [(xt.ap[:, dc, :], xt.wait_ready, xt.consumed) for dc in range(DC)]
                    rms_rstd(None, chunks, D, banks[6], split=True)
                    rstd.wait_ready(DVE)
                    xt.wait_ready(DVE)
                    u.acquire(ACT)
                    for dc in range(DC):
                        tmp = tmps[dc % 3]
                        tmp.acquire(DVE)
                        i_ = DVE.scalar_tensor_tensor(out=tmp.ap[:, :], in0=xt.ap[:, dc, :],
                                                      scalar=tabA[:, k * DC + dc:k * DC + dc + 1],
                                                      in1=rstd.ap[:, :], op0=ALU.mult, op1=ALU.mult)
                        tmp.produced(i_)
                        xt.consumed(i_)
                        if dc == DC - 1:
                            rstd.consumed(i_)
                        tmp.wait_ready(ACT)
                        a = ACT.activation(out=u.ap[:, dc, :], in_=tmp.ap[:, :], func=AF.Identity,
                                           bias=tabB(k)[:, dc:dc + 1], scale=1.0)
                        tmp.consumed(a)
                        u.produced(a)
                        ev = Evs("uch")
                        ev.add(a)
                        uch[u.i][dc] = ev

                def ffn(k, w1, w3, w2, do_norm=True):
                    if do_norm:
                        norm_mod(k)
                    w1v, w3v = wview(w1), wview(w3)
                    w2v = w2.ap().rearrange("(j p) n -> p j n", p=128)
                    hid.acquire(DVE)
                    for s_ in range(NJ // 2):
                        sl = wstage([
                            (lambda a: a[:, 0:4096].rearrange("p (k n) -> p k n", n=256), w1v[:, :, s_ * 256:(s_ + 1) * 256]),
                            (lambda a: a[:, 4096:8192].rearrange("p (k n) -> p k n", n=256), w3v[:, :, s_ * 256:(s_ + 1) * 256]),
                        ])
                        sl.wait_ready(PE)
                        v1 = sl.ap[:, 0:4096].rearrange("p (k n) -> p k n", n=256)
                        v3 = sl.ap[:, 4096:8192].rearrange("p (k n) -> p k n", n=256)
                        for jj in range(2):
                            j = 2 * s_ + jj
                            b1 = banks[j % 2]
                            b3 = banks[2 + j % 2]
                            b1.acquire(PE)
                            for kc in range(DC):
                                if j == 0:
                                    uch[u.i][kc].wait(PE)
                                m = PE.matmul(b1.ap[:, :], lhsT=v1[:, kc, jj * 128:(jj + 1) * 128], rhs=u.ap[:, kc, :],
                                              start=(kc == 0), stop=(kc == DC - 1))
                            b1.produced(m)
                            b3.acquire(PE)
                            for kc in range(DC):
                                m = PE.matmul(b3.ap[:, :], lhsT=v3[:, kc, jj * 128:(jj + 1) * 128], rhs=u.ap[:, kc, :],
                                              start=(kc == 0), stop=(kc == DC - 1))
                            b3.produced(m)
                            if jj == 1:
                                sl.consumed(m)
                            if j == NJ - 1:
                                u.consumed(m)
                            sil = sils[j % 2]
                            b1.wait_ready(ACT)
                            sil.acquire(ACT)
                            a = ACT.activation(out=sil.ap[:, :], in_=b1.ap[:, :], func=AF.Silu)
                            b1.consumed(a)
                            sil.produced(a)
                            sil.wait_ready(DVE)
                            b3.wait_ready(DVE)
                            d_ = DVE.tensor_tensor(out=hid.ap[:, j, :], in0=sil.ap[:, :], in1=b3.ap[:, :], op=ALU.mult)
                            sil.consumed(d_)
                            b3.consumed(d_)
                            hid.produced(d_)
                    hid.wait_ready(PE)
                    HJ = NJ // 2
                    for dp in range(DC // 2):
                        for half in range(2):
                            sl = wstage([(lambda a: a[:, 0:HJ * 256].rearrange("p (j n) -> p j n", n=256),
                                          w2v[:, half * HJ:(half + 1) * HJ, dp * 256:(dp + 1) * 256])])
                            sl.wait_ready(PE)
                            v2 = sl.ap[:, 0:HJ * 256].rearrange("p (j n) -> p j n", n=256)
                            for i in range(2):
                                db = dp * 2 + i
                                yb = banks[4 + (dp % 2) * 2 + i]
                                if half == 0:
                                    yb.acquire(PE)
                                for jj in range(HJ):
                                    j = half * HJ + jj
                                    m = PE.matmul(yb.ap[:, :], lhsT=v2[:, jj, i * 128:(i + 1) * 128], rhs=hid.ap[:, j, :],
                                                  start=(j == 0), stop=(j == NJ - 1))
                                if i == 1:
                                    sl.consumed(m)
                                if half == 1:
                                    yb.produced(m)
                                    if db == DC - 1:
                                        hid.consumed(m)
                                    yb.wait_ready(DVE)
                                    xt.acquire(DVE)
                                    d_ = DVE.scalar_tensor_tensor(out=xt.ap[:, db, :], in0=yb.ap[:, :],
                                                                  scalar=tabG[:, k * DC + db:k * DC + db + 1],
                                                                  in1=xt.ap[:, db, :], op0=ALU.mult, op1=ALU.add)
                                    yb.consumed(d_)
                                    xt.produced(d_)

                def store_stage(dst_ap, eng_producer_fn, extra=None):
                    o = osts[ost_c[0] % 4]
                    ost_c[0] += 1
                    eng, ins = eng_producer_fn(o)
                    o.produced(ins)
                    o.wait_ready(SP)
                    dm = SP.dma_start(out=dst_ap, in_=o.ap[:, :])
                    o.consumed(dm, 16)
                    st.inc(dm, 16)
                    if extra is not None:
                        extra.add(dm)

                if first:
                    ropeC = sbl("ropeC", [128, TOK], F32)
                    ropeS = sbl("ropeS", [128, TOK], F32)
                    winv = wview(win)
                    for t in range(NT):
                        tsl = slice(t * T, (t + 1) * T)
                        if t == 0:
                            xt.acquire(SP)
                            xt.produced(SP.dma_start(out=xt.ap[:, :, :],
                                                     in_=xT.ap().rearrange("(c p) t -> p c t", p=128)[:, :, tsl]), 16)
                            ldr.inc(SP.dma_start(out=ropeC[:, :], in_=ropeC_d[:, :]), 16)
                            ldr.inc(SP.dma_start(out=ropeS[:, :], in_=ropeS_d[:, :]), 16)
                        u.sel(0)
                        ffn(0, w1a, w3a, w2a, do_norm=(t == 0))
                        xt.wait_ready(SP)
                        dm = SP.dma_start(out=h1T.ap().rearrange("(c p) t -> p c t", p=128)[:, :, tsl], in_=xt.ap[:, :, :])
                        xt.consumed(dm, 16)
                        st.inc(dm, 16)
                        u.sel(1)
                        norm_mod(1)
                        if t + 1 < NT:
                            nsl = slice((t + 1) * T, (t + 2) * T)
                            xt.acquire(SP)
                            xt.produced(SP.dma_start(out=xt.ap[:, :, :],
                                                     in_=xT.ap().rearrange("(c p) t -> p c t", p=128)[:, :, nsl]), 16)
                        first_pb = [True]
                        ldr.wait(DVE)
                        hb = 0

                        def fm_stage(si):
                            return wstage([(lambda a: a[:, :].rearrange("p (k n) -> p k n", n=512),
                                            winv[:, :, si * 512:(si + 1) * 512])])

                        def proj_block(sl, bi, bank, last_of_slot):
                            sv = sl.ap[:, :].rearrange("p (k n) -> p k n", n=512)
                            bank.acquire(PE)
                            for kc in range(DC):
                                if first_pb[0]:
                                    uch[u.i][kc].wait(PE)
                                m = PE.matmul(bank.ap[:, :], lhsT=sv[:, kc, bi * 128:(bi + 1) * 128], rhs=u.ap[:, kc, :],
                                              start=(kc == 0), stop=(kc == DC - 1))
                            first_pb[0] = False
                            bank.produced(m)
                            if last_of_slot:
                                sl.consumed(m)
                            return m

                        ssb_c = [0]

                        def qk_post(qbank, pbank, gcol, gpcol, dst_ap, extra=None):
                            ssb = banks[4 + ssb_c[0] % 2] if pbank is None else banks[6 + ssb_c[0] % 2]
                            ssb_c[0] += 1
                            rms_rstd(None, [(qbank.ap[:, :], qbank.wait_ready, lambda a: None)], HD, ssb)
                            rstd.wait_ready(DVE)
                            qbank.wait_ready(DVE)
                            if pbank is None:
                                def prod(o):
                                    o.acquire(DVE)
                                    i_ = DVE.scalar_tensor_tensor(out=o.ap[:, :], in0=qbank.ap[:, :],
                                                                  scalar=qkg[:, gcol:gcol + 1], in1=rstd.ap[:, :],
                                                                  op0=ALU.mult, op1=ALU.mult)
                                    qbank.consumed(i_)
                                    rstd.consumed(i_)
                                    return DVE, i_
                                store_stage(dst_ap, prod)
                            else:
                                t1, t2 = tmps[0], tmps[1]
                                t1.acquire(DVE)
                                i_ = DVE.scalar_tensor_tensor(out=t1.ap[:, :], in0=qbank.ap[:, :],
                                                              scalar=qkg[:, gcol:gcol + 1], in1=ropeC[:, tsl],
                                                              op0=ALU.mult, op1=ALU.mult)
                                qbank.consumed(i_)
                                t1.produced(i_)
                                t2.acquire(DVE)
                                pbank.wait_ready(DVE)
                                i_ = DVE.scalar_tensor_tensor(out=t2.ap[:, :], in0=pbank.ap[:, :],
                                                              scalar=qkg[:, gpcol:gpcol + 1], in1=ropeS[:, tsl],
                                                              op0=ALU.mult, op1=ALU.mult)
                                pbank.consumed(i_)
                                t2.produced(i_)
                                t3 = tmps[2]
                                t3.acquire(DVE)
                                t1.wait_ready(DVE)
                                t2.wait_ready(DVE)
                                i_ = DVE.tensor_tensor(out=t3.ap[:, :], in0=t1.ap[:, :], in1=t2.ap[:, :], op=ALU.add)
                                t1.consumed(i_)
                                t2.consumed(i_)
                                t3.produced(i_)

                                def prod(o):
                                    o.acquire(DVE)
                                    t3.wait_ready(DVE)
                                    i2 = DVE.tensor_tensor(out=o.ap[:, :], in0=t3.ap[:, :], in1=rstd.ap[:, :], op=ALU.mult)
                                    t3.consumed(i2)
                                    rstd.consumed(i2)
                                    return DVE, i2
                                store_stage(dst_ap, prod, extra)

                        lastm = [None]
                        kvst = Evs("kvst")

                        def do_qk_plain(stages):
                            nonlocal hb
                            for si in stages:
                                sl = fm_stage(si)
                                sl.wait_ready(PE)
                                for bi in range(4):
                                    head = (si % 3) * 4 + bi
                                    bank = banks[hb % 4]
                                    hb += 1
                                    lastm[0] = proj_block(sl, bi, bank, bi == 3)
                                    if si < 3:
                                        dst = qaT.ap()[head * 128:(head + 1) * 128, tsl]
                                    else:
                                        dst = kaTc[head // 2].ap()[(head % 2) * 128:(head % 2 + 1) * 128, tsl]
                                    qk_post(bank, None, 0 if si < 3 else 1, None, dst)

                        def do_rope(stages):
                            nonlocal hb
                            for si in stages:
                                sl = fm_stage(si)
                                sl.wait_ready(PE)
                                for hh in range(2):
                                    qbank = banks[(hb) % 4]
                                    pbank = banks[(hb + 1) % 4]
                                    hb += 2
                                    proj_block(sl, 2 * hh, qbank, False)
                                    lastm[0] = proj_block(sl, 2 * hh + 1, pbank, hh == 1)
                                    if si < 10:
                                        head = (si - 6) * 2 + hh
                                        dst = qbT.ap()[head * 128:(head + 1) * 128, tsl]
                                        qk_post(qbank, pbank, 2, 4, dst)
                                    else:
                                        dst = kbTt[t].ap()[hh * 128:(hh + 1) * 128, :]
                                        qk_post(qbank, pbank, 3, 5, dst, kvst)

                        def do_gates():
                            nonlocal hb
                            for si in range(11, 19):
                                sl = fm_stage(si)
                                sl.wait_ready(PE)
                                for bi in range(4):
                                    bank = banks[hb % 4]
                                    hb += 1
                                    lastm[0] = proj_block(sl, bi, bank, bi == 3)
                                    row = (si - 11) * 4 + bi

                                    def prod(o, bank=bank):
                                        o.acquire(ACT)
                                        bank.wait_ready(ACT)
                                        a = ACT.activation(out=o.ap[:, :], in_=bank.ap[:, :], func=AF.Sigmoid)
                                        bank.consumed(a)
                                        return ACT, a
                                    store_stage(sgT.ap()[row * 128:(row + 1) * 128, tsl], prod)

                        def do_v(vis):
                            nonlocal hb
                            for vi in vis:
                                ncol = 512 if vi < 3 else 256
                                c0 = FM_COLS + vi * 512
                                sl = wstage([(lambda a, ncol=ncol: a[:, 0:DC * ncol].rearrange("p (k n) -> p k n", n=ncol),
                                              winv[:, :, c0:c0 + ncol])])
                                sl.wait_ready(PE)
                                sv = sl.ap[:, 0:DC * ncol].rearrange("p (k n) -> p k n", n=ncol)
                                for tb in range(4):
                                    bank = banks[hb % 4]
                                    hb += 1
                                    bank.acquire(PE)
                                    for kc in range(DC):
                                        m = PE.matmul(bank.ap[:, 0:ncol], lhsT=u.ap[:, kc, tb * 128:(tb + 1) * 128], rhs=sv[:, kc, :],
                                                      start=(kc == 0), stop=(kc == DC - 1))
                                    bank.produced(m)
                                    lastm[0] = m
                                    if tb == 3:
                                        sl.consumed(m)
                                    r0 = t * T + tb * 128
                                    if vi < 3:
                                        dst = vaLc[r0 // 256].ap()[r0 % 256:r0 % 256 + 128, vi * 512:(vi + 1) * 512]
                                    else:
                                        dst = vbLt[t].ap()[tb * 128:(tb + 1) * 128, :]
                                    o = osts[ost_c[0] % 4]
                                    ost_c[0] += 1
                                    o.acquire(DVE)
                                    bank.wait_ready(DVE)
                                    i_ = DVE.tensor_copy(out=o.ap[:, 0:ncol], in_=bank.ap[:, 0:ncol])
                                    bank.consumed(i_)
                                    o.produced(i_)
                                    o.wait_ready(SP)
                                    dm = SP.dma_start(out=dst, in_=o.ap[:, 0:ncol])
                                    o.consumed(dm, 16)
                                    st.inc(dm, 16)
                                    if vi == 3:
                                        kvst.add(dm)

                        do_rope([10])
                        do_v([3])

                        def trig(t=t, kvst=kvst):
                            kvst.wait(GP)
                            ccb.inc(GP.collective_compute("AllGather", ALU.bypass, replica_groups=groups,
                                                          ins=[kbTt[t].ap().opt()], outs=[kbGt[t].ap().opt()]))
                            ccb.inc(GP.collective_compute("AllGather", ALU.bypass, replica_groups=groups,
                                                          ins=[vbLt[t].ap().opt()], outs=[vbGt[t].ap().opt()]))
                        deferred.append([3, trig])
                        do_qk_plain([3, 4, 5])
                        do_v([0, 1, 2])
                        do_qk_plain([0, 1, 2])
                        if t + 1 < NT:
                            u.sel(0)
                            norm_mod(0)
                            u.sel(1)
                        do_rope([6, 7, 8, 9])
                        do_gates()
                        u.consumed(lastm[0])
                else:
                    oat = Buf(nc, "oat", sbl("oat", [128, 4, T], BF16))
                    obt = Buf(nc, "obt", sbl("obt", [128, 8, T], BF16))
                    sga = [Buf(nc, f"sga{i}", sbl(f"sga{i}", [128, 4, T], BF16)) for i in range(2)]
                    sgb = [Buf(nc, f"sgb{i}", sbl(f"sgb{i}", [128, 4, T], BF16)) for i in range(2)]
                    wpav = wpa.ap().rearrange("(s p) n -> p s n", p=128)
                    wpbv = wpb.ap().rearrange("(s p) n -> p s n", p=128)
                    woutv = wview(wout)
                    for t in range(NT):
                        tsl = slice(t * T, (t + 1) * T)
                        def load_ops(tt):
                            sl_ = slice(tt * T, (tt + 1) * T)
                            oat.acquire(SP)
                            oat.produced(SP.dma_start(out=oat.ap[:, :, :],
                                                      in_=oaT.ap().rearrange("(s p) t -> p s t", p=128)[:, :, sl_]), 16)
                            obt.acquire(SP)
                            obt.produced(SP.dma_start(out=obt.ap[:, :, :],
                                                      in_=obT.ap().rearrange("(s p) t -> p s t", p=128)[:, :, sl_]), 16)
                            for s0 in range(2):
                                sg_load(s0, sl_)

                        def sg_load(s0, sl_):
                            ga0 = sga[s0 % 2]
                            gb0 = sgb[s0 % 2]
                            ga0.acquire(SP)
                            ga0.produced(SP.dma_start(out=ga0.ap[:, :, :], in_=sgv[:, s0 * 4:(s0 + 1) * 4, sl_]), 16)
                            gb0.acquire(SP)
                            gb0.produced(SP.dma_start(out=gb0.ap[:, :, :], in_=sgv[:, DC + s0 * 4:DC + (s0 + 1) * 4, sl_]), 16)
                        sgv = sgT.ap().rearrange("(c p) t -> p c t", p=128)
                        if t == 0:
                            load_ops(0)
                        oat.wait_ready(PE)
                        obt.wait_ready(PE)
                        u.acquire(DVE)
                        for s_ in range(4):
                            ga_ = sga[s_ % 2]
                            gb_ = sgb[s_ % 2]
                            if s_ >= 2:
                                sg_load(s_, tsl)
                            sl = wstage([
                                (lambda a: a[:, 0:2048].rearrange("p (s n) -> p s n", n=512), wpav[:, :, s_ * 512:(s_ + 1) * 512]),
                                (lambda a: a[:, 2048:6144].rearrange("p (s n) -> p s n", n=512), wpbv[:, :, s_ * 512:(s_ + 1) * 512]),
                            ])
                            sl.wait_ready(PE)
                            va_ = sl.ap[:, 0:2048].rearrange("p (s n) -> p s n", n=512)
                            vb_ = sl.ap[:, 2048:6144].rearrange("p (s n) -> p s n", n=512)
                            for i in range(4):
                                db = s_ * 4 + i
                                ba = banks[db % 2]
                                bb = banks[2 + db % 2]
                                ba.acquire(PE)
                                for s2 in range(4):
                                    m = PE.matmul(ba.ap[:, :], lhsT=va_[:, s2, i * 128:(i + 1) * 128], rhs=oat.ap[:, s2, :],
                                                  start=(s2 == 0), stop=(s2 == 3))
                                ba.produced(m)
                                bb.acquire(PE)
                                for s2 in range(8):
                                    m = PE.matmul(bb.ap[:, :], lhsT=vb_[:, s2, i * 128:(i + 1) * 128], rhs=obt.ap[:, s2, :],
                                                  start=(s2 == 0), stop=(s2 == 7))
                                bb.produced(m)
                                if i == 3:
                                    sl.consumed(m)
                                    if s_ == 3:
                                        oat.consumed(m)
                                        obt.consumed(m)
                                t1, t2 = tmps[0], tmps[1]
                                ga_.wait_ready(DVE)
                                gb_.wait_ready(DVE)
                                t1.acquire(DVE)
                                ba.wait_ready(DVE)
                                i_ = DVE.tensor_tensor(out=t1.ap[:, :], in0=ba.ap[:, :], in1=ga_.ap[:, i, :], op=ALU.mult)
                                ba.consumed(i_)
                                t1.produced(i_)
                                t2.acquire(DVE)
                                bb.wait_ready(DVE)
                                i_ = DVE.tensor_tensor(out=t2.ap[:, :], in0=bb.ap[:, :], in1=gb_.ap[:, i, :], op=ALU.mult)
                                bb.consumed(i_)
                                t2.produced(i_)
                                if i == 3:
                                    ga_.consumed(i_)
                                    gb_.consumed(i_)
                                t1.wait_ready(DVE)
                                t2.wait_ready(DVE)
                                i_ = DVE.tensor_tensor(out=u.ap[:, db, :], in0=t1.ap[:, :], in1=t2.ap[:, :], op=ALU.add)
                                t1.consumed(i_)
                                t2.consumed(i_)
                                u.produced(i_)
                        xt.acquire(SP)
                        xt.produced(SP.dma_start(out=xt.ap[:, :, :],
                                                 in_=h1T.ap().rearrange("(c p) t -> p c t", p=128)[:, :, tsl]), 16)
                        if t + 1 < NT:
                            load_ops(t + 1)
                        u.wait_ready(PE)
                        for s_ in range(4):
                            sl = wstage([(lambda a: a[:, :].rearrange("p (k n) -> p k n", n=512),
                                          woutv[:, :, s_ * 512:(s_ + 1) * 512])])
                            sl.wait_ready(PE)
                            sv = sl.ap[:, :].rearrange("p (k n) -> p k n", n=512)
                            for i in range(4):
                                db = s_ * 4 + i
                                yb = banks[4 + db % 2]
                                yb.acquire(PE)
                                for kc in range(DC):
                                    m = PE.matmul(yb.ap[:, :], lhsT=sv[:, kc, i * 128:(i + 1) * 128], rhs=u.ap[:, kc, :],
                                                  start=(kc == 0), stop=(kc == DC - 1))
                                yb.produced(m)
                                if i == 3:
                                    sl.consumed(m)
                                    if s_ == 3:
                                        u.consumed(m)
                                yb.wait_ready(DVE)
                                xt.acquire(DVE)
                                d_ = DVE.scalar_tensor_tensor(out=xt.ap[:, db, :], in0=yb.ap[:, :],
                                                              scalar=tabG[:, DC + db:DC + db + 1],
                                                              in1=xt.ap[:, db, :], op0=ALU.mult, op1=ALU.add)
                                yb.consumed(d_)
                                xt.produced(d_)
                        ffn(2, w1b, w3b, w2b)
                        chunks = [(xt.ap[:, dc, :], xt.wait_ready, xt.consumed) for dc in range(DC)]
                        rms_rstd(None, chunks, D, banks[6], split=True)
                        rstd.wait_ready(DVE)
                        xt.acquire(DVE)
                        for dc in range(DC):
                            d_ = DVE.scalar_tensor_tensor(out=xt.ap[:, dc, :], in0=xt.ap[:, dc, :],
                                                          scalar=gtab[:, 3 * DC + dc:3 * DC + dc + 1],
                                                          in1=rstd.ap[:, :], op0=ALU.mult, op1=ALU.mult)
                            xt.produced(d_)
                        rstd.consumed(d_)
                        xt.wait_ready(SP)
                        dm = SP.dma_start(out=yT.ap().rearrange("(c p) t -> p c t", p=128)[:, :, tsl], in_=xt.ap[:, :, :])
                        xt.consumed(dm, 16)
                        st.inc(dm, 16)

        run_token_phases(True)
        if STOP == "ab":
            raise _Stop()

        st.wait(GP)
        flush_deferred()
        for e in (PE, ACT, DVE, SP):
            st.wait(e)

        if STOP == "kv":
            raise _Stop()
        def attn_tail(ob, db_, dst_ap, rden, osta):
            db_.wait_ready(ACT)
            rden.acquire(ACT)
            a_ = ACT.activation(out=rden.ap[:, :], in_=db_.ap[:, :], func=AF.Ln)
            db_.consumed(a_)
            lnev = Evs("lnev")
            lnev.add(a_)
            lnev.wait(ACT)
            a_ = ACT.activation(out=rden.ap[:, :], in_=rden.ap[:, :], func=AF.Exp, scale=-1.0)
            rden.produced(a_)
            rden.wait_ready(DVE)
            ob.wait_ready(DVE)
            osta.acquire(DVE)
            i_ = DVE.tensor_tensor(out=osta.ap[:, :], in0=ob.ap[:, :], in1=rden.ap[:, :], op=ALU.mult)
            ob.consumed(i_)
            rden.consumed(i_)
            osta.produced(i_)
            osta.wait_ready(SP)
            dm = SP.dma_start(out=dst_ap, in_=osta.ap[:, :])
            osta.consumed(dm, 16)
            st.inc(dm, 16)

        with ExitStack() as e2:
            def sbl(name, shape, dt):
                return e2.enter_context(nc.sbuf_tensor(name, list(shape), dt))
            kbt = sbl("kbt", [128, 2, S], BF16)
            vbt = sbl("vbt", [128, 64, 256], BF16)
            qbt = sbl("qbt", [128, 8, TOK], BF16)
            spairs = [Buf(nc, f"bank{2 * i}", psum_all[:, i * 1024:(i + 1) * 1024]) for i in range(2)]
            pps = [Buf(nc, ("sq%d" % i if i < 2 else "sil0"), sbl(f"pp{i}", [128, 2 * T], BF16)) for i in range(3)]
            rden = Buf(nc, "rstd", sbl("rdenb", [128, T], F32))
            ostb = [Buf(nc, f"ost{i}", sbl(f"ostb{i}", [128, T], BF16)) for i in range(2)]
            paccs = [Buf(nc, f"tmp{i}", sbl(f"pacc{i}", [128, T], F32)) for i in range(2)]
            accw = [[sbl(f"accw{i}{j}", [128, 2 * T], F32) for j in range(2)] for i in range(2)]
            ccb.wait(SP)
            ldv = Evs("ldv")
            ldb1 = Evs("ldb1")

            def k_load(kvh, evs):
                for tt in range(NT):
                    evs.inc(SP.dma_start(
                        out=kbt[:, kvh, :].rearrange("p (r t c) -> p r t c", r=4, t=NT)[:, :, tt, :],
                        in_=kbGt[tt].ap().rearrange("(r h p) c -> h p r c", h=2, p=128)[kvh]), 16)
            qv = qbT.ap().rearrange("(h p) t -> p h t", p=128)
            k_load(0, ldb)
            ldb.inc(SP.dma_start(out=qbt[:, 0:1, :], in_=qv[:, 0:1, :]), 16)
            for tt in range(NT):
                for r in range(4):
                    ldv.inc(SP.dma_start(
                        out=vbt[:, r * 16 + tt * 4:r * 16 + tt * 4 + 4, :],
                        in_=vbGt[tt].ap()[r * T:(r + 1) * T, :].rearrange("(b p) c -> p b c", p=128)), 16)
            ldq = Evs("ldq")
            ldq.inc(SP.dma_start(out=qbt[:, 1:4, :], in_=qv[:, 1:4, :]), 16)
            k_load(1, ldb1)
            ldb1.inc(SP.dma_start(out=qbt[:, 4:8, :], in_=qv[:, 4:8, :]), 16)
            ldv.wait(GP)
            ldq.wait(GP)
            ldb.wait(GP)
            ldb1.wait(GP)
            for c in range(6):
                cca.inc(GP.collective_compute("AllGather", ALU.bypass, replica_groups=groups,
                                              ins=[kaTc[c].ap().opt()], outs=[kaGc[c].ap().opt()]))
            for c in range(8):
                cca.inc(GP.collective_compute("AllGather", ALU.bypass, replica_groups=groups,
                                              ins=[vaLc[c].ap().opt()], outs=[vaGc[c].ap().opt()]))

            ldb.wait(PE)
            def emit_halo():
                cca.wait(SP)
                cca.wait(GP)
                pid = GP.partition_id()
                prv = (pid + 3) % 4
                nxt = (pid + 1) % 4
                pid2 = SP.partition_id()
                prv2 = (pid2 + 3) % 4
                nxt2 = (pid2 + 1) % 4
                hal = Evs("hal")
                for c in range(6):
                    kx = kaGc[c].ap().rearrange("(r x) t -> x r t", r=4)
                    hal.inc(GP.dma_start(out=hKp.ap()[c * 256:(c + 1) * 256], in_=kx[:, bass.ds(prv, 1), 1024:2048]), 16)
                    hal.inc(GP.dma_start(out=hKn.ap()[c * 256:(c + 1) * 256], in_=kx[:, bass.ds(nxt, 1), 0:1024]), 16)
                hal2 = Evs("hal2")
                for c in range(4):
                    vp = vaGc[4 + c].ap().rearrange("(r t) c -> t r c", r=4)
                    vn = vaGc[c].ap().rearrange("(r t) c -> t r c", r=4)
                    hal2.inc(SP.dma_start(out=hVp.ap()[c * 256:(c + 1) * 256], in_=vp[:, bass.ds(prv2, 1), :]), 16)
                    hal2.inc(SP.dma_start(out=hVn.ap()[c * 256:(c + 1) * 256], in_=vn[:, bass.ds(nxt2, 1), :]), 16)

                return hal, hal2
            halo_evs = [None]
            pend_tail = [None]
            it = 0
            for hq in range(8):
                kv = hq // 4
                if hq == 1:
                    ldq.wait(PE)
                if hq == 4:
                    ldb1.wait(PE)
                if hq == 6:
                    halo_evs[0] = emit_halo()
                for qt in range(NT):
                    tsl = slice(qt * T, (qt + 1) * T)
                    ob = banks[4 + it % 2]
                    db_ = banks[6 + it % 2]
                    ob.acquire(PE)
                    db_.acquire(PE)
                    NKB = S // 128

                    NP = NKB // 2
                    pacc = paccs[it % 2]
                    accs = accw[it % 2]
                    chain = [Evs("chain0"), Evs("chain1")]

                    def s_mm(j):
                        sp = spairs[j % 2]
                        sp.acquire(PE)
                        for hh in range(2):
                            kb = 2 * j + hh
                            m = PE.matmul(sp.ap[:, hh * T:(hh + 1) * T], lhsT=kbt[:, kv, kb * 128:(kb + 1) * 128],
                                          rhs=qbt[:, hq, tsl], start=True, stop=True)
                        sp.produced(m)
                        p = pps[j % 3]
                        sp.wait_ready(ACT)
                        p.acquire(ACT)
                        a = ACT.activation(out=p.ap[:, :], in_=sp.ap[:, :], func=AF.Exp, scale=SCALE)
                        sp.consumed(a)
                        p.produced(a)

                    def pv_mm(j):
                        p = pps[j % 3]
                        ldv.wait(PE)
                        p.wait_ready(PE)
                        for hh in range(2):
                            kb = 2 * j + hh
                            m = PE.matmul(ob.ap[:, :], lhsT=vbt[:, kb, kv * 128:(kv + 1) * 128], rhs=p.ap[:, hh * T:(hh + 1) * T],
                                          start=(kb == 0), stop=(kb == NKB - 1))
                        p.consumed(m)
                        p.wait_ready(DVE)
                        acc = accs[j % 2]
                        if j < 2:
                            if j == 0:
                                pacc.acquire(DVE)
                            d_ = DVE.tensor_copy(out=acc[:, :], in_=p.ap[:, :])
                        else:
                            chain[j % 2].wait(DVE)
                            d_ = DVE.tensor_tensor(out=acc[:, :], in0=acc[:, :], in1=p.ap[:, :], op=ALU.add)
                        p.consumed(d_)
                        chain[j % 2].add(d_)
                        if j == NP - 1:
                            chain[0].wait(DVE)
                            chain[1].wait(DVE)
                            d2 = DVE.tensor_tensor(out=accs[0][:, :], in0=accs[0][:, :], in1=accs[1][:, :], op=ALU.add)
                            fin = Evs("fin")
                            fin.add(d2)
                            fin.wait(DVE)
                            d3 = DVE.tensor_tensor(out=pacc.ap[:, :], in0=accs[0][:, 0:T], in1=accs[0][:, T:2 * T], op=ALU.add)
                            pacc.produced(d3)
                        return m
                    s_mm(0)
                    s_mm(1)
                    for j in range(NP):
                        if j + 2 < NP:
                            s_mm(j + 2)
                        m = pv_mm(j)
                        if j == 2 and pend_tail[0] is not None:
                            pend_tail[0]()
                            pend_tail[0] = None
                    ob.produced(m)
                    pacc.wait_ready(PE)
                    m = PE.matmul(db_.ap[:, :], lhsT=ones_f[:, :], rhs=pacc.ap[:, :], start=True, stop=True)
                    pacc.consumed(m)
                    db_.produced(m)
                    pend_tail[0] = (lambda ob=ob, db_=db_, dst=obT.ap()[hq * 128:(hq + 1) * 128, tsl], o_=ostb[it % 2]:
                                    attn_tail(ob, db_, dst, rden, o_))
                    it += 1
            pend_tail[0]()
            pend_tail[0] = None
            banks[7].acquire(PE)
            misc.inc(PE.matmul(banks[7].ap[:, 0:1], lhsT=ones[:, :], rhs=ones[:, 0:1], start=True, stop=True))
            for e in (SP, ACT, DVE):
                misc.wait(e)
                st.wait(e)
            st.wait(PE)

        if STOP == "mb":
            raise _Stop()
        with ExitStack() as e3:
            def sbl(name, shape, dt):
                return e3.enter_context(nc.sbuf_tensor(name, list(shape), dt))
            kwin = Cur([Buf(nc, f"kwin{i}", sbl(f"kwin{i}", [128, 3, 4096], BF16)) for i in range(2)])
            vwin = Cur([Buf(nc, f"vwin{i}", sbl(f"vwin{i}", [128, 32, 384], BF16)) for i in range(2)])
            qwin = Cur([Buf(nc, f"qwin{i}", sbl(f"qwin{i}", [128, 3, TOK], BF16)) for i in range(2)])
            etb = Buf(nc, "etb", sbl("etb", [128, ETOT], BF16))
            sx = [Buf(nc, (f"tmp{i}" if i < 3 else "lnt"), sbl(f"sx{i}", [128, T], F32)) for i in range(4)]
            ps = [Buf(nc, ("sq%d" % i if i < 2 else ("sil%d" % (i - 2) if i < 4 else "hid")), sbl(f"pa{i}", [128, T], BF16)) for i in range(5)]
            rden = Buf(nc, "rstd", sbl("rdena", [128, T], F32))
            osta = [Buf(nc, f"ost{i}", sbl(f"osta{i}", [128, T], BF16)) for i in range(2)]
            hal, hal2 = halo_evs[0]
            hal2.wait(SP)
            hal.wait(SP)
            qaTv = qaT.ap().rearrange("(h p) t -> h p t", p=128)
            hKpv = hKp.ap().rearrange("(h p) o t -> h p (o t)", p=128)
            hKnv = hKn.ap().rearrange("(h p) o t -> h p (o t)", p=128)
            hVpv = hVp.ap().rearrange("(b p) o c -> p b (o c)", p=128)
            hVnv = hVn.ap().rearrange("(b p) o c -> p b (o c)", p=128)
            it = 0
            lastd = [None]

            def load_slot(h):
                for b_ in (kwin, vwin, qwin):
                    b_.sel(h % 2)
                    b_.acquire(SP)
                for g in range(3):
                    hd = 4 * g + h
                    kwin.produced(SP.dma_start(out=kwin.ap[:, g, 1024:3072],
                                               in_=kaTc[hd // 2].ap()[(hd % 2) * 128:(hd % 2 + 1) * 128, :]), 16)
                    kwin.produced(SP.dma_start(out=kwin.ap[:, g, 0:1024], in_=hKpv[hd]), 16)
                    kwin.produced(SP.dma_start(out=kwin.ap[:, g, 3072:4096], in_=hKnv[hd]), 16)
                    for c in range(8):
                        vwin.produced(SP.dma_start(
                            out=vwin.ap[:, 8 + 2 * c:10 + 2 * c, g * 128:(g + 1) * 128],
                            in_=vaLc[c].ap()[:, hd * 128:(hd + 1) * 128].rearrange("(b p) c -> p b c", p=128)), 16)
                    vwin.produced(SP.dma_start(out=vwin.ap[:, 0:8, g * 128:(g + 1) * 128],
                                               in_=hVpv[:, :, hd * 128:(hd + 1) * 128]), 16)
                    vwin.produced(SP.dma_start(out=vwin.ap[:, 24:32, g * 128:(g + 1) * 128],
                                               in_=hVnv[:, :, hd * 128:(hd + 1) * 128]), 16)
                    qwin.produced(SP.dma_start(out=qwin.ap[:, g, :], in_=qaTv[hd]), 16)

            def compute_slot(h):
                nonlocal it
                for b_ in (kwin, vwin, qwin):
                    b_.sel(h % 2)
                for b_ in (kwin, vwin, qwin):
                    b_.wait_ready(PE)
                etb.wait_ready(DVE)
                for qt in range(NT):
                    tsl = slice(qt * T, (qt + 1) * T)
                    ob = banks[3 + it % 2]
                    db_ = banks[5 + it % 2]
                    ob.acquire(PE)
                    db_.acquire(PE)
                    work = []
                    for g in range(3):
                        for kb in range(4 * qt - DMAX[g], 4 * qt + 3 + DMAX[g] + 1):
                            work.append((g, kb))
                    nw = len(work)

                    def s_mm(i):
                        g, kb = work[i]
                        p_ = kb + 8
                        sbk = banks[(0, 1, 2, 7)[i % 4]]
                        sbk.acquire(PE)
                        m = PE.matmul(sbk.ap[:, :], lhsT=kwin.ap[:, g, p_ * 128:(p_ + 1) * 128], rhs=qwin.ap[:, g, tsl],
                                      start=True, stop=True)
                        sbk.produced(m)
                        x_ = sx[i % 4]
                        sbk.wait_ready(ACT)
                        x_.acquire(ACT)
                        a = ACT.activation(out=x_.ap[:, :], in_=sbk.ap[:, :], func=AF.Exp, scale=SCALE,
                                           bias=vtab[:, p_:p_ + 1])
                        sbk.consumed(a)
                        x_.produced(a)
                        p = ps[i % 5]
                        x_.wait_ready(DVE)
                        p.acquire(DVE)
                        d0 = kb - 4 * qt
                        c0 = EOFF[g] + (DMAXP[g] - d0) * 128
                        d_ = DVE.tensor_tensor(out=p.ap[:, :], in0=x_.ap[:, :], in1=etb.ap[:, c0:c0 + 512], op=ALU.mult)
                        x_.consumed(d_)
                        p.produced(d_)
                        lastd[0] = d_

                    def pv_mm(i):
                        g, kb = work[i]
                        p_ = kb + 8
                        p = ps[i % 5]
                        p.wait_ready(PE)
                        PE.matmul(ob.ap[:, :], lhsT=vwin.ap[:, p_, g * 128:(g + 1) * 128], rhs=p.ap[:, :],
                                  start=(i == 0), stop=(i == nw - 1))
                        m = PE.matmul(db_.ap[:, :], lhsT=ones[:, :], rhs=p.ap[:, :],
                                      start=(i == 0), stop=(i == nw - 1))
                        p.consumed(m)
                        return m
                    s_mm(0)
                    s_mm(1)
                    s_mm(2)
                    for i in range(nw):
                        if i + 3 < nw:
                            s_mm(i + 3)
                        m = pv_mm(i)
                        if i == 2 and pend_tail[0] is not None:
                            pend_tail[0]()
                            pend_tail[0] = None
                    ob.produced(m)
                    db_.produced(m)
                    if qt == NT - 1:
                        for b_ in (kwin, vwin, qwin):
                            b_.consumed(m)
                        etb.consumed(lastd[0])
                    pend_tail[0] = (lambda ob=ob, db_=db_, dst=oaT.ap()[h * 128:(h + 1) * 128, tsl], o_=osta[it % 2]:
                                    attn_tail(ob, db_, dst, rden, o_))
                    it += 1
            def load_etb(h):
                etb.acquire(SP)
                etb.produced(SP.dma_start(out=etb.ap[:, :], in_=etab_d.ap()[h]), 16)
            load_slot(0)
            load_etb(0)
            load_slot(1)
            for h in range(4):
                compute_slot(h)
                if h + 1 < 4:
                    load_etb(h + 1)
                if h + 2 < 4:
                    load_slot(h + 2)
            pend_tail[0]()
            pend_tail[0] = None
            banks[7].acquire(PE)
            misc.inc(PE.matmul(banks[7].ap[:, 0:1], lhsT=ones[:, :], rhs=ones[:, 0:1], start=True, stop=True))
            for e in (SP, ACT, DVE):
                misc.wait(e)
                st.wait(e)
            st.wait(PE)

        if STOP == "ma":
            raise _Stop()
        run_token_phases(False)
        st.wait(SP)
        for e in (PE, ACT, DVE, GP):
            st.wait(e)
    return nc


def _tab16(v):
    return np.ascontiguousarray(np.asarray(v, np.float32).reshape(DC, 128).T)


def _const_tables():
    half = 64
    inv_freq = (10000.0 ** (-np.arange(0, half, 2, dtype=np.float32) / half)).astype(np.float32)
    tpos = np.arange(S)
    row = (tpos // 64).astype(np.float32)
    col = (tpos % 64).astype(np.float32)
    ang = np.concatenate([row[:, None] * inv_freq, col[:, None] * inv_freq], axis=-1).astype(np.float32)
    cos = np.cos(ang).astype(np.float32)
    sin = np.sin(ang).astype(np.float32)
    C = np.concatenate([cos, cos], axis=1).T
    Sn = np.concatenate([-sin, sin], axis=1).T
    slopes = (2.0 ** (-8.0 * np.arange(1, 13, dtype=np.float32) / 12.0)).astype(np.float32).reshape(3, 4)
    dil = (1, 4, 16)
    et = np.zeros((4, 128, ETOT), np.float32)
    for h in range(4):
        for g in range(3):
            k = np.arange(128)[:, None]
            c = np.arange(EW[g])[None, :]
            diff = 128 * DMAXP[g] + k - c
            ok = (diff % dil[g] == 0) & (np.abs(diff) <= 64 * dil[g])
            val = np.exp(-slopes[g, h] * np.abs(diff).astype(np.float32)).astype(np.float32)
            et[h, :, EOFF[g]:EOFF[g] + EW[g]] = np.where(ok, val, 0.0)
    return np.ascontiguousarray(C), np.ascontiguousarray(Sn), et.astype(ml_dtypes.bfloat16)


def _win_layout(w_in):
    qa = np.arange(0, 1536)
    ka = np.arange(1536, 3072)
    va = np.arange(3072, 4608)
    qb0 = 4608
    kb0 = 4608 + 1024
    vb0 = kb0 + 256
    ga0 = vb0 + 256
    gb0 = ga0 + 2048

    def swap(base):
        return np.concatenate([np.arange(base + 64, base + 128), np.arange(base, base + 64)])
    cols = [qa, ka]
    for h in range(8):
        cols.append(np.arange(qb0 + h * 128, qb0 + (h + 1) * 128))
        cols.append(swap(qb0 + h * 128))
    for h in range(2):
        cols.append(np.arange(kb0 + h * 128, kb0 + (h + 1) * 128))
        cols.append(swap(kb0 + h * 128))
    cols.append(np.arange(ga0, ga0 + 2048))
    cols.append(np.arange(gb0, gb0 + 2048))
    cols.append(va)
    cols.append(np.arange(vb0, vb0 + 256))
    idx = np.concatenate(cols)
    assert idx.shape[0] == WIN_COLS
    return np.ascontiguousarray(w_in[:, idx])


_NC_CACHE = {}


def kernel(x, c, w_ada, b_ada, norm_ffn1, w1_ffn1, w3_ffn1, w2_ffn1, norm_mix, w_in,
           q_norm_a, k_norm_a, q_norm_b, k_norm_b, w_branch_a, w_branch_b, w_out,
           norm_ffn2, w1_ffn2, w3_ffn2, w2_ffn2, norm_final):
    f = lambda a: np.ascontiguousarray(np.asarray(a, dtype=np.float32))
    x = f(x); c = f(c)
    w_ada0 = f(w_ada)[0]; b_ada0 = f(b_ada)[0]
    C, Sn, et = _const_tables()
    gt = np.concatenate([_tab16(f(norm_ffn1)[0]), _tab16(f(norm_mix)[0]), _tab16(f(norm_ffn2)[0]),
                         _tab16(f(norm_final))], axis=1)
    gqa, gka, gqb, gkb = f(q_norm_a)[0], f(k_norm_a)[0], f(q_norm_b)[0], f(k_norm_b)[0]
    sw = lambda v: np.concatenate([v[64:], v[:64]])
    qk = np.stack([gqa, gka, gqb, gkb, sw(gqb), sw(gkb), gqa, gqa], axis=1)
    qk = np.ascontiguousarray(qk.astype(np.float32))
    winp = _win_layout(f(w_in)[0])
    shared = {
        "w1a": f(w1_ffn1)[0], "w3a": f(w3_ffn1)[0], "w2a": f(w2_ffn1)[0],
        "w1b": f(w1_ffn2)[0], "w3b": f(w3_ffn2)[0], "w2b": f(w2_ffn2)[0],
        "win": winp, "wpa": f(w_branch_a)[0], "wpb": f(w_branch_b)[0], "wout": f(w_out)[0],
        "gtab": np.ascontiguousarray(gt), "qkg": qk, "etab": et,
    }
    in_maps = []
    for r in range(NCORES):
        b, g = r // 4, r % 4
        t0 = g * TOK
        vt = np.zeros((128, 32), np.float32)
        for p in range(32):
            gb = g * 16 - 8 + p
            if gb < 0 or gb >= 64:
                vt[:, p] = NEG
        m = dict(shared)
        m.update({
            "xT": np.ascontiguousarray(x[b, t0:t0 + TOK, :].T),
            "cT": _tab16(c[b]),
            "wada": np.ascontiguousarray(w_ada0[:, g * 4608:(g + 1) * 4608]),
            "bada": np.ascontiguousarray(b_ada0[g * 4608:(g + 1) * 4608].reshape(36, 128).T),
            "ropeC": np.ascontiguousarray(C[:, t0:t0 + TOK]),
            "ropeS": np.ascontiguousarray(Sn[:, t0:t0 + TOK]),
            "vtab": vt,
        })
        in_maps.append(m)
    if "nc" not in _NC_CACHE:
        _NC_CACHE["nc"] = build_program()
    nc = _NC_CACHE["nc"]
    res = run_bass_kernel_spmd(nc, in_maps, core_ids=list(range(NCORES)))
    out = np.empty((2, S, D), np.float32)
    for r in range(NCORES):
        b, g = r // 4, r % 4
        out[b, g * TOK:(g + 1) * TOK, :] = res.results[r]["yT"].T
    return out


if __name__ == "__main__":
    import time
    t0 = time.time()
    nc = build_program()
    print("build ok", time.time() - t0)
```

```python
import math
import os
from contextlib import ExitStack

import numpy as np
import ml_dtypes

import concourse.bass as bass
import concourse.mybir as mybir
from concourse.bass_utils import run_bass_kernel_spmd

F32 = mybir.dt.float32
BF16 = mybir.dt.bfloat16
AF = mybir.ActivationFunctionType
ALU = mybir.AluOpType

NCORES = 8
D = 2048
S = 8192
TOK = 2048
T = 512
NT = TOK // T
DC = D // 128
DFF = 5632
NJ = DFF // 128
HD = 128
EPS = 1e-6
SCALE = HD ** -0.5
NEG = -30000.0

FM_COLS = 1536 + 1536 + 2048 + 512 + 2048 + 2048
TM_COLS = 1536 + 256
WIN_COLS = FM_COLS + TM_COLS

DMAXP = (4, 5, 11)
DMAX = (1, 2, 8)
EW = tuple((2 * d + 1) * 128 for d in DMAXP)
EOFF = (0, EW[0], EW[0] + EW[1])
ETOT = sum(EW)


class _Stop(Exception):
    pass


STOP = os.environ.get("KSTOP", "")


class Sem:
    def __init__(self, nc, name):
        self.s = nc.alloc_semaphore(name)
        self.n = 0
        self.waited = {}

    def inc(self, ins, by=1):
        ins.then_inc(self.s, by)
        self.n += by
        return self.n

    def wait(self, eng, val=None):
        v = self.n if val is None else val
        if v <= 0:
            return
        k = id(eng)
        if self.waited.get(k, 0) >= v:
            return
        eng.wait_ge(self.s, v)
        self.waited[k] = v


class G:
    nc = None
    PROG = {}
    SIG = {}
    DSEM = {}

    @staticmethod
    def reset(nc):
        G.nc = nc
        G.PROG = {}
        G.SIG = {}
        G.DSEM = {}

    @staticmethod
    def dsem(name):
        if name not in G.DSEM:
            G.DSEM[name] = Sem(G.nc, name + "_d")
        return G.DSEM[name]

    @staticmethod
    def signal(ins, dname):
        k = id(ins)
        if k in G.SIG:
            return G.SIG[k][1]
        txt = str(ins.ins)[:24]
        eng = txt.split()[0]
        if "DMA" in txt:
            sem = G.dsem(dname)
            ev = (sem, sem.inc(ins, 16))
        else:
            if eng not in G.PROG:
                G.PROG[eng] = Sem(G.nc, "prog_" + eng)
            sem = G.PROG[eng]
            ev = (sem, sem.inc(ins, 1))
        G.SIG[k] = (ins, ev)
        return ev


class Evs:
    def __init__(self, name):
        self.name = name
        self.m = {}

    def add(self, ins, by=None):
        sem, v = G.signal(ins, self.name)
        self.m[sem] = max(self.m.get(sem, 0), v)

    inc = add

    def wait(self, eng):
        for sem, v in self.m.items():
            sem.wait(eng, v)


class Cur:
    def __init__(self, bufs):
        self.__dict__["bufs"] = bufs
        self.__dict__["i"] = 0

    def sel(self, i):
        self.__dict__["i"] = i

    def __getattr__(self, name):
        return getattr(self.bufs[self.i], name)


class Buf:
    def __init__(self, nc, name, ap):
        self.ap = ap
        self.rd = Evs(name)
        self.fr = Evs(name)

    def acquire(self, eng):
        self.fr.wait(eng)
        self.rd.wait(eng)

    def produced(self, ins, by=None):
        self.rd.add(ins)

    def wait_ready(self, eng):
        self.rd.wait(eng)

    def consumed(self, ins, by=None):
        self.fr.add(ins)


def build_program():
    holder = {}
    try:
        _build_program(holder)
    except _Stop:
        pass
    return holder["nc"]


def _build_program(holder):
    nc = bass.Bass("TRN2", target_bir_lowering=False)
    G.reset(nc)
    holder["nc"] = nc
    PE, ACT, DVE, SP, GP = nc.tensor, nc.scalar, nc.vector, nc.sync, nc.gpsimd

    def din(name, shape, dt=F32):
        return nc.dram_tensor(name, list(shape), dt, kind="ExternalInput")

    def dscr(name, shape, dt):
        return nc.dram_tensor(name, list(shape), dt)

    xT = din("xT", [D, TOK])
    cT = din("cT", [128, DC])
    wada = din("wada", [D, 4608])
    bada = din("bada", [128, 36])
    gtab_d = din("gtab", [128, 4 * DC])
    qkg_d = din("qkg", [128, 8])
    w1a = din("w1a", [D, DFF]); w3a = din("w3a", [D, DFF]); w2a = din("w2a", [DFF, D])
    w1b = din("w1b", [D, DFF]); w3b = din("w3b", [D, DFF]); w2b = din("w2b", [DFF, D])
    win = din("win", [D, WIN_COLS])
    wpa = din("wpa", [512, D]); wpb = din("wpb", [1024, D]); wout = din("wout", [D, D])
    ropeC_d = din("ropeC", [128, TOK]); ropeS_d = din("ropeS", [128, TOK])
    etab_d = din("etab", [4, 128, ETOT], BF16)
    vtab_d = din("vtab", [128, 32])
    yT = nc.dram_tensor("yT", [D, TOK], F32, kind="ExternalOutput")

    modin = dscr("modin", [128, 36], F32)
    modout = dscr("modout", [512, 36], F32)
    h1T = dscr("h1T", [D, TOK], F32)
    qaT = dscr("qaT", [1536, TOK], BF16)
    kaTc = [dscr(f"kaT{c}", [256, TOK], BF16) for c in range(6)]
    kaGc = [dscr(f"kaG{c}", [1024, TOK], BF16) for c in range(6)]
    vaLc = [dscr(f"vaL{c}", [256, 1536], BF16) for c in range(8)]
    vaGc = [dscr(f"vaG{c}", [1024, 1536], BF16) for c in range(8)]
    qbT = dscr("qbT", [1024, TOK], BF16)
    kbTt = [dscr(f"kbT{t}", [256, T], BF16) for t in range(NT)]
    kbGt = [dscr(f"kbG{t}", [4 * 256, T], BF16) for t in range(NT)]
    vbLt = [dscr(f"vbL{t}", [T, 256], BF16) for t in range(NT)]
    vbGt = [dscr(f"vbG{t}", [4 * T, 256], BF16) for t in range(NT)]
    sgT = dscr("sgT", [2 * D, TOK], BF16)
    hKp = dscr("hKp", [1536, 1, 1024], BF16)
    hKn = dscr("hKn", [1536, 1, 1024], BF16)
    hVp = dscr("hVp", [1024, 1, 1536], BF16)
    hVn = dscr("hVn", [1024, 1, 1536], BF16)
    oaT = dscr("oaT", [512, TOK], BF16)
    obT = dscr("obT", [1024, TOK], BF16)

    def wview(w):
        return w.ap().rearrange("(kc p) n -> p kc n", p=128)

    with ExitStack() as es:
        def sb(name, shape, dt):
            return es.enter_context(nc.sbuf_tensor(name, list(shape), dt))

        ones = sb("ones", [128, 128], BF16)
        ones_f = sb("ones_f", [128, 128], F32)
        gtab = sb("gtabs", [128, 4 * DC], F32)
        qkg = sb("qkgs", [128, 8], F32)
        vtab = sb("vtabs", [128, 32], F32)
        modT = sb("modT", [128, 144], F32)
        tabA = sb("tabA", [128, 3 * DC], F32)
        tabG = sb("tabG", [128, 3 * DC], F32)
        NSLOT = 3
        slots = [Buf(nc, f"slot{i}", sb(f"slot{i}", [128, 8192], BF16)) for i in range(NSLOT)]
        psum_all = nc.alloc_psum_tensor("psum_all", [128, 4096], F32)
        banks = [Buf(nc, f"bank{i}", psum_all[:, i * 512:(i + 1) * 512]) for i in range(8)]

        ld = Evs("ld")
        ldr = Evs("ldr")
        ldb = Evs("ldb")
        st = Evs("st")
        cc = Sem(nc, "cc")
        ccb = Sem(nc, "ccb")
        cca = Sem(nc, "cca")
        groups = [[0, 1, 2, 3], [4, 5, 6, 7]]
        misc = Evs("misc")

        slot_ctr = [0]

        def wstage(parts):
            sl = slots[slot_ctr[0] % NSLOT]
            slot_ctr[0] += 1
            sl.acquire(GP)
            for vf, src in parts:
                sl.produced(GP.dma_start(out=vf(sl.ap), in_=src), 16)
            for d in list(deferred):
                d[0] -= 1
                if d[0] <= 0:
                    deferred.remove(d)
                    d[1]()
            return sl

        deferred = []

        def flush_deferred():
            for d in list(deferred):
                deferred.remove(d)
                d[1]()

        ld.inc(SP.dma_start(out=gtab[:, :], in_=gtab_d[:, :]), 16)
        ld.inc(SP.dma_start(out=qkg[:, :], in_=qkg_d[:, :]), 16)
        ld.inc(SP.dma_start(out=vtab[:, :], in_=vtab_d[:, :]), 16)
        misc.inc(DVE.memset(ones[:, :], 1.0))
        misc.inc(DVE.memset(ones_f[:, :], 1.0))
        for e in (PE, ACT, DVE):
            ld.wait(e)
            misc.wait(e)

        with ExitStack() as e0:
            cts = e0.enter_context(nc.sbuf_tensor("cts", [128, DC], F32))
            cact = e0.enter_context(nc.sbuf_tensor("cact", [128, DC], BF16))
            bad = e0.enter_context(nc.sbuf_tensor("bads", [128, 36], F32))
            modp = e0.enter_context(nc.sbuf_tensor("modp", [128, 36], F32))
            ld0 = Evs("ld0")
            ld0.inc(SP.dma_start(out=cts[:, :], in_=cT[:, :]), 16)
            ld0.inc(SP.dma_start(out=bad[:, :], in_=bada[:, :]), 16)
            ld0.wait(ACT)
            misc.inc(ACT.activation(out=cact[:, :], in_=cts[:, :], func=AF.Silu))
            misc.wait(PE)
            wv = wview(wada)
            bk = banks[7]
            bk.acquire(PE)
            last = None
            for s_ in range(9):
                sl = wstage([(lambda a: a[:, :].rearrange("p (k n) -> p k n", n=512),
                              wv[:, :, s_ * 512:(s_ + 1) * 512])])
                sl.wait_ready(PE)
                sv = sl.ap[:, :].rearrange("p (k n) -> p k n", n=512)
                for i in range(4):
                    col = s_ * 4 + i
                    for kc in range(DC):
                        last = PE.matmul(bk.ap[:, col:col + 1], lhsT=sv[:, kc, i * 128:(i + 1) * 128],
                                         rhs=cact[:, kc:kc + 1], start=(kc == 0), stop=(kc == DC - 1))
                sl.consumed(last)
            bk.produced(last)
            bk.wait_ready(DVE)
            ld0.wait(DVE)
            i_ = DVE.tensor_tensor(out=modp[:, :], in0=bk.ap[:, 0:36], in1=bad[:, :], op=ALU.add)
            bk.consumed(i_)
            misc.inc(i_)
            misc.wait(SP)
            st.inc(SP.dma_start(out=modin[:, :], in_=modp[:, :]), 16)
            st.wait(GP)
            cc.inc(GP.collective_compute("AllGather", ALU.bypass, replica_groups=[[0, 1, 2, 3], [4, 5, 6, 7]],
                                         ins=[modin.ap().opt()], outs=[modout.ap().opt()]))
            cc.wait(ACT)
            ldm = Evs("ldm")
            ldm.inc(ACT.dma_start(out=modT[:, :].rearrange("p (r c) -> p r c", c=36),
                                 in_=modout.ap().rearrange("(r p) c -> p r c", p=128)), 16)
            ldm.wait(DVE)
            for k in range(3):
                sc = modT[:, (3 * k + 1) * DC:(3 * k + 2) * DC]
                g = modT[:, (3 * k + 2) * DC:(3 * k + 3) * DC]
                misc.inc(DVE.scalar_tensor_tensor(out=tabA[:, k * DC:(k + 1) * DC], in0=sc, scalar=1.0,
                                                  in1=gtab[:, k * DC:(k + 1) * DC], op0=ALU.add, op1=ALU.mult))
                misc.inc(DVE.tensor_scalar(out=tabG[:, k * DC:(k + 1) * DC], in0=g,
                                           scalar1=(1.0 if k == 1 else 0.5), scalar2=None, op0=ALU.mult))
            for e in (ACT, DVE, PE):
                misc.wait(e)

        if STOP == "p0":
            raise _Stop()

        def tabB(k):
            return modT[:, (3 * k) * DC:(3 * k + 1) * DC]

        def run_token_phases(first):
            with ExitStack() as e1:
                def sbl(name, shape, dt):
                    return e1.enter_context(nc.sbuf_tensor(name + ("A" if first else "B"), list(shape), dt))

                xt = Buf(nc, "xt", sbl("xt", [128, DC, T], F32))
                class _Cur:
                    def __init__(self, bufs):
                        self.__dict__["bufs"] = bufs
                        self.__dict__["i"] = 0

                    def sel(self, i):
                        self.__dict__["i"] = i

                    def __getattr__(self, name):
                        return getattr(self.bufs[self.i], name)
                _u0 = Buf(nc, "u", sbl("u", [128, DC, T], BF16))
                _u1 = Buf(nc, "u2", sbl("u2", [128, DC, T], BF16)) if first else _u0
                u = _Cur([_u0, _u1])
                hid = Buf(nc, "hid", sbl("hid", [128, NJ, T], BF16))
                rstd = Buf(nc, "rstd", sbl("rstd", [128, T], F32))
                lnt = Buf(nc, "lnt", sbl("lnt", [128, T], F32))
                sqs = [Buf(nc, f"sq{i}", sbl(f"sq{i}", [128, T], BF16)) for i in range(4)]
                tmps = [Buf(nc, f"tmp{i}", sbl(f"tmp{i}", [128, T], F32)) for i in range(3)]
                sils = [Buf(nc, f"sil{i}", sbl(f"sil{i}", [128, T], F32)) for i in range(2)]
                osts = [Buf(nc, f"ost{i}", sbl(f"ost{i}", [128, T], BF16)) for i in range(4)]
                ost_c = [0]

                def rms_rstd(src_bank_or_none, src_chunks, nfeat, statbank, split=False):
                    n = len(src_chunks)
                    statbank.acquire(PE)
                    for i, (ap, wfn, cfn) in enumerate(src_chunks):
                        sq = sqs[i % 4]
                        if split and i % 2 == 1:
                            sq.acquire(DVE)
                            wfn(DVE)
                            a = DVE.tensor_tensor(out=sq.ap[:, :], in0=ap, in1=ap, op=ALU.mult)
                        else:
                            sq.acquire(ACT)
                            wfn(ACT)
                            a = ACT.activation(out=sq.ap[:, :], in_=ap, func=AF.Square)
                        sq.produced(a)
                        cfn(a)
                        sq.wait_ready(PE)
                        m = PE.matmul(statbank.ap[:, :], lhsT=ones[:, :], rhs=sq.ap[:, :],
                                      start=(i == 0), stop=(i == n - 1))
                        sq.consumed(m)
                    statbank.produced(m)
                    statbank.wait_ready(ACT)
                    lnt.acquire(ACT)
                    a = ACT.activation(out=lnt.ap[:, :], in_=statbank.ap[:, :], func=AF.Ln,
                                       scale=1.0 / nfeat, bias=EPS)
                    statbank.consumed(a)
                    lnt.produced(a)
                    lnt.wait_ready(ACT)
                    rstd.acquire(ACT)
                    a = ACT.activation(out=rstd.ap[:, :], in_=lnt.ap[:, :], func=AF.Exp, scale=-0.5)
                    lnt.consumed(a)
                    rstd.produced(a)

                uch = [[None] * DC, [None] * DC]

                def norm_mod(k):
                    chunks = [(xt.ap[:, dc, :], xt.wait_ready, xt.consumed) for dc in range(DC)]
                    rms_rstd(None, chunks, D, banks[6], split=True)
                    rstd.wait_ready(DVE)
                    xt.wait_ready(DVE)
                    u.acquire(ACT)
                    for dc in range(DC):
                        tmp = tmps[dc % 3]
                        tmp.acquire(DVE)
                        i_ = DVE.scalar_tensor_tensor(out=tmp.ap[:, :], in0=xt.ap[:, dc, :],
                                                      scalar=tabA[:, k * DC + dc:k * DC + dc + 1],
                                                      in1=rstd.ap[:, :], op0=ALU.mult, op1=ALU.mult)
                        tmp.produced(i_)
                        xt.consumed(i_)
                        if dc == DC - 1:
                            rstd.consumed(i_)
                        tmp.wait_ready(ACT)
                        a = ACT.activation(out=u.ap[:, dc, :], in_=tmp.ap[:, :], func=AF.Identity,
                                           bias=tabB(k)[:, dc:dc + 1], scale=1.0)
                        tmp.consumed(a)
                        u.produced(a)
                        ev = Evs("uch")
                        ev.add(a)
                        uch[u.i][dc] = ev

                def ffn(k, w1, w3, w2, do_norm=True):
                    if do_norm:
                        norm_mod(k)
                    w1v, w3v = wview(w1), wview(w3)
                    w2v = w2.ap().rearrange("(j p) n -> p j n", p=128)
                    hid.acquire(DVE)
                    for s_ in range(NJ // 2):
                        sl = wstage([
                            (lambda a: a[:, 0:4096].rearrange("p (k n) -> p k n", n=256), w1v[:, :, s_ * 256:(s_ + 1) * 256]),
                            (lambda a: a[:, 4096:8192].rearrange("p (k n) -> p k n", n=256), w3v[:, :, s_ * 256:(s_ + 1) * 256]),
                        ])
                        sl.wait_ready(PE)
                        v1 = sl.ap[:, 0:4096].rearrange("p (k n) -> p k n", n=256)
                        v3 = sl.ap[:, 4096:8192].rearrange("p (k n) -> p k n", n=256)
                        for jj in range(2):
                            j = 2 * s_ + jj
                            b1 = banks[j % 2]
                            b3 = banks[2 + j % 2]
                            b1.acquire(PE)
                            for kc in range(DC):
                                if j == 0:
                                    uch[u.i][kc].wait(PE)
                                m = PE.matmul(b1.ap[:, :], lhsT=v1[:, kc, jj * 128:(jj + 1) * 128], rhs=u.ap[:, kc, :],
                                              start=(kc == 0), stop=(kc == DC - 1))
                            b1.produced(m)
                            b3.acquire(PE)
                            for kc in range(DC):
                                m = PE.matmul(b3.ap[:, :], lhsT=v3[:, kc, jj * 128:(jj + 1) * 128], rhs=u.ap[:, kc, :],
                                              start=(kc == 0), stop=(kc == DC - 1))
                            b3.produced(m)
                            if jj == 1:
                                sl.consumed(m)
                            if j == NJ - 1:
                                u.consumed(m)
                            sil = sils[j % 2]
                            b1.wait_ready(ACT)
                            sil.acquire(ACT)
                            a = ACT.activation(out=sil.ap[:, :], in_=b1.ap[:, :], func=AF.Silu)
                            b1.consumed(a)
                            sil.produced(a)
                            sil.wait_ready(DVE)
                            b3.wait_ready(DVE)
                            d_ = DVE.tensor_tensor(out=hid.ap[:, j, :], in0=sil.ap[:, :], in1=b3.ap[:, :], op=ALU.mult)
                            sil.consumed(d_)
                            b3.consumed(d_)
                            hid.produced(d_)
                    hid.wait_ready(PE)
                    HJ = NJ // 2
                    for dp in range(DC // 2):
                        for half in range(2):
                            sl = wstage([(lambda a: a[:, 0:HJ * 256].rearrange("p (j n) -> p j n", n=256),
                                          w2v[:, half * HJ:(half + 1) * HJ, dp * 256:(dp + 1) * 256])])
                            sl.wait_ready(PE)
                            v2 = sl.ap[:, 0:HJ * 256].rearrange("p (j n) -> p j n", n=256)
                            for i in range(2):
                                db = dp * 2 + i
                                yb = banks[4 + (dp % 2) * 2 + i]
                                if half == 0:
                                    yb.acquire(PE)
                                for jj in range(HJ):
                                    j = half * HJ + jj
                                    m = PE.matmul(yb.ap[:, :], lhsT=v2[:, jj, i * 128:(i + 1) * 128], rhs=hid.ap[:, j, :],
                                                  start=(j == 0), stop=(j == NJ - 1))
                                if i == 1:
                                    sl.consumed(m)
                                if half == 1:
                                    yb.produced(m)
                                    if db == DC - 1:
                                        hid.consumed(m)
                                    yb.wait_ready(DVE)
                                    xt.acquire(DVE)
                                    d_ = DVE.scalar_tensor_tensor(out=xt.ap[:, db, :], in0=yb.ap[:, :],
                                                                  scalar=tabG[:, k * DC + db:k * DC + db + 1],
                                                                  in1=xt.ap[:, db, :], op0=ALU.mult, op1=ALU.add)
                                    yb.consumed(d_)
                                    xt.produced(d_)

                def store_stage(dst_ap, eng_producer_fn, extra=None):
                    o = osts[ost_c[0] % 4]
                    ost_c[0] += 1
                    eng, ins = eng_producer_fn(o)
                    o.produced(ins)
                    o.wait_ready(SP)
                    dm = SP.dma_start(out=dst_ap, in_=o.ap[:, :])
                    o.consumed(dm, 16)
                    st.inc(dm, 16)
                    if extra is not None:
                        extra.add(dm)

                if first:
                    ropeC = sbl("ropeC", [128, TOK], F32)
                    ropeS = sbl("ropeS", [128, TOK], F32)
                    winv = wview(win)
                    for t in range(NT):
                        tsl = slice(t * T, (t + 1) * T)
                        if t == 0:
                            xt.acquire(SP)
                            xt.produced(SP.dma_start(out=xt.ap[:, :, :],
                                                     in_=xT.ap().rearrange("(c p) t -> p c t", p=128)[:, :, tsl]), 16)
                            ldr.inc(SP.dma_start(out=ropeC[:, :], in_=ropeC_d[:, :]), 16)
                            ldr.inc(SP.dma_start(out=ropeS[:, :], in_=ropeS_d[:, :]), 16)
                        u.sel(0)
                        ffn(0, w1a, w3a, w2a, do_norm=(t == 0))
                        xt.wait_ready(SP)
                        dm = SP.dma_start(out=h1T.ap().rearrange("(c p) t -> p c t", p=128)[:, :, tsl], in_=xt.ap[:, :, :])
                        xt.consumed(dm, 16)
                        st.inc(dm, 16)
                        u.sel(1)
                        norm_mod(1)
                        if t + 1 < NT:
                            nsl = slice((t + 1) * T, (t + 2) * T)
                            xt.acquire(SP)
                            xt.produced(SP.dma_start(out=xt.ap[:, :, :],
                                                     in_=xT.ap().rearrange("(c p) t -> p c t", p=128)[:, :, nsl]), 16)
                        first_pb = [True]
                        ldr.wait(DVE)
                        hb = 0

                        def fm_stage(si):
                            return wstage([(lambda a: a[:, :].rearrange("p (k n) -> p k n", n=512),
                                            winv[:, :, si * 512:(si + 1) * 512])])

                        def proj_block(sl, bi, bank, last_of_slot):
                            sv = sl.ap[:, :].rearrange("p (k n) -> p k n", n=512)
                            bank.acquire(PE)
                            for kc in range(DC):
                                if first_pb[0]:
                                    uch[u.i][kc].wait(PE)
                                m = PE.matmul(bank.ap[:, :], lhsT=sv[:, kc, bi * 128:(bi + 1) * 128], rhs=u.ap[:, kc, :],
                                              start=(kc == 0), stop=(kc == DC - 1))
                            first_pb[0] = False
                            bank.produced(m)
                            if last_of_slot:
                                sl.consumed(m)
                            return m

                        ssb_c = [0]

                        def qk_post(qbank, pbank, gcol, gpcol, dst_ap, extra=None):
                            ssb = banks[4 + ssb_c[0] % 2] if pbank is None else banks[6 + ssb_c[0] % 2]
                            ssb_c[0] += 1
                            rms_rstd(None, [(qbank.ap[:, :], qbank.wait_ready, lambda a: None)], HD, ssb)
                            rstd.wait_ready(DVE)
                            qbank.wait_ready(DVE)
                            if pbank is None:
                                def prod(o):
                                    o.acquire(DVE)
                                    i_ = DVE.scalar_tensor_tensor(out=o.ap[:, :], in0=qbank.ap[:, :],
                                                                  scalar=qkg[:, gcol:gcol + 1], in1=rstd.ap[:, :],
                                                                  op0=ALU.mult, op1=ALU.mult)
                                    qbank.consumed(i_)
                                    rstd.consumed(i_)
                                    return DVE, i_
                                store_stage(dst_ap, prod)
                            else:
                                t1, t2 = tmps[0], tmps[1]
                                t1.acquire(DVE)
                                i_ = DVE.scalar_tensor_tensor(out=t1.ap[:, :], in0=qbank.ap[:, :],
                                                              scalar=qkg[:, gcol:gcol + 1], in1=ropeC[:, tsl],
                                                              op0=ALU.mult, op1=ALU.mult)
                                qbank.consumed(i_)
                                t1.produced(i_)
                                t2.acquire(DVE)
                                pbank.wait_ready(DVE)
                                i_ = DVE.scalar_tensor_tensor(out=t2.ap[:, :], in0=pbank.ap[:, :],
                                                              scalar=qkg[:, gpcol:gpcol + 1], in1=ropeS[:, tsl],
                                                              op0=ALU.mult, op1=ALU.mult)
                                pbank.consumed(i_)
                                t2.produced(i_)
                                t3 = tmps[2]
                                t3.acquire(DVE)
                                t1.wait_ready(DVE)
                                t2.wait_ready(DVE)
                                i_ = DVE.tensor_tensor(out=t3.ap[:, :], in0=t1.ap[:, :], in1=t2.ap[:, :], op=ALU.add)
                                t1.consumed(i_)
                                t2.consumed(i_)
                                t3.produced(i_)

                                def prod(o):
                                    o.acquire(DVE)
                                    t3.wait_ready(DVE)
                                    i2 = DVE.tensor_tensor(out=o.ap[:, :], in0=t3.ap[:, :], in1=rstd.ap[:, :], op=ALU.mult)
                                    t3.consumed(i2)
                                    rstd.consumed(i2)
                                    return DVE, i2
                                store_stage(dst_ap, prod, extra)

                        lastm = [None]
                        kvst = Evs("kvst")

                        def do_qk_plain(stages):
                            nonlocal hb
                            for si in stages:
                                sl = fm_stage(si)
                                sl.wait_ready(PE)
                                for bi in range(4):
                                    head = (si % 3) * 4 + bi
                                    bank = banks[hb % 4]
                                    hb += 1
                                    lastm[0] = proj_block(sl, bi, bank, bi == 3)
                                    if si < 3:
                                        dst = qaT.ap()[head * 128:(head + 1) * 128, tsl]
                                    else:
                                        dst = kaTc[head // 2].ap()[(head % 2) * 128:(head % 2 + 1) * 128, tsl]
                                    qk_post(bank, None, 0 if si < 3 else 1, None, dst)

                        def do_rope(stages):
                            nonlocal hb
                            for si in stages:
                                sl = fm_stage(si)
                                sl.wait_ready(PE)
                                for hh in range(2):
                                    qbank = banks[(hb) % 4]
                                    pbank = banks[(hb + 1) % 4]
                                    hb += 2
                                    proj_block(sl, 2 * hh, qbank, False)
                                    lastm[0] = proj_block(sl, 2 * hh + 1, pbank, hh == 1)
                                    if si < 10:
                                        head = (si - 6) * 2 + hh
                                        dst = qbT.ap()[head * 128:(head + 1) * 128, tsl]
                                        qk_post(qbank, pbank, 2, 4, dst)
                                    else:
                                        dst = kbTt[t].ap()[hh * 128:(hh + 1) * 128, :]
                                        qk_post(qbank, pbank, 3, 5, dst, kvst)

                        def do_gates():
                            nonlocal hb
                            for si in range(11, 19):
                                sl = fm_stage(si)
                                sl.wait_ready(PE)
                                for bi in range(4):
                                    bank = banks[hb % 4]
                                    hb += 1
                                    lastm[0] = proj_block(sl, bi, bank, bi == 3)
                                    row = (si - 11) * 4 + bi

                                    def prod(o, bank=bank):
                                        o.acquire(ACT)
                                        bank.wait_ready(ACT)
                                        a = ACT.activation(out=o.ap[:, :], in_=bank.ap[:, :], func=AF.Sigmoid)
                                        bank.consumed(a)
                                        return ACT, a
                                    store_stage(sgT.ap()[row * 128:(row + 1) * 128, tsl], prod)

                        def do_v(vis):
                            nonlocal hb
                            for vi in vis:
                                ncol = 512 if vi < 3 else 256
                                c0 = FM_COLS + vi * 512
                                sl = wstage([(lambda a, ncol=ncol: a[:, 0:DC * ncol].rearrange("p (k n) -> p k n", n=ncol),
                                              winv[:, :, c0:c0 + ncol])])
                                sl.wait_ready(PE)
                                sv = sl.ap[:, 0:DC * ncol].rearrange("p (k n) -> p k n", n=ncol)
                                for tb in range(4):
                                    bank = banks[hb % 4]
                                    hb += 1
                                    bank.acquire(PE)
                                    for kc in range(DC):
                                        m = PE.matmul(bank.ap[:, 0:ncol], lhsT=u.ap[:, kc, tb * 128:(tb + 1) * 128], rhs=sv[:, kc, :],
                                                      start=(kc == 0), stop=(kc == DC - 1))
                                    bank.produced(m)
                                    lastm[0] = m
                                    if tb == 3:
                                        sl.consumed(m)
                                    r0 = t * T + tb * 128
                                    if vi < 3:
                                        dst = vaLc[r0 // 256].ap()[r0 % 256:r0 % 256 + 128, vi * 512:(vi + 1) * 512]
                                    else:
                                        dst = vbLt[t].ap()[tb * 128:(tb + 1) * 128, :]
                                    o = osts[ost_c[0] % 4]
                                    ost_c[0] += 1
                                    o.acquire(DVE)
                                    bank.wait_ready(DVE)
                                    i_ = DVE.tensor_copy(out=o.ap[:, 0:ncol], in_=bank.ap[:, 0:ncol])
                                    bank.consumed(i_)
                                    o.produced(i_)
                                    o.wait_ready(SP)
                                    dm = SP.dma_start(out=dst, in_=o.ap[:, 0:ncol])
                                    o.consumed(dm, 16)
                                    st.inc(dm, 16)
                                    if vi == 3:
                                        kvst.add(dm)

                        do_rope([10])
                        do_v([3])

                        def trig(t=t, kvst=kvst):
                            kvst.wait(GP)
                            ccb.inc(GP.collective_compute("AllGather", ALU.bypass, replica_groups=groups,
                                                          ins=[kbTt[t].ap().opt()], outs=[kbGt[t].ap().opt()]))
                            ccb.inc(GP.collective_compute("AllGather", ALU.bypass, replica_groups=groups,
                                                          ins=[vbLt[t].ap().opt()], outs=[vbGt[t].ap().opt()]))
                        deferred.append([3, trig])
                        do_qk_plain([3, 4, 5])
                        do_v([0, 1, 2])
                        do_qk_plain([0, 1, 2])
                        if t + 1 < NT:
                            u.sel(0)
                            norm_mod(0)
                            u.sel(1)
                        do_rope([6, 7, 8, 9])
                        do_gates()
                        u.consumed(lastm[0])
                else:
                    oat = Buf(nc, "oat", sbl("oat", [128, 4, T], BF16))
                    obt = Buf(nc, "obt", sbl("obt", [128, 8, T], BF16))
                    sga = [Buf(nc, f"sga{i}", sbl(f"sga{i}", [128, 4, T], BF16)) for i in range(2)]
                    sgb = [Buf(nc, f"sgb{i}", sbl(f"sgb{i}", [128, 4, T], BF16)) for i in range(2)]
                    wpav = wpa.ap().rearrange("(s p) n -> p s n", p=128)
                    wpbv = wpb.ap().rearrange("(s p) n -> p s n", p=128)
                    woutv = wview(wout)
                    for t in range(NT):
                        tsl = slice(t * T, (t + 1) * T)
                        def load_ops(tt):
                            sl_ = slice(tt * T, (tt + 1) * T)
                            oat.acquire(SP)
                            oat.produced(SP.dma_start(out=oat.ap[:, :, :],
                                                      in_=oaT.ap().rearrange("(s p) t -> p s t", p=128)[:, :, sl_]), 16)
                            obt.acquire(SP)
                            obt.produced(SP.dma_start(out=obt.ap[:, :, :],
                                                      in_=obT.ap().rearrange("(s p) t -> p s t", p=128)[:, :, sl_]), 16)
                            for s0 in range(2):
                                sg_load(s0, sl_)

                        def sg_load(s0, sl_):
                            ga0 = sga[s0 % 2]
                            gb0 = sgb[s0 % 2]
                            ga0.acquire(SP)
                            ga0.produced(SP.dma_start(out=ga0.ap[:, :, :], in_=sgv[:, s0 * 4:(s0 + 1) * 4, sl_]), 16)
                            gb0.acquire(SP)
                            gb0.produced(SP.dma_start(out=gb0.ap[:, :, :], in_=sgv[:, DC + s0 * 4:DC + (s0 + 1) * 4, sl_]), 16)
                        sgv = sgT.ap().rearrange("(c p) t -> p c t", p=128)
                        if t == 0:
                            load_ops(0)
                        oat.wait_ready(PE)
                        obt.wait_ready(PE)
                        u.acquire(DVE)
                        for s_ in range(4):
                            ga_ = sga[s_ % 2]
                            gb_ = sgb[s_ % 2]
                            if s_ >= 2:
                                sg_load(s_, tsl)
                            sl = wstage([
                                (lambda a: a[:, 0:2048].rearrange("p (s n) -> p s n", n=512), wpav[:, :, s_ * 512:(s_ + 1) * 512]),
                                (lambda a: a[:, 2048:6144].rearrange("p (s n) -> p s n", n=512), wpbv[:, :, s_ * 512:(s_ + 1) * 512]),
                            ])
                            sl.wait_ready(PE)
                            va_ = sl.ap[:, 0:2048].rearrange("p (s n) -> p s n", n=512)
                            vb_ = sl.ap[:, 2048:6144].rearrange("p (s n) -> p s n", n=512)
                            for i in range(4):
                                db = s_ * 4 + i
                                ba = banks[db % 2]
                                bb = banks[2 + db % 2]
                                ba.acquire(PE)
                                for s2 in range(4):
                                    m = PE.matmul(ba.ap[:, :], lhsT=va_[:, s2, i * 128:(i + 1) * 128], rhs=oat.ap[:, s2, :],
                                                  start=(s2 == 0), stop=(s2 == 3))
                                ba.produced(m)
                                bb.acquire(PE)
                                for s2 in range(8):
                                    m = PE.matmul(bb.ap[:, :], lhsT=vb_[:, s2, i * 128:(i + 1) * 128], rhs=obt.ap[:, s2, :],
                                                  start=(s2 == 0), stop=(s2 == 7))
                                bb.produced(m)
                                if i == 3:
                                    sl.consumed(m)
                                    if s_ == 3:
                                        oat.consumed(m)
                                        obt.consumed(m)
                                t1, t2 = tmps[0], tmps[1]
                                ga_.wait_ready(DVE)
                                gb_.wait_ready(DVE)
                                t1.acquire(DVE)
                                ba.wait_ready(DVE)
                                i_ = DVE.tensor_tensor(out=t1.ap[:, :], in0=ba.ap[:, :], in1=ga_.ap[:, i, :], op=ALU.mult)
                                ba.consumed(i_)
                                t1.produced(i_)
                                t2.acquire(DVE)
                                bb.wait_ready(DVE)
                                i_ = DVE.tensor_tensor(out=t2.ap[:, :], in0=bb.ap[:, :], in1=gb_.ap[:, i, :], op=ALU.mult)
                                bb.consumed(i_)
                                t2.produced(i_)
                                if i == 3:
                                    ga_.consumed(i_)
                                    gb_.consumed(i_)
                                t1.wait_ready(DVE)
                                t2.wait_ready(DVE)
                                i_ = DVE.tensor_tensor(out=u.ap[:, db, :], in0=t1.ap[:, :], in1=t2.ap[:, :], op=ALU.add)
                                t1.consumed(i_)
                                t2.consumed(i_)
                                u.produced(i_)
                        xt.acquire(SP)
                        xt.produced(SP.dma_start(out=xt.ap[:, :, :],
                                                 in_=h1T.ap().rearrange("(c p) t -> p c t", p=128)[:, :, tsl]), 16)
                        if t + 1 < NT:
                            load_ops(t + 1)
                        u.wait_ready(PE)
                        for s_ in range(4):
                            sl = wstage([(lambda a: a[:, :].rearrange("p (k n) -> p k n", n=512),
                                          woutv[:, :, s_ * 512:(s_ + 1) * 512])])
                            sl.wait_ready(PE)
                            sv = sl.ap[:, :].rearrange("p (k n) -> p k n", n=512)
                            for i in range(4):
                                db = s_ * 4 + i
                                yb = banks[4 + db % 2]
                                yb.acquire(PE)
                                for kc in range(DC):
                                    m = PE.matmul(yb.ap[:, :], lhsT=sv[:, kc, i * 128:(i + 1) * 128], rhs=u.ap[:, kc, :],
                                                  start=(kc == 0), stop=(kc == DC - 1))
                                yb.produced(m)
                                if i == 3:
                                    sl.consumed(m)
                                    if s_ == 3:
                                        u.consumed(m)
                                yb.wait_ready(DVE)
                                xt.acquire(DVE)
                                d_ = DVE.scalar_tensor_tensor(out=xt.ap[:, db, :], in0=yb.ap[:, :],
                                                              scalar=tabG[:, DC + db:DC + db + 1],
                                                              in1=xt.ap[:, db, :], op0=ALU.mult, op1=ALU.add)
                                yb.consumed(d_)
                                xt.produced(d_)
                        ffn(2, w1b, w3b, w2b)
                        chunks = [(xt.ap[:, dc, :], xt.wait_ready, xt.consumed) for dc in range(DC)]
                        rms_rstd(None, chunks, D, banks[6], split=True)
                        rstd.wait_ready(DVE)
                        xt.acquire(DVE)
                        for dc in range(DC):
                            d_ = DVE.scalar_tensor_tensor(out=xt.ap[:, dc, :], in0=xt.ap[:, dc, :],
                                                          scalar=gtab[:, 3 * DC + dc:3 * DC + dc + 1],
                                                          in1=rstd.ap[:, :], op0=ALU.mult, op1=ALU.mult)
                            xt.produced(d_)
                        rstd.consumed(d_)
                        xt.wait_ready(SP)
                        dm = SP.dma_start(out=yT.ap().rearrange("(c p) t -> p c t", p=128)[:, :, tsl], in_=xt.ap[:, :, :])
                        xt.consumed(dm, 16)
                        st.inc(dm, 16)

        run_token_phases(True)
        if STOP == "ab":
            raise _Stop()

        st.wait(GP)
        flush_deferred()
        for e in (PE, ACT, DVE, SP):
            st.wait(e)

        if STOP == "kv":
            raise _Stop()
        def attn_tail(ob, db_, dst_ap, rden, osta):
            db_.wait_ready(ACT)
            rden.acquire(ACT)
            a_ = ACT.activation(out=rden.ap[:, :], in_=db_.ap[:, :], func=AF.Ln)
            db_.consumed(a_)
            lnev = Evs("lnev")
            lnev.add(a_)
            lnev.wait(ACT)
            a_ = ACT.activation(out=rden.ap[:, :], in_=rden.ap[:, :], func=AF.Exp, scale=-1.0)
            rden.produced(a_)
            rden.wait_ready(DVE)
            ob.wait_ready(DVE)
            osta.acquire(DVE)
            i_ = DVE.tensor_tensor(out=osta.ap[:, :], in0=ob.ap[:, :], in1=rden.ap[:, :], op=ALU.mult)
            ob.consumed(i_)
            rden.consumed(i_)
            osta.produced(i_)
            osta.wait_ready(SP)
            dm = SP.dma_start(out=dst_ap, in_=osta.ap[:, :])
            osta.consumed(dm, 16)
            st.inc(dm, 16)

        with ExitStack() as e2:
            def sbl(name, shape, dt):
                return e2.enter_context(nc.sbuf_tensor(name, list(shape), dt))
            kbt = sbl("kbt", [128, 2, S], BF16)
            vbt = sbl("vbt", [128, 64, 256], BF16)
            qbt = sbl("qbt", [128, 8, TOK], BF16)
            spairs = [Buf(nc, f"bank{2 * i}", psum_all[:, i * 1024:(i + 1) * 1024]) for i in range(2)]
            pps = [Buf(nc, ("sq%d" % i if i < 2 else "sil0"), sbl(f"pp{i}", [128, 2 * T], BF16)) for i in range(3)]
            rden = Buf(nc, "rstd", sbl("rdenb", [128, T], F32))
            ostb = [Buf(nc, f"ost{i}", sbl(f"ostb{i}", [128, T], BF16)) for i in range(2)]
            paccs = [Buf(nc, f"tmp{i}", sbl(f"pacc{i}", [128, T], F32)) for i in range(2)]
            accw = [[sbl(f"accw{i}{j}", [128, 2 * T], F32) for j in range(2)] for i in range(2)]
            ccb.wait(SP)
            ldv = Evs("ldv")
            ldb1 = Evs("ldb1")

            def k_load(kvh, evs):
                for tt in range(NT):
                    evs.inc(SP.dma_start(
                        out=kbt[:, kvh, :].rearrange("p (r t c) -> p r t c", r=4, t=NT)[:, :, tt, :],
                        in_=kbGt[tt].ap().rearrange("(r h p) c -> h p r c", h=2, p=128)[kvh]), 16)
            qv = qbT.ap().rearrange("(h p) t -> p h t", p=128)
            k_load(0, ldb)
            ldb.inc(SP.dma_start(out=qbt[:, 0:1, :], in_=qv[:, 0:1, :]), 16)
            for tt in range(NT):
                for r in range(4):
                    ldv.inc(SP.dma_start(
                        out=vbt[:, r * 16 + tt * 4:r * 16 + tt * 4 + 4, :],
                        in_=vbGt[tt].ap()[r * T:(r + 1) * T, :].rearrange("(b p) c -> p b c", p=128)), 16)
            ldq = Evs("ldq")
            ldq.inc(SP.dma_start(out=qbt[:, 1:4, :], in_=qv[:, 1:4, :]), 16)
            k_load(1, ldb1)
            ldb1.inc(SP.dma_start(out=qbt[:, 4:8, :], in_=qv[:, 4:8, :]), 16)
            ldv.wait(GP)
            ldq.wait(GP)
            ldb.wait(GP)
            ldb1.wait(GP)
            for c in range(6):
                cca.inc(GP.collective_compute("AllGather", ALU.bypass, replica_groups=groups,
                                              ins=[kaTc[c].ap().opt()], outs=[kaGc[c].ap().opt()]))
            for c in range(8):
                cca.inc(GP.collective_compute("AllGather", ALU.bypass, replica_groups=groups,
                                              ins=[vaLc[c].ap().opt()], outs=[vaGc[c].ap().opt()]))

            ldb.wait(PE)
            def emit_halo():
                cca.wait(SP)
                cca.wait(GP)
                pid = GP.partition_id()
                prv = (pid + 3) % 4
                nxt = (pid + 1) % 4
                pid2 = SP.partition_id()
                prv2 = (pid2 + 3) % 4
                nxt2 = (pid2 + 1) % 4
                hal = Evs("hal")
                for c in range(6):
                    kx = kaGc[c].ap().rearrange("(r x) t -> x r t", r=4)
                    hal.inc(GP.dma_start(out=hKp.ap()[c * 256:(c + 1) * 256], in_=kx[:, bass.ds(prv, 1), 1024:2048]), 16)
                    hal.inc(GP.dma_start(out=hKn.ap()[c * 256:(c + 1) * 256], in_=kx[:, bass.ds(nxt, 1), 0:1024]), 16)
                hal2 = Evs("hal2")
                for c in range(4):
                    vp = vaGc[4 + c].ap().rearrange("(r t) c -> t r c", r=4)
                    vn = vaGc[c].ap().rearrange("(r t) c -> t r c", r=4)
                    hal2.inc(SP.dma_start(out=hVp.ap()[c * 256:(c + 1) * 256], in_=vp[:, bass.ds(prv2, 1), :]), 16)
                    hal2.inc(SP.dma_start(out=hVn.ap()[c * 256:(c + 1) * 256], in_=vn[:, bass.ds(nxt2, 1), :]), 16)

                return hal, hal2
            halo_evs = [None]
            pend_tail = [None]
            it = 0
            for hq in range(8):
                kv = hq // 4
                if hq == 1:
                    ldq.wait(PE)
                if hq == 4:
                    ldb1.wait(PE)
                if hq == 6:
                    halo_evs[0] = emit_halo()
                for qt in range(NT):
                    tsl = slice(qt * T, (qt + 1) * T)
                    ob = banks[4 + it % 2]
                    db_ = banks[6 + it % 2]
                    ob.acquire(PE)
                    db_.acquire(PE)
                    NKB = S // 128

                    NP = NKB // 2
                    pacc = paccs[it % 2]
                    accs = accw[it % 2]
                    chain = [Evs("chain0"), Evs("chain1")]

                    def s_mm(j):
                        sp = spairs[j % 2]
                        sp.acquire(PE)
                        for hh in range(2):
                            kb = 2 * j + hh
                            m = PE.matmul(sp.ap[:, hh * T:(hh + 1) * T], lhsT=kbt[:, kv, kb * 128:(kb + 1) * 128],
                                          rhs=qbt[:, hq, tsl], start=True, stop=True)
                        sp.produced(m)
                        p = pps[j % 3]
                        sp.wait_ready(ACT)
                        p.acquire(ACT)
                        a = ACT.activation(out=p.ap[:, :], in_=sp.ap[:, :], func=AF.Exp, scale=SCALE)
                        sp.consumed(a)
                        p.produced(a)

                    def pv_mm(j):
                        p = pps[j % 3]
                        ldv.wait(PE)
                        p.wait_ready(PE)
                        for hh in range(2):
                            kb = 2 * j + hh
                            m = PE.matmul(ob.ap[:, :], lhsT=vbt[:, kb, kv * 128:(kv + 1) * 128], rhs=p.ap[:, hh * T:(hh + 1) * T],
                                          start=(kb == 0), stop=(kb == NKB - 1))
                        p.consumed(m)
                        p.wait_ready(DVE)
                        acc = accs[j % 2]
                        if j < 2:
                            if j == 0:
                                pacc.acquire(DVE)
                            d_ = DVE.tensor_copy(out=acc[:, :], in_=p.ap[:, :])
                        else:
                            chain[j % 2].wait(DVE)
                            d_ = DVE.tensor_tensor(out=acc[:, :], in0=acc[:, :], in1=p.ap[:, :], op=ALU.add)
                        p.consumed(d_)
                        chain[j % 2].add(d_)
                        if j == NP - 1:
                            chain[0].wait(DVE)
                            chain[1].wait(DVE)
                            d2 = DVE.tensor_tensor(out=accs[0][:, :], in0=accs[0][:, :], in1=accs[1][:, :], op=ALU.add)
                            fin = Evs("fin")
                            fin.add(d2)
                            fin.wait(DVE)
                            d3 = DVE.tensor_tensor(out=pacc.ap[:, :], in0=accs[0][:, 0:T], in1=accs[0][:, T:2 * T], op=ALU.add)
                            pacc.produced(d3)
                        return m
                    s_mm(0)
                    s_mm(1)
                    for j in range(NP):
                        if j + 2 < NP:
                            s_mm(j + 2)
                        m = pv_mm(j)
                        if j == 2 and pend_tail[0] is not None:
                            pend_tail[0]()
                            pend_tail[0] = None
                    ob.produced(m)
                    pacc.wait_ready(PE)
                    m = PE.matmul(db_.ap[:, :], lhsT=ones_f[:, :], rhs=pacc.ap[:, :], start=True, stop=True)
                    pacc.consumed(m)
                    db_.produced(m)
                    pend_tail[0] = (lambda ob=ob, db_=db_, dst=obT.ap()[hq * 128:(hq + 1) * 128, tsl], o_=ostb[it % 2]:
                                    attn_tail(ob, db_, dst, rden, o_))
                    it += 1
            pend_tail[0]()
            pend_tail[0] = None
            banks[7].acquire(PE)
            misc.inc(PE.matmul(banks[7].ap[:, 0:1], lhsT=ones[:, :], rhs=ones[:, 0:1], start=True, stop=True))
            for e in (SP, ACT, DVE):
                misc.wait(e)
                st.wait(e)
            st.wait(PE)

        if STOP == "mb":
            raise _Stop()
        with ExitStack() as e3:
            def sbl(name, shape, dt):
                return e3.enter_context(nc.sbuf_tensor(name, list(shape), dt))
            kwin = Cur([Buf(nc, f"kwin{i}", sbl(f"kwin{i}", [128, 3, 4096], BF16)) for i in range(2)])
            vwin = Cur([Buf(nc, f"vwin{i}", sbl(f"vwin{i}", [128, 32, 384], BF16)) for i in range(2)])
            qwin = Cur([Buf(nc, f"qwin{i}", sbl(f"qwin{i}", [128, 3, TOK], BF16)) for i in range(2)])
            etb = Buf(nc, "etb", sbl("etb", [128, ETOT], BF16))
            sx = [Buf(nc, (f"tmp{i}" if i < 3 else "lnt"), sbl(f"sx{i}", [128, T], F32)) for i in range(4)]
            ps = [Buf(nc, ("sq%d" % i if i < 2 else ("sil%d" % (i - 2) if i < 4 else "hid")), sbl(f"pa{i}", [128, T], BF16)) for i in range(5)]
            rden = Buf(nc, "rstd", sbl("rdena", [128, T], F32))
            osta = [Buf(nc, f"ost{i}", sbl(f"osta{i}", [128, T], BF16)) for i in range(2)]
            hal, hal2 = halo_evs[0]
            hal2.wait(SP)
            hal.wait(SP)
            qaTv = qaT.ap().rearrange("(h p) t -> h p t", p=128)
            hKpv = hKp.ap().rearrange("(h p) o t -> h p (o t)", p=128)
            hKnv = hKn.ap().rearrange("(h p) o t -> h p (o t)", p=128)
            hVpv = hVp.ap().rearrange("(b p) o c -> p b (o c)", p=128)
            hVnv = hVn.ap().rearrange("(b p) o c -> p b (o c)", p=128)
            it = 0
            lastd = [None]

            def load_slot(h):
                for b_ in (kwin, vwin, qwin):
                    b_.sel(h % 2)
                    b_.acquire(SP)
                for g in range(3):
                    hd = 4 * g + h
                    kwin.produced(SP.dma_start(out=kwin.ap[:, g, 1024:3072],
                                               in_=kaTc[hd // 2].ap()[(hd % 2) * 128:(hd % 2 + 1) * 128, :]), 16)
                    kwin.produced(SP.dma_start(out=kwin.ap[:, g, 0:1024], in_=hKpv[hd]), 16)
                    kwin.produced(SP.dma_start(out=kwin.ap[:, g, 3072:4096], in_=hKnv[hd]), 16)
                    for c in range(8):
                        vwin.produced(SP.dma_start(
                            out=vwin.ap[:, 8 + 2 * c:10 + 2 * c, g * 128:(g + 1) * 128],
                            in_=vaLc[c].ap()[:, hd * 128:(hd + 1) * 128].rearrange("(b p) c -> p b c", p=128)), 16)
                    vwin.produced(SP.dma_start(out=vwin.ap[:, 0:8, g * 128:(g + 1) * 128],
                                               in_=hVpv[:, :, hd * 128:(hd + 1) * 128]), 16)
                    vwin.produced(SP.dma_start(out=vwin.ap[:, 24:32, g * 128:(g + 1) * 128],
                                               in_=hVnv[:, :, hd * 128:(hd + 1) * 128]), 16)
                    qwin.produced(SP.dma_start(out=qwin.ap[:, g, :], in_=qaTv[hd]), 16)

            def compute_slot(h):
                nonlocal it
                for b_ in (kwin, vwin, qwin):
                    b_.sel(h % 2)
                for b_ in (kwin, vwin, qwin):
                    b_.wait_ready(PE)
                etb.wait_ready(DVE)
                for qt in range(NT):
                    tsl = slice(qt * T, (qt + 1) * T)
                    ob = banks[3 + it % 2]
                    db_ = banks[5 + it % 2]
                    ob.acquire(PE)
                    db_.acquire(PE)
                    work = []
                    for g in range(3):
                        for kb in range(4 * qt - DMAX[g], 4 * qt + 3 + DMAX[g] + 1):
                            work.append((g, kb))
                    nw = len(work)

                    def crange(i):
                        g, kb = work[i]
                        d0 = kb - 4 * qt
                        lo = max(0, d0 - DMAX[g])
                        hi = min(3, d0 + DMAX[g])
                        return lo * 128, (hi + 1) * 128

                    def s_mm(i):
                        g, kb = work[i]
                        p_ = kb + 8
                        ca, cb = crange(i)
                        sbk = banks[(0, 1, 2, 7)[i % 4]]
                        sbk.acquire(PE)
                        m = PE.matmul(sbk.ap[:, ca:cb], lhsT=kwin.ap[:, g, p_ * 128:(p_ + 1) * 128],
                                      rhs=qwin.ap[:, g, qt * T + ca:qt * T + cb], start=True, stop=True)
                        sbk.produced(m)
                        x_ = sx[i % 4]
                        sbk.wait_ready(ACT)
                        x_.acquire(ACT)
                        a = ACT.activation(out=x_.ap[:, ca:cb], in_=sbk.ap[:, ca:cb], func=AF.Exp, scale=SCALE,
                                           bias=vtab[:, p_:p_ + 1])
                        sbk.consumed(a)
                        x_.produced(a)
                        p = ps[i % 5]
                        x_.wait_ready(DVE)
                        p.acquire(DVE)
                        d0 = kb - 4 * qt
                        c0 = EOFF[g] + (DMAXP[g] - d0) * 128
                        d_ = DVE.tensor_tensor(out=p.ap[:, ca:cb], in0=x_.ap[:, ca:cb], in1=etb.ap[:, c0 + ca:c0 + cb], op=ALU.mult)
                        x_.consumed(d_)
                        p.produced(d_)
                        lastd[0] = d_

                    def pv_mm(i):
                        g, kb = work[i]
                        p_ = kb + 8
                        ca, cb = crange(i)
                        p = ps[i % 5]
                        p.wait_ready(PE)
                        PE.matmul(ob.ap[:, ca:cb], lhsT=vwin.ap[:, p_, g * 128:(g + 1) * 128], rhs=p.ap[:, ca:cb],
                                  start=(i == 0), stop=(i == nw - 1), skip_group_check=True)
                        m = PE.matmul(db_.ap[:, ca:cb], lhsT=ones[:, :], rhs=p.ap[:, ca:cb],
                                      start=(i == 0), stop=(i == nw - 1), skip_group_check=True)
                        p.consumed(m)
                        return m
                    s_mm(0)
                    s_mm(1)
                    s_mm(2)
                    for i in range(nw):
                        if i + 3 < nw:
                            s_mm(i + 3)
                        m = pv_mm(i)
                        if i == 2 and pend_tail[0] is not None:
                            pend_tail[0]()
                            pend_tail[0] = None
                    ob.produced(m)
                    db_.produced(m)
                    if qt == NT - 1:
                        for b_ in (kwin, vwin, qwin):
                            b_.consumed(m)
                        etb.consumed(lastd[0])
                    pend_tail[0] = (lambda ob=ob, db_=db_, dst=oaT.ap()[h * 128:(h + 1) * 128, tsl], o_=osta[it % 2]:
                                    attn_tail(ob, db_, dst, rden, o_))
                    it += 1
            def load_etb(h):
                etb.acquire(SP)
                etb.produced(SP.dma_start(out=etb.ap[:, :], in_=etab_d.ap()[h]), 16)
            load_slot(0)
            load_etb(0)
            load_slot(1)
            for h in range(4):
                compute_slot(h)
                if h + 1 < 4:
                    load_etb(h + 1)
                if h + 2 < 4:
                    load_slot(h + 2)
            pend_tail[0]()
            pend_tail[0] = None
            banks[7].acquire(PE)
            misc.inc(PE.matmul(banks[7].ap[:, 0:1], lhsT=ones[:, :], rhs=ones[:, 0:1], start=True, stop=True))
            for e in (SP, ACT, DVE):
                misc.wait(e)
                st.wait(e)
            st.wait(PE)

        if STOP == "ma":
            raise _Stop()
        run_token_phases(False)
        st.wait(SP)
        for e in (PE, ACT, DVE, GP):
            st.wait(e)
    return nc


def _tab16(v):
    return np.ascontiguousarray(np.asarray(v, np.float32).reshape(DC, 128).T)


def _const_tables():
    half = 64
    inv_freq = (10000.0 ** (-np.arange(0, half, 2, dtype=np.float32) / half)).astype(np.float32)
    tpos = np.arange(S)
    row = (tpos // 64).astype(np.float32)
    col = (tpos % 64).astype(np.float32)
    ang = np.concatenate([row[:, None] * inv_freq, col[:, None] * inv_freq], axis=-1).astype(np.float32)
    cos = np.cos(ang).astype(np.float32)
    sin = np.sin(ang).astype(np.float32)
    C = np.concatenate([cos, cos], axis=1).T
    Sn = np.concatenate([-sin, sin], axis=1).T
    slopes = (2.0 ** (-8.0 * np.arange(1, 13, dtype=np.float32) / 12.0)).astype(np.float32).reshape(3, 4)
    dil = (1, 4, 16)
    et = np.zeros((4, 128, ETOT), np.float32)
    for h in range(4):
        for g in range(3):
            k = np.arange(128)[:, None]
            c = np.arange(EW[g])[None, :]
            diff = 128 * DMAXP[g] + k - c
            ok = (diff % dil[g] == 0) & (np.abs(diff) <= 64 * dil[g])
            val = np.exp(-slopes[g, h] * np.abs(diff).astype(np.float32)).astype(np.float32)
            et[h, :, EOFF[g]:EOFF[g] + EW[g]] = np.where(ok, val, 0.0)
    return np.ascontiguousarray(C), np.ascontiguousarray(Sn), et.astype(ml_dtypes.bfloat16)


def _win_layout(w_in):
    qa = np.arange(0, 1536)
    ka = np.arange(1536, 3072)
    va = np.arange(3072, 4608)
    qb0 = 4608
    kb0 = 4608 + 1024
    vb0 = kb0 + 256
    ga0 = vb0 + 256
    gb0 = ga0 + 2048

    def swap(base):
        return np.concatenate([np.arange(base + 64, base + 128), np.arange(base, base + 64)])
    cols = [qa, ka]
    for h in range(8):
        cols.append(np.arange(qb0 + h * 128, qb0 + (h + 1) * 128))
        cols.append(swap(qb0 + h * 128))
    for h in range(2):
        cols.append(np.arange(kb0 + h * 128, kb0 + (h + 1) * 128))
        cols.append(swap(kb0 + h * 128))
    cols.append(np.arange(ga0, ga0 + 2048))
    cols.append(np.arange(gb0, gb0 + 2048))
    cols.append(va)
    cols.append(np.arange(vb0, vb0 + 256))
    idx = np.concatenate(cols)
    assert idx.shape[0] == WIN_COLS
    return np.ascontiguousarray(w_in[:, idx])


_NC_CACHE = {}


def kernel(x, c, w_ada, b_ada, norm_ffn1, w1_ffn1, w3_ffn1, w2_ffn1, norm_mix, w_in,
           q_norm_a, k_norm_a, q_norm_b, k_norm_b, w_branch_a, w_branch_b, w_out,
           norm_ffn2, w1_ffn2, w3_ffn2, w2_ffn2, norm_final):
    f = lambda a: np.ascontiguousarray(np.asarray(a, dtype=np.float32))
    x = f(x); c = f(c)
    w_ada0 = f(w_ada)[0]; b_ada0 = f(b_ada)[0]
    C, Sn, et = _const_tables()
    gt = np.concatenate([_tab16(f(norm_ffn1)[0]), _tab16(f(norm_mix)[0]), _tab16(f(norm_ffn2)[0]),
                         _tab16(f(norm_final))], axis=1)
    gqa, gka, gqb, gkb = f(q_norm_a)[0], f(k_norm_a)[0], f(q_norm_b)[0], f(k_norm_b)[0]
    sw = lambda v: np.concatenate([v[64:], v[:64]])
    qk = np.stack([gqa, gka, gqb, gkb, sw(gqb), sw(gkb), gqa, gqa], axis=1)
    qk = np.ascontiguousarray(qk.astype(np.float32))
    winp = _win_layout(f(w_in)[0])
    shared = {
        "w1a": f(w1_ffn1)[0], "w3a": f(w3_ffn1)[0], "w2a": f(w2_ffn1)[0],
        "w1b": f(w1_ffn2)[0], "w3b": f(w3_ffn2)[0], "w2b": f(w2_ffn2)[0],
        "win": winp, "wpa": f(w_branch_a)[0], "wpb": f(w_branch_b)[0], "wout": f(w_out)[0],
        "gtab": np.ascontiguousarray(gt), "qkg": qk, "etab": et,
    }
    in_maps = []
    for r in range(NCORES):
        b, g = r // 4, r % 4
        t0 = g * TOK
        vt = np.zeros((128, 32), np.float32)
        for p in range(32):
            gb = g * 16 - 8 + p
            if gb < 0 or gb >= 64:
                vt[:, p] = NEG
        m = dict(shared)
        m.update({
            "xT": np.ascontiguousarray(x[b, t0:t0 + TOK, :].T),
            "cT": _tab16(c[b]),
            "wada": np.ascontiguousarray(w_ada0[:, g * 4608:(g + 1) * 4608]),
            "bada": np.ascontiguousarray(b_ada0[g * 4608:(g + 1) * 4608].reshape(36, 128).T),
            "ropeC": np.ascontiguousarray(C[:, t0:t0 + TOK]),
            "ropeS": np.ascontiguousarray(Sn[:, t0:t0 + TOK]),
            "vtab": vt,
        })
        in_maps.append(m)
    if "nc" not in _NC_CACHE:
        _NC_CACHE["nc"] = build_program()
    nc = _NC_CACHE["nc"]
    res = run_bass_kernel_spmd(nc, in_maps, core_ids=list(range(NCORES)))
    out = np.empty((2, S, D), np.float32)
    for r in range(NCORES):
        b, g = r // 4, r % 4
        out[b, g * TOK:(g + 1) * TOK, :] = res.results[r]["yT"].T
    return out


if __name__ == "__main__":
    import time
    t0 = time.time()
    nc = build_program()
    print("build ok", time.time() - t0)
```

```python
import math
import os
from contextlib import ExitStack

import numpy as np
import ml_dtypes

import concourse.bass as bass
import concourse.mybir as mybir
from concourse.bass_utils import run_bass_kernel_spmd

F32 = mybir.dt.float32
BF16 = mybir.dt.bfloat16
AF = mybir.ActivationFunctionType
ALU = mybir.AluOpType

NCORES = 8
D = 2048
S = 8192
TOK = 2048
T = 512
NT = TOK // T
DC = D // 128
DFF = 5632
NJ = DFF // 128
HD = 128
EPS = 1e-6
SCALE = HD ** -0.5
NEG = -30000.0

FM_COLS = 1536 + 1536 + 2048 + 512 + 2048 + 2048
TM_COLS = 1536 + 256
WIN_COLS = FM_COLS + TM_COLS

DMAXP = (4, 5, 11)
DMAX = (1, 2, 8)
EW = tuple((2 * d + 1) * 128 for d in DMAXP)
EOFF = (0, EW[0], EW[0] + EW[1])
ETOT = sum(EW)


class _Stop(Exception):
    pass


STOP = os.environ.get("KSTOP", "")


class Sem:
    def __init__(self, nc, name):
        self.s = nc.alloc_semaphore(name)
        self.n = 0
        self.waited = {}

    def inc(self, ins, by=1):
        ins.then_inc(self.s, by)
        self.n += by
        return self.n

    def wait(self, eng, val=None):
        v = self.n if val is None else val
        if v <= 0:
            return
        k = id(eng)
        if self.waited.get(k, 0) >= v:
            return
        eng.wait_ge(self.s, v)
        self.waited[k] = v


class G:
    nc = None
    PROG = {}
    SIG = {}
    DSEM = {}

    @staticmethod
    def reset(nc):
        G.nc = nc
        G.PROG = {}
        G.SIG = {}
        G.DSEM = {}

    @staticmethod
    def dsem(name):
        if name not in G.DSEM:
            G.DSEM[name] = Sem(G.nc, name + "_d")
        return G.DSEM[name]

    @staticmethod
    def signal(ins, dname):
        k = id(ins)
        if k in G.SIG:
            return G.SIG[k][1]
        txt = str(ins.ins)[:24]
        eng = txt.split()[0]
        if "DMA" in txt:
            sem = G.dsem(dname)
            ev = (sem, sem.inc(ins, 16))
        else:
            if eng not in G.PROG:
                G.PROG[eng] = Sem(G.nc, "prog_" + eng)
            sem = G.PROG[eng]
            ev = (sem, sem.inc(ins, 1))
        G.SIG[k] = (ins, ev)
        return ev


class Evs:
    def __init__(self, name):
        self.name = name
        self.m = {}

    def add(self, ins, by=None):
        sem, v = G.signal(ins, self.name)
        self.m[sem] = max(self.m.get(sem, 0), v)

    inc = add

    def wait(self, eng):
        for sem, v in self.m.items():
            sem.wait(eng, v)


class Cur:
    def __init__(self, bufs):
        self.__dict__["bufs"] = bufs
        self.__dict__["i"] = 0

    def sel(self, i):
        self.__dict__["i"] = i

    def __getattr__(self, name):
        return getattr(self.bufs[self.i], name)


class Buf:
    def __init__(self, nc, name, ap):
        self.ap = ap
        self.rd = Evs(name)
        self.fr = Evs(name)

    def acquire(self, eng):
        self.fr.wait(eng)
        self.rd.wait(eng)

    def produced(self, ins, by=None):
        self.rd.add(ins)

    def wait_ready(self, eng):
        self.rd.wait(eng)

    def consumed(self, ins, by=None):
        self.fr.add(ins)


def build_program():
    holder = {}
    try:
        _build_program(holder)
    except _Stop:
        pass
    return holder["nc"]


def _build_program(holder):
    nc = bass.Bass("TRN2", target_bir_lowering=False)
    G.reset(nc)
    holder["nc"] = nc
    PE, ACT, DVE, SP, GP = nc.tensor, nc.scalar, nc.vector, nc.sync, nc.gpsimd

    def din(name, shape, dt=F32):
        return nc.dram_tensor(name, list(shape), dt, kind="ExternalInput")

    def dscr(name, shape, dt):
        return nc.dram_tensor(name, list(shape), dt)

    xT = din("xT", [D, TOK])
    cT = din("cT", [128, DC])
    wada = din("wada", [D, 4608])
    bada = din("bada", [128, 36])
    gtab_d = din("gtab", [128, 4 * DC])
    qkg_d = din("qkg", [128, 8])
    w1a = din("w1a", [D, DFF]); w3a = din("w3a", [D, DFF]); w2a = din("w2a", [DFF, D])
    w1b = din("w1b", [D, DFF]); w3b = din("w3b", [D, DFF]); w2b = din("w2b", [DFF, D])
    win = din("win", [D, WIN_COLS])
    wpa = din("wpa", [512, D]); wpb = din("wpb", [1024, D]); wout = din("wout", [D, D])
    ropeC_d = din("ropeC", [128, TOK]); ropeS_d = din("ropeS", [128, TOK])
    etab_d = din("etab", [4, 128, ETOT], BF16)
    vtab_d = din("vtab", [128, 32])
    yT = nc.dram_tensor("yT", [D, TOK], F32, kind="ExternalOutput")

    modinA = dscr("modinA", [128, 12], F32)
    modoutA = dscr("modoutA", [512, 12], F32)
    modinB = dscr("modinB", [128, 24], F32)
    modoutB = dscr("modoutB", [512, 24], F32)
    h1T = dscr("h1T", [D, TOK], F32)
    qaT = dscr("qaT", [1536, TOK], BF16)
    kaTc = [dscr(f"kaT{c}", [256, TOK], BF16) for c in range(6)]
    kaGc = [dscr(f"kaG{c}", [1024, TOK], BF16) for c in range(6)]
    vaLc = [dscr(f"vaL{c}", [256, 1536], BF16) for c in range(8)]
    vaGc = [dscr(f"vaG{c}", [1024, 1536], BF16) for c in range(8)]
    qbT = dscr("qbT", [1024, TOK], BF16)
    kbTt = [dscr(f"kbT{t}", [256, T], BF16) for t in range(NT)]
    kbGt = [dscr(f"kbG{t}", [4 * 256, T], BF16) for t in range(NT)]
    vbLt = [dscr(f"vbL{t}", [T, 256], BF16) for t in range(NT)]
    vbGt = [dscr(f"vbG{t}", [4 * T, 256], BF16) for t in range(NT)]
    sgT = dscr("sgT", [2 * D, TOK], BF16)
    hKp = dscr("hKp", [1536, 1, 1024], BF16)
    hKn = dscr("hKn", [1536, 1, 1024], BF16)
    hVp = dscr("hVp", [1024, 1, 1536], BF16)
    hVn = dscr("hVn", [1024, 1, 1536], BF16)
    oaT = dscr("oaT", [512, TOK], BF16)
    obT = dscr("obT", [1024, TOK], BF16)

    def wview(w):
        return w.ap().rearrange("(kc p) n -> p kc n", p=128)

    with ExitStack() as es:
        def sb(name, shape, dt):
            return es.enter_context(nc.sbuf_tensor(name, list(shape), dt))

        ones = sb("ones", [128, 128], BF16)
        ones_f = sb("ones_f", [128, 128], F32)
        gtab = sb("gtabs", [128, 4 * DC], F32)
        qkg = sb("qkgs", [128, 8], F32)
        vtab = sb("vtabs", [128, 32], F32)
        modT = sb("modT", [128, 144], F32)
        tabA = sb("tabA", [128, 3 * DC], F32)
        tabG = sb("tabG", [128, 3 * DC], F32)
        NSLOT = 3
        slots = [Buf(nc, f"slot{i}", sb(f"slot{i}", [128, 8192], BF16)) for i in range(NSLOT)]
        psum_all = nc.alloc_psum_tensor("psum_all", [128, 4096], F32)
        banks = [Buf(nc, f"bank{i}", psum_all[:, i * 512:(i + 1) * 512]) for i in range(8)]

        ld = Evs("ld")
        ldr = Evs("ldr")
        ldb = Evs("ldb")
        st = Evs("st")
        cc = Sem(nc, "cc")
        ccb = Sem(nc, "ccb")
        cca = Sem(nc, "cca")
        groups = [[0, 1, 2, 3], [4, 5, 6, 7]]
        misc = Evs("misc")

        slot_ctr = [0]

        def wstage(parts):
            sl = slots[slot_ctr[0] % NSLOT]
            slot_ctr[0] += 1
            sl.acquire(GP)
            for vf, src in parts:
                sl.produced(GP.dma_start(out=vf(sl.ap), in_=src), 16)
            for d in list(deferred):
                d[0] -= 1
                if d[0] <= 0:
                    deferred.remove(d)
                    d[1]()
            return sl

        deferred = []

        def flush_deferred():
            for d in list(deferred):
                deferred.remove(d)
                d[1]()

        ld.inc(SP.dma_start(out=gtab[:, :], in_=gtab_d[:, :]), 16)
        ld.inc(SP.dma_start(out=qkg[:, :], in_=qkg_d[:, :]), 16)
        ld.inc(SP.dma_start(out=vtab[:, :], in_=vtab_d[:, :]), 16)
        misc.inc(DVE.memset(ones[:, :], 1.0))
        misc.inc(DVE.memset(ones_f[:, :], 1.0))
        for e in (PE, ACT, DVE):
            ld.wait(e)
            misc.wait(e)

        cts = sb("cts", [128, DC], F32)
        cact = sb("cact", [128, DC], BF16)
        bad = sb("bads", [128, 36], F32)
        modp = sb("modp", [128, 36], F32)
        ld0 = Evs("ld0")
        ld0.inc(SP.dma_start(out=cts[:, :], in_=cT[:, :]), 16)
        ld0.inc(SP.dma_start(out=bad[:, :], in_=bada[:, :]), 16)
        ld0.wait(ACT)
        misc.inc(ACT.activation(out=cact[:, :], in_=cts[:, :], func=AF.Silu))
        misc.wait(PE)
        wv = wview(wada)
        bk = banks[7]
        cc2 = Sem(nc, "cc2")
        grp4 = [[0, 1, 2, 3], [4, 5, 6, 7]]

        def adaln_stage(si):
            sl = wstage([(lambda a: a[:, :].rearrange("p (k n) -> p k n", n=512),
                          wv[:, :, si * 512:(si + 1) * 512])])
            sl.wait_ready(PE)
            if si == 0 or si == 3:
                bk.acquire(PE)
            sv = sl.ap[:, :].rearrange("p (k n) -> p k n", n=512)
            last = None
            for i in range(4):
                col = si * 4 + i
                for kc in range(DC):
                    last = PE.matmul(bk.ap[:, col:col + 1], lhsT=sv[:, kc, i * 128:(i + 1) * 128],
                                     rhs=cact[:, kc:kc + 1], start=(kc == 0), stop=(kc == DC - 1))
            sl.consumed(last)
            if si == 2 or si == 8:
                bk.produced(last)
            return last

        def adaln_finish(c0, c1, din_t, dout_t, sem, defer):
            bk.wait_ready(DVE)
            ld0.wait(DVE)
            i_ = DVE.tensor_tensor(out=modp[:, c0:c1], in0=bk.ap[:, c0:c1], in1=bad[:, c0:c1], op=ALU.add)
            bk.consumed(i_)
            ev = Evs("modst" + str(c0))
            ev.add(i_)
            ev.wait(SP)
            dm = SP.dma_start(out=din_t.ap(), in_=modp[:, c0:c1])
            st.inc(dm, 16)
            ev2 = Evs("modst2" + str(c0))
            ev2.add(dm)

            def trig():
                ev2.wait(GP)
                sem.inc(GP.collective_compute("AllGather", ALU.bypass, replica_groups=grp4,
                                              ins=[din_t.ap().opt()], outs=[dout_t.ap().opt()]))
            if defer:
                deferred.append([2, trig])
            else:
                trig()

        def adaln_tables(ks, j0, j1, dout_t, sem):
            n = (j1 - j0) // 4
            sem.wait(ACT)
            ldm = Evs("ldm" + str(j0))
            ldm.inc(ACT.dma_start(out=modT[:, j0:j1].rearrange("p (r c) -> p r c", c=n),
                                  in_=dout_t.ap().rearrange("(r p) c -> p r c", p=128)), 16)
            ldm.wait(DVE)
            tev = Evs("tabev" + str(j0))
            for k in ks:
                sc = modT[:, (3 * k + 1) * DC:(3 * k + 2) * DC]
                g = modT[:, (3 * k + 2) * DC:(3 * k + 3) * DC]
                tev.inc(DVE.scalar_tensor_tensor(out=tabA[:, k * DC:(k + 1) * DC], in0=sc, scalar=1.0,
                                                 in1=gtab[:, k * DC:(k + 1) * DC], op0=ALU.add, op1=ALU.mult))
                tev.inc(DVE.tensor_scalar(out=tabG[:, k * DC:(k + 1) * DC], in0=g,
                                          scalar1=(1.0 if k == 1 else 0.5), scalar2=None, op0=ALU.mult))
            for e in (ACT, DVE, PE):
                tev.wait(e)

        for si in range(3):
            adaln_stage(si)
        adaln_finish(0, 12, modinA, modoutA, cc, False)
        adaln_tables([0], 0, 48, modoutA, cc)

        if STOP == "p0":
            raise _Stop()

        def tabB(k):
            return modT[:, (3 * k) * DC:(3 * k + 1) * DC]

        def run_token_phases(first):
            with ExitStack() as e1:
                def sbl(name, shape, dt):
                    return e1.enter_context(nc.sbuf_tensor(name + ("A" if first else "B"), list(shape), dt))

                xt = Buf(nc, "xt", sbl("xt", [128, DC, T], F32))
                class _Cur:
                    def __init__(self, bufs):
                        self.__dict__["bufs"] = bufs
                        self.__dict__["i"] = 0

                    def sel(self, i):
                        self.__dict__["i"] = i

                    def __getattr__(self, name):
                        return getattr(self.bufs[self.i], name)
                _u0 = Buf(nc, "u", sbl("u", [128, DC, T], BF16))
                _u1 = Buf(nc, "u2", sbl("u2", [128, DC, T], BF16)) if first else _u0
                u = _Cur([_u0, _u1])
                hid = Buf(nc, "hid", sbl("hid", [128, NJ, T], BF16))
                rstd = Buf(nc, "rstd", sbl("rstd", [128, T], F32))
                lnt = Buf(nc, "lnt", sbl("lnt", [128, T], F32))
                sqs = [Buf(nc, f"sq{i}", sbl(f"sq{i}", [128, T], BF16)) for i in range(4)]
                tmps = [Buf(nc, f"tmp{i}", sbl(f"tmp{i}", [128, T], F32)) for i in range(3)]
                sils = [Buf(nc, f"sil{i}", sbl(f"sil{i}", [128, T], F32)) for i in range(2)]
                osts = [Buf(nc, f"ost{i}", sbl(f"ost{i}", [128, T], BF16)) for i in range(4)]
                ost_c = [0]

                def rms_rstd(src_bank_or_none, src_chunks, nfeat, statbank, split=False):
                    n = len(src_chunks)
                    statbank.acquire(PE)
                    for i, (ap, wfn, cfn) in enumerate(src_chunks):
                        sq = sqs[i % 4]
                        if split and i % 2 == 1:
                            sq.acquire(DVE)
                            wfn(DVE)
                            a = DVE.tensor_tensor(out=sq.ap[:, :], in0=ap, in1=ap, op=ALU.mult)
                        else:
                            sq.acquire(ACT)
                            wfn(ACT)
                            a = ACT.activation(out=sq.ap[:, :], in_=ap, func=AF.Square)
                        sq.produced(a)
                        cfn(a)
                        sq.wait_ready(PE)
                        m = PE.matmul(statbank.ap[:, :], lhsT=ones[:, :], rhs=sq.ap[:, :],
                                      start=(i == 0), stop=(i == n - 1))
                        sq.consumed(m)
                    statbank.produced(m)
                    statbank.wait_ready(ACT)
                    lnt.acquire(ACT)
                    a = ACT.activation(out=lnt.ap[:, :], in_=statbank.ap[:, :], func=AF.Ln,
                                       scale=1.0 / nfeat, bias=EPS)
                    statbank.consumed(a)
                    lnt.produced(a)
                    lnt.wait_ready(ACT)
                    rstd.acquire(ACT)
                    a = ACT.activation(out=rstd.ap[:, :], in_=lnt.ap[:, :], func=AF.Exp, scale=-0.5)
                    lnt.consumed(a)
                    rstd.produced(a)

                uch = [[None] * DC, [None] * DC]

                def norm_mod(k):
                    chunks = [(xt.ap[:, dc, :], xt.wait_ready, xt.consumed) for dc in range(DC)]
                    rms_rstd(None, chunks, D, banks[6], split=True)
                    rstd.wait_ready(DVE)
                    xt.wait_ready(DVE)
                    u.acquire(ACT)
                    for dc in range(DC):
                        tmp = tmps[dc % 3]
                        tmp.acquire(DVE)
                        i_ = DVE.scalar_tensor_tensor(out=tmp.ap[:, :], in0=xt.ap[:, dc, :],
                                                      scalar=tabA[:, k * DC + dc:k * DC + dc + 1],
                                                      in1=rstd.ap[:, :], op0=ALU.mult, op1=ALU.mult)
                        tmp.produced(i_)
                        xt.consumed(i_)
                        if dc == DC - 1:
                            rstd.consumed(i_)
                        tmp.wait_ready(ACT)
                        a = ACT.activation(out=u.ap[:, dc, :], in_=tmp.ap[:, :], func=AF.Identity,
                                           bias=tabB(k)[:, dc:dc + 1], scale=1.0)
                        tmp.consumed(a)
                        u.produced(a)
                        ev = Evs("uch")
                        ev.add(a)
                        uch[u.i][dc] = ev

                def ffn(k, w1, w3, w2, do_norm=True, stage_hook=None):
                    if do_norm:
                        norm_mod(k)
                    w1v, w3v = wview(w1), wview(w3)
                    w2v = w2.ap().rearrange("(j p) n -> p j n", p=128)
                    hid.acquire(DVE)
                    for s_ in range(NJ // 2):
                        if stage_hook is not None:
                            stage_hook(s_)
                        sl = wstage([
                            (lambda a: a[:, 0:4096].rearrange("p (k n) -> p k n", n=256), w1v[:, :, s_ * 256:(s_ + 1) * 256]),
                            (lambda a: a[:, 4096:8192].rearrange("p (k n) -> p k n", n=256), w3v[:, :, s_ * 256:(s_ + 1) * 256]),
                        ])
                        sl.wait_ready(PE)
                        v1 = sl.ap[:, 0:4096].rearrange("p (k n) -> p k n", n=256)
                        v3 = sl.ap[:, 4096:8192].rearrange("p (k n) -> p k n", n=256)
                        for jj in range(2):
                            j = 2 * s_ + jj
                            b1 = banks[j % 2]
                            b3 = banks[2 + j % 2]
                            b1.acquire(PE)
                            for kc in range(DC):
                                if j == 0:
                                    uch[u.i][kc].wait(PE)
                                m = PE.matmul(b1.ap[:, :], lhsT=v1[:, kc, jj * 128:(jj + 1) * 128], rhs=u.ap[:, kc, :],
                                              start=(kc == 0), stop=(kc == DC - 1))
                            b1.produced(m)
                            b3.acquire(PE)
                            for kc in range(DC):
                                m = PE.matmul(b3.ap[:, :], lhsT=v3[:, kc, jj * 128:(jj + 1) * 128], rhs=u.ap[:, kc, :],
                                              start=(kc == 0), stop=(kc == DC - 1))
                            b3.produced(m)
                            if jj == 1:
                                sl.consumed(m)
                            if j == NJ - 1:
                                u.consumed(m)
                            sil = sils[j % 2]
                            b1.wait_ready(ACT)
                            sil.acquire(ACT)
                            a = ACT.activation(out=sil.ap[:, :], in_=b1.ap[:, :], func=AF.Silu)
                            b1.consumed(a)
                            sil.produced(a)
                            sil.wait_ready(DVE)
                            b3.wait_ready(DVE)
                            d_ = DVE.tensor_tensor(out=hid.ap[:, j, :], in0=sil.ap[:, :], in1=b3.ap[:, :], op=ALU.mult)
                            sil.consumed(d_)
                            b3.consumed(d_)
                            hid.produced(d_)
                    hid.wait_ready(PE)
                    HJ = NJ // 2
                    for dp in range(DC // 2):
                        for half in range(2):
                            sl = wstage([(lambda a: a[:, 0:HJ * 256].rearrange("p (j n) -> p j n", n=256),
                                          w2v[:, half * HJ:(half + 1) * HJ, dp * 256:(dp + 1) * 256])])
                            sl.wait_ready(PE)
                            v2 = sl.ap[:, 0:HJ * 256].rearrange("p (j n) -> p j n", n=256)
                            for i in range(2):
                                db = dp * 2 + i
                                yb = banks[4 + (dp % 2) * 2 + i]
                                if half == 0:
                                    yb.acquire(PE)
                                for jj in range(HJ):
                                    j = half * HJ + jj
                                    m = PE.matmul(yb.ap[:, :], lhsT=v2[:, jj, i * 128:(i + 1) * 128], rhs=hid.ap[:, j, :],
                                                  start=(j == 0), stop=(j == NJ - 1))
                                if i == 1:
                                    sl.consumed(m)
                                if half == 1:
                                    yb.produced(m)
                                    if db == DC - 1:
                                        hid.consumed(m)
                                    yb.wait_ready(DVE)
                                    xt.acquire(DVE)
                                    d_ = DVE.scalar_tensor_tensor(out=xt.ap[:, db, :], in0=yb.ap[:, :],
                                                                  scalar=tabG[:, k * DC + db:k * DC + db + 1],
                                                                  in1=xt.ap[:, db, :], op0=ALU.mult, op1=ALU.add)
                                    yb.consumed(d_)
                                    xt.produced(d_)

                def store_stage(dst_ap, eng_producer_fn, extra=None):
                    o = osts[ost_c[0] % 4]
                    ost_c[0] += 1
                    eng, ins = eng_producer_fn(o)
                    o.produced(ins)
                    o.wait_ready(SP)
                    dm = SP.dma_start(out=dst_ap, in_=o.ap[:, :])
                    o.consumed(dm, 16)
                    st.inc(dm, 16)
                    if extra is not None:
                        extra.add(dm)

                if first:
                    ropeC = sbl("ropeC", [128, TOK], F32)
                    ropeS = sbl("ropeS", [128, TOK], F32)
                    winv = wview(win)
                    for t in range(NT):
                        tsl = slice(t * T, (t + 1) * T)
                        if t == 0:
                            xt.acquire(SP)
                            xt.produced(SP.dma_start(out=xt.ap[:, :, :],
                                                     in_=xT.ap().rearrange("(c p) t -> p c t", p=128)[:, :, tsl]), 16)
                            ldr.inc(SP.dma_start(out=ropeC[:, :], in_=ropeC_d[:, :]), 16)
                            ldr.inc(SP.dma_start(out=ropeS[:, :], in_=ropeS_d[:, :]), 16)
                        u.sel(0)
                        def hookB(s_):
                            if s_ % 3 == 1 and s_ // 3 < 6:
                                si = 3 + s_ // 3
                                adaln_stage(si)
                                if si == 8:
                                    adaln_finish(12, 36, modinB, modoutB, cc2, True)
                        ffn(0, w1a, w3a, w2a, do_norm=(t == 0), stage_hook=(hookB if t == 0 else None))
                        if t == 0:
                            adaln_tables([1, 2], 48, 144, modoutB, cc2)
                        xt.wait_ready(SP)
                        dm = SP.dma_start(out=h1T.ap().rearrange("(c p) t -> p c t", p=128)[:, :, tsl], in_=xt.ap[:, :, :])
                        xt.consumed(dm, 16)
                        st.inc(dm, 16)
                        u.sel(1)
                        norm_mod(1)
                        if t + 1 < NT:
                            nsl = slice((t + 1) * T, (t + 2) * T)
                            xt.acquire(SP)
                            xt.produced(SP.dma_start(out=xt.ap[:, :, :],
                                                     in_=xT.ap().rearrange("(c p) t -> p c t", p=128)[:, :, nsl]), 16)
                        first_pb = [True]
                        ldr.wait(DVE)
                        hb = 0

                        def fm_stage(si):
                            return wstage([(lambda a: a[:, :].rearrange("p (k n) -> p k n", n=512),
                                            winv[:, :, si * 512:(si + 1) * 512])])

                        def proj_block(sl, bi, bank, last_of_slot):
                            sv = sl.ap[:, :].rearrange("p (k n) -> p k n", n=512)
                            bank.acquire(PE)
                            for kc in range(DC):
                                if first_pb[0]:
                                    uch[u.i][kc].wait(PE)
                                m = PE.matmul(bank.ap[:, :], lhsT=sv[:, kc, bi * 128:(bi + 1) * 128], rhs=u.ap[:, kc, :],
                                              start=(kc == 0), stop=(kc == DC - 1))
                            first_pb[0] = False
                            bank.produced(m)
                            if last_of_slot:
                                sl.consumed(m)
                            return m

                        ssb_c = [0]

                        def qk_post(qbank, pbank, gcol, gpcol, dst_ap, extra=None):
                            ssb = banks[4 + ssb_c[0] % 2] if pbank is None else banks[6 + ssb_c[0] % 2]
                            ssb_c[0] += 1
                            rms_rstd(None, [(qbank.ap[:, :], qbank.wait_ready, lambda a: None)], HD, ssb)
                            rstd.wait_ready(DVE)
                            qbank.wait_ready(DVE)
                            if pbank is None:
                                def prod(o):
                                    o.acquire(DVE)
                                    i_ = DVE.scalar_tensor_tensor(out=o.ap[:, :], in0=qbank.ap[:, :],
                                                                  scalar=qkg[:, gcol:gcol + 1], in1=rstd.ap[:, :],
                                                                  op0=ALU.mult, op1=ALU.mult)
                                    qbank.consumed(i_)
                                    rstd.consumed(i_)
                                    return DVE, i_
                                store_stage(dst_ap, prod)
                            else:
                                t1, t2 = tmps[0], tmps[1]
                                t1.acquire(DVE)
                                i_ = DVE.scalar_tensor_tensor(out=t1.ap[:, :], in0=qbank.ap[:, :],
                                                              scalar=qkg[:, gcol:gcol + 1], in1=ropeC[:, tsl],
                                                              op0=ALU.mult, op1=ALU.mult)
                                qbank.consumed(i_)
                                t1.produced(i_)
                                t2.acquire(DVE)
                                pbank.wait_ready(DVE)
                                i_ = DVE.scalar_tensor_tensor(out=t2.ap[:, :], in0=pbank.ap[:, :],
                                                              scalar=qkg[:, gpcol:gpcol + 1], in1=ropeS[:, tsl],
                                                              op0=ALU.mult, op1=ALU.mult)
                                pbank.consumed(i_)
                                t2.produced(i_)
                                t3 = tmps[2]
                                t3.acquire(DVE)
                                t1.wait_ready(DVE)
                                t2.wait_ready(DVE)
                                i_ = DVE.tensor_tensor(out=t3.ap[:, :], in0=t1.ap[:, :], in1=t2.ap[:, :], op=ALU.add)
                                t1.consumed(i_)
                                t2.consumed(i_)
                                t3.produced(i_)

                                def prod(o):
                                    o.acquire(DVE)
                                    t3.wait_ready(DVE)
                                    i2 = DVE.tensor_tensor(out=o.ap[:, :], in0=t3.ap[:, :], in1=rstd.ap[:, :], op=ALU.mult)
                                    t3.consumed(i2)
                                    rstd.consumed(i2)
                                    return DVE, i2
                                store_stage(dst_ap, prod, extra)

                        lastm = [None]
                        kvst = Evs("kvst")

                        def do_qk_plain(stages):
                            nonlocal hb
                            for si in stages:
                                sl = fm_stage(si)
                                sl.wait_ready(PE)
                                for bi in range(4):
                                    head = (si % 3) * 4 + bi
                                    bank = banks[hb % 4]
                                    hb += 1
                                    lastm[0] = proj_block(sl, bi, bank, bi == 3)
                                    if si < 3:
                                        dst = qaT.ap()[head * 128:(head + 1) * 128, tsl]
                                    else:
                                        dst = kaTc[head // 2].ap()[(head % 2) * 128:(head % 2 + 1) * 128, tsl]
                                    qk_post(bank, None, 0 if si < 3 else 1, None, dst)

                        def do_rope(stages):
                            nonlocal hb
                            for si in stages:
                                sl = fm_stage(si)
                                sl.wait_ready(PE)
                                for hh in range(2):
                                    qbank = banks[(hb) % 4]
                                    pbank = banks[(hb + 1) % 4]
                                    hb += 2
                                    proj_block(sl, 2 * hh, qbank, False)
                                    lastm[0] = proj_block(sl, 2 * hh + 1, pbank, hh == 1)
                                    if si < 10:
                                        head = (si - 6) * 2 + hh
                                        dst = qbT.ap()[head * 128:(head + 1) * 128, tsl]
                                        qk_post(qbank, pbank, 2, 4, dst)
                                    else:
                                        dst = kbTt[t].ap()[hh * 128:(hh + 1) * 128, :]
                                        qk_post(qbank, pbank, 3, 5, dst, kvst)

                        def do_gates():
                            nonlocal hb
                            for si in range(11, 19):
                                sl = fm_stage(si)
                                sl.wait_ready(PE)
                                for bi in range(4):
                                    bank = banks[hb % 4]
                                    hb += 1
                                    lastm[0] = proj_block(sl, bi, bank, bi == 3)
                                    row = (si - 11) * 4 + bi

                                    def prod(o, bank=bank):
                                        o.acquire(ACT)
                                        bank.wait_ready(ACT)
                                        a = ACT.activation(out=o.ap[:, :], in_=bank.ap[:, :], func=AF.Sigmoid)
                                        bank.consumed(a)
                                        return ACT, a
                                    store_stage(sgT.ap()[row * 128:(row + 1) * 128, tsl], prod)

                        def do_v(vis):
                            nonlocal hb
                            for vi in vis:
                                ncol = 512 if vi < 3 else 256
                                c0 = FM_COLS + vi * 512
                                sl = wstage([(lambda a, ncol=ncol: a[:, 0:DC * ncol].rearrange("p (k n) -> p k n", n=ncol),
                                              winv[:, :, c0:c0 + ncol])])
                                sl.wait_ready(PE)
                                sv = sl.ap[:, 0:DC * ncol].rearrange("p (k n) -> p k n", n=ncol)
                                for tb in range(4):
                                    bank = banks[hb % 4]
                                    hb += 1
                                    bank.acquire(PE)
                                    for kc in range(DC):
                                        m = PE.matmul(bank.ap[:, 0:ncol], lhsT=u.ap[:, kc, tb * 128:(tb + 1) * 128], rhs=sv[:, kc, :],
                                                      start=(kc == 0), stop=(kc == DC - 1))
                                    bank.produced(m)
                                    lastm[0] = m
                                    if tb == 3:
                                        sl.consumed(m)
                                    r0 = t * T + tb * 128
                                    if vi < 3:
                                        dst = vaLc[r0 // 256].ap()[r0 % 256:r0 % 256 + 128, vi * 512:(vi + 1) * 512]
                                    else:
                                        dst = vbLt[t].ap()[tb * 128:(tb + 1) * 128, :]
                                    o = osts[ost_c[0] % 4]
                                    ost_c[0] += 1
                                    o.acquire(DVE)
                                    bank.wait_ready(DVE)
                                    i_ = DVE.tensor_copy(out=o.ap[:, 0:ncol], in_=bank.ap[:, 0:ncol])
                                    bank.consumed(i_)
                                    o.produced(i_)
                                    o.wait_ready(SP)
                                    dm = SP.dma_start(out=dst, in_=o.ap[:, 0:ncol])
                                    o.consumed(dm, 16)
                                    st.inc(dm, 16)
                                    if vi == 3:
                                        kvst.add(dm)

                        do_rope([10])
                        do_v([3])

                        def trig(t=t, kvst=kvst):
                            kvst.wait(GP)
                            ccb.inc(GP.collective_compute("AllGather", ALU.bypass, replica_groups=groups,
                                                          ins=[kbTt[t].ap().opt()], outs=[kbGt[t].ap().opt()]))
                            ccb.inc(GP.collective_compute("AllGather", ALU.bypass, replica_groups=groups,
                                                          ins=[vbLt[t].ap().opt()], outs=[vbGt[t].ap().opt()]))
                        deferred.append([3, trig])
                        do_qk_plain([3, 4, 5])
                        do_v([0, 1, 2])
                        do_qk_plain([0, 1, 2])
                        if t + 1 < NT:
                            u.sel(0)
                            norm_mod(0)
                            u.sel(1)
                        do_rope([6, 7, 8, 9])
                        do_gates()
                        u.consumed(lastm[0])
                else:
                    oat = Buf(nc, "oat", sbl("oat", [128, 4, T], BF16))
                    obt = Buf(nc, "obt", sbl("obt", [128, 8, T], BF16))
                    sga = [Buf(nc, f"sga{i}", sbl(f"sga{i}", [128, 4, T], BF16)) for i in range(2)]
                    sgb = [Buf(nc, f"sgb{i}", sbl(f"sgb{i}", [128, 4, T], BF16)) for i in range(2)]
                    wpav = wpa.ap().rearrange("(s p) n -> p s n", p=128)
                    wpbv = wpb.ap().rearrange("(s p) n -> p s n", p=128)
                    woutv = wview(wout)
                    for t in range(NT):
                        tsl = slice(t * T, (t + 1) * T)
                        def load_ops(tt):
                            sl_ = slice(tt * T, (tt + 1) * T)
                            oat.acquire(SP)
                            oat.produced(SP.dma_start(out=oat.ap[:, :, :],
                                                      in_=oaT.ap().rearrange("(s p) t -> p s t", p=128)[:, :, sl_]), 16)
                            obt.acquire(SP)
                            obt.produced(SP.dma_start(out=obt.ap[:, :, :],
                                                      in_=obT.ap().rearrange("(s p) t -> p s t", p=128)[:, :, sl_]), 16)
                            for s0 in range(2):
                                sg_load(s0, sl_)

                        def sg_load(s0, sl_):
                            ga0 = sga[s0 % 2]
                            gb0 = sgb[s0 % 2]
                            ga0.acquire(SP)
                            ga0.produced(SP.dma_start(out=ga0.ap[:, :, :], in_=sgv[:, s0 * 4:(s0 + 1) * 4, sl_]), 16)
                            gb0.acquire(SP)
                            gb0.produced(SP.dma_start(out=gb0.ap[:, :, :], in_=sgv[:, DC + s0 * 4:DC + (s0 + 1) * 4, sl_]), 16)
                        sgv = sgT.ap().rearrange("(c p) t -> p c t", p=128)
                        if t == 0:
                            load_ops(0)
                        oat.wait_ready(PE)
                        obt.wait_ready(PE)
                        u.acquire(DVE)
                        for s_ in range(4):
                            ga_ = sga[s_ % 2]
                            gb_ = sgb[s_ % 2]
                            if s_ >= 2:
                                sg_load(s_, tsl)
                            sl = wstage([
                                (lambda a: a[:, 0:2048].rearrange("p (s n) -> p s n", n=512), wpav[:, :, s_ * 512:(s_ + 1) * 512]),
                                (lambda a: a[:, 2048:6144].rearrange("p (s n) -> p s n", n=512), wpbv[:, :, s_ * 512:(s_ + 1) * 512]),
                            ])
                            sl.wait_ready(PE)
                            va_ = sl.ap[:, 0:2048].rearrange("p (s n) -> p s n", n=512)
                            vb_ = sl.ap[:, 2048:6144].rearrange("p (s n) -> p s n", n=512)
                            for i in range(4):
                                db = s_ * 4 + i
                                ba = banks[db % 2]
                                bb = banks[2 + db % 2]
                                ba.acquire(PE)
                                for s2 in range(4):
                                    m = PE.matmul(ba.ap[:, :], lhsT=va_[:, s2, i * 128:(i + 1) * 128], rhs=oat.ap[:, s2, :],
                                                  start=(s2 == 0), stop=(s2 == 3))
                                ba.produced(m)
                                bb.acquire(PE)
                                for s2 in range(8):
                                    m = PE.matmul(bb.ap[:, :], lhsT=vb_[:, s2, i * 128:(i + 1) * 128], rhs=obt.ap[:, s2, :],
                                                  start=(s2 == 0), stop=(s2 == 7))
                                bb.produced(m)
                                if i == 3:
                                    sl.consumed(m)
                                    if s_ == 3:
                                        oat.consumed(m)
                                        obt.consumed(m)
                                t1, t2 = tmps[0], tmps[1]
                                ga_.wait_ready(DVE)
                                gb_.wait_ready(DVE)
                                t1.acquire(DVE)
                                ba.wait_ready(DVE)
                                i_ = DVE.tensor_tensor(out=t1.ap[:, :], in0=ba.ap[:, :], in1=ga_.ap[:, i, :], op=ALU.mult)
                                ba.consumed(i_)
                                t1.produced(i_)
                                t2.acquire(DVE)
                                bb.wait_ready(DVE)
                                i_ = DVE.tensor_tensor(out=t2.ap[:, :], in0=bb.ap[:, :], in1=gb_.ap[:, i, :], op=ALU.mult)
                                bb.consumed(i_)
                                t2.produced(i_)
                                if i == 3:
                                    ga_.consumed(i_)
                                    gb_.consumed(i_)
                                t1.wait_ready(DVE)
                                t2.wait_ready(DVE)
                                i_ = DVE.tensor_tensor(out=u.ap[:, db, :], in0=t1.ap[:, :], in1=t2.ap[:, :], op=ALU.add)
                                t1.consumed(i_)
                                t2.consumed(i_)
                                u.produced(i_)
                        xt.acquire(SP)
                        xt.produced(SP.dma_start(out=xt.ap[:, :, :],
                                                 in_=h1T.ap().rearrange("(c p) t -> p c t", p=128)[:, :, tsl]), 16)
                        if t + 1 < NT:
                            load_ops(t + 1)
                        u.wait_ready(PE)
                        for s_ in range(4):
                            sl = wstage([(lambda a: a[:, :].rearrange("p (k n) -> p k n", n=512),
                                          woutv[:, :, s_ * 512:(s_ + 1) * 512])])
                            sl.wait_ready(PE)
                            sv = sl.ap[:, :].rearrange("p (k n) -> p k n", n=512)
                            for i in range(4):
                                db = s_ * 4 + i
                                yb = banks[4 + db % 2]
                                yb.acquire(PE)
                                for kc in range(DC):
                                    m = PE.matmul(yb.ap[:, :], lhsT=sv[:, kc, i * 128:(i + 1) * 128], rhs=u.ap[:, kc, :],
                                                  start=(kc == 0), stop=(kc == DC - 1))
                                yb.produced(m)
                                if i == 3:
                                    sl.consumed(m)
                                    if s_ == 3:
                                        u.consumed(m)
                                yb.wait_ready(DVE)
                                xt.acquire(DVE)
                                d_ = DVE.scalar_tensor_tensor(out=xt.ap[:, db, :], in0=yb.ap[:, :],
                                                              scalar=tabG[:, DC + db:DC + db + 1],
                                                              in1=xt.ap[:, db, :], op0=ALU.mult, op1=ALU.add)
                                yb.consumed(d_)
                                xt.produced(d_)
                        ffn(2, w1b, w3b, w2b)
                        chunks = [(xt.ap[:, dc, :], xt.wait_ready, xt.consumed) for dc in range(DC)]
                        rms_rstd(None, chunks, D, banks[6], split=True)
                        rstd.wait_ready(DVE)
                        xt.acquire(DVE)
                        for dc in range(DC):
                            d_ = DVE.scalar_tensor_tensor(out=xt.ap[:, dc, :], in0=xt.ap[:, dc, :],
                                                          scalar=gtab[:, 3 * DC + dc:3 * DC + dc + 1],
                                                          in1=rstd.ap[:, :], op0=ALU.mult, op1=ALU.mult)
                            xt.produced(d_)
                        rstd.consumed(d_)
                        xt.wait_ready(SP)
                        dm = SP.dma_start(out=yT.ap().rearrange("(c p) t -> p c t", p=128)[:, :, tsl], in_=xt.ap[:, :, :])
                        xt.consumed(dm, 16)
                        st.inc(dm, 16)

        run_token_phases(True)
        if STOP == "ab":
            raise _Stop()

        st.wait(GP)
        flush_deferred()
        for e in (PE, ACT, DVE, SP):
            st.wait(e)

        if STOP == "kv":
            raise _Stop()
        def attn_tail(ob, db_, dst_ap, rden, osta):
            db_.wait_ready(ACT)
            rden.acquire(ACT)
            a_ = ACT.activation(out=rden.ap[:, :], in_=db_.ap[:, :], func=AF.Ln)
            db_.consumed(a_)
            lnev = Evs("lnev")
            lnev.add(a_)
            lnev.wait(ACT)
            a_ = ACT.activation(out=rden.ap[:, :], in_=rden.ap[:, :], func=AF.Exp, scale=-1.0)
            rden.produced(a_)
            rden.wait_ready(DVE)
            ob.wait_ready(DVE)
            osta.acquire(DVE)
            i_ = DVE.tensor_tensor(out=osta.ap[:, :], in0=ob.ap[:, :], in1=rden.ap[:, :], op=ALU.mult)
            ob.consumed(i_)
            rden.consumed(i_)
            osta.produced(i_)
            osta.wait_ready(SP)
            dm = SP.dma_start(out=dst_ap, in_=osta.ap[:, :])
            osta.consumed(dm, 16)
            st.inc(dm, 16)

        with ExitStack() as e2:
            def sbl(name, shape, dt):
                return e2.enter_context(nc.sbuf_tensor(name, list(shape), dt))
            kbt = sbl("kbt", [128, 2, S], BF16)
            vbt = sbl("vbt", [128, 64, 256], BF16)
            qbt = sbl("qbt", [128, 8, TOK], BF16)
            spairs = [Buf(nc, f"bank{2 * i}", psum_all[:, i * 1024:(i + 1) * 1024]) for i in range(2)]
            pps = [Buf(nc, ("sq%d" % i if i < 2 else "sil0"), sbl(f"pp{i}", [128, 2 * T], BF16)) for i in range(3)]
            rden = Buf(nc, "rstd", sbl("rdenb", [128, T], F32))
            ostb = [Buf(nc, f"ost{i}", sbl(f"ostb{i}", [128, T], BF16)) for i in range(2)]
            paccs = [Buf(nc, f"tmp{i}", sbl(f"pacc{i}", [128, T], F32)) for i in range(2)]
            accw = [[sbl(f"accw{i}{j}", [128, 2 * T], F32) for j in range(2)] for i in range(2)]
            ccb.wait(SP)
            ldv = Evs("ldv")
            ldb1 = Evs("ldb1")

            def k_load(kvh, evs):
                for tt in range(NT):
                    evs.inc(SP.dma_start(
                        out=kbt[:, kvh, :].rearrange("p (r t c) -> p r t c", r=4, t=NT)[:, :, tt, :],
                        in_=kbGt[tt].ap().rearrange("(r h p) c -> h p r c", h=2, p=128)[kvh]), 16)
            qv = qbT.ap().rearrange("(h p) t -> p h t", p=128)
            k_load(0, ldb)
            ldb.inc(SP.dma_start(out=qbt[:, 0:1, :], in_=qv[:, 0:1, :]), 16)
            for tt in range(NT):
                for r in range(4):
                    ldv.inc(SP.dma_start(
                        out=vbt[:, r * 16 + tt * 4:r * 16 + tt * 4 + 4, :],
                        in_=vbGt[tt].ap()[r * T:(r + 1) * T, :].rearrange("(b p) c -> p b c", p=128)), 16)
            ldq = Evs("ldq")
            ldq.inc(SP.dma_start(out=qbt[:, 1:4, :], in_=qv[:, 1:4, :]), 16)
            k_load(1, ldb1)
            ldb1.inc(SP.dma_start(out=qbt[:, 4:8, :], in_=qv[:, 4:8, :]), 16)
            ldv.wait(GP)
            ldq.wait(GP)
            ldb.wait(GP)
            ldb1.wait(GP)
            for c in range(6):
                cca.inc(GP.collective_compute("AllGather", ALU.bypass, replica_groups=groups,
                                              ins=[kaTc[c].ap().opt()], outs=[kaGc[c].ap().opt()]))
            for c in range(8):
                cca.inc(GP.collective_compute("AllGather", ALU.bypass, replica_groups=groups,
                                              ins=[vaLc[c].ap().opt()], outs=[vaGc[c].ap().opt()]))

            ldb.wait(PE)
            def emit_halo():
                cca.wait(SP)
                cca.wait(GP)
                pid = GP.partition_id()
                prv = (pid + 3) % 4
                nxt = (pid + 1) % 4
                pid2 = SP.partition_id()
                prv2 = (pid2 + 3) % 4
                nxt2 = (pid2 + 1) % 4
                hal = Evs("hal")
                for c in range(6):
                    kx = kaGc[c].ap().rearrange("(r x) t -> x r t", r=4)
                    hal.inc(GP.dma_start(out=hKp.ap()[c * 256:(c + 1) * 256], in_=kx[:, bass.ds(prv, 1), 1024:2048]), 16)
                    hal.inc(GP.dma_start(out=hKn.ap()[c * 256:(c + 1) * 256], in_=kx[:, bass.ds(nxt, 1), 0:1024]), 16)
                hal2 = Evs("hal2")
                for c in range(4):
                    vp = vaGc[4 + c].ap().rearrange("(r t) c -> t r c", r=4)
                    vn = vaGc[c].ap().rearrange("(r t) c -> t r c", r=4)
                    hal2.inc(SP.dma_start(out=hVp.ap()[c * 256:(c + 1) * 256], in_=vp[:, bass.ds(prv2, 1), :]), 16)
                    hal2.inc(SP.dma_start(out=hVn.ap()[c * 256:(c + 1) * 256], in_=vn[:, bass.ds(nxt2, 1), :]), 16)

                return hal, hal2
            halo_evs = [None]
            pend_tail = [None]
            it = 0
            for hq in range(8):
                kv = hq // 4
                if hq == 1:
                    ldq.wait(PE)
                if hq == 4:
                    ldb1.wait(PE)
                if hq == 6:
                    halo_evs[0] = emit_halo()
                for qt in range(NT):
                    tsl = slice(qt * T, (qt + 1) * T)
                    ob = banks[4 + it % 2]
                    db_ = banks[6 + it % 2]
                    ob.acquire(PE)
                    db_.acquire(PE)
                    NKB = S // 128

                    NP = NKB // 2
                    pacc = paccs[it % 2]
                    accs = accw[it % 2]
                    chain = [Evs("chain0"), Evs("chain1")]

                    def s_mm(j):
                        sp = spairs[j % 2]
                        sp.acquire(PE)
                        for hh in range(2):
                            kb = 2 * j + hh
                            m = PE.matmul(sp.ap[:, hh * T:(hh + 1) * T], lhsT=kbt[:, kv, kb * 128:(kb + 1) * 128],
                                          rhs=qbt[:, hq, tsl], start=True, stop=True)
                        sp.produced(m)
                        p = pps[j % 3]
                        sp.wait_ready(ACT)
                        p.acquire(ACT)
                        a = ACT.activation(out=p.ap[:, :], in_=sp.ap[:, :], func=AF.Exp, scale=SCALE)
                        sp.consumed(a)
                        p.produced(a)

                    def pv_mm(j):
                        p = pps[j % 3]
                        ldv.wait(PE)
                        p.wait_ready(PE)
                        for hh in range(2):
                            kb = 2 * j + hh
                            m = PE.matmul(ob.ap[:, :], lhsT=vbt[:, kb, kv * 128:(kv + 1) * 128], rhs=p.ap[:, hh * T:(hh + 1) * T],
                                          start=(kb == 0), stop=(kb == NKB - 1))
                        p.consumed(m)
                        p.wait_ready(DVE)
                        acc = accs[j % 2]
                        if j < 2:
                            if j == 0:
                                pacc.acquire(DVE)
                            d_ = DVE.tensor_copy(out=acc[:, :], in_=p.ap[:, :])
                        else:
                            chain[j % 2].wait(DVE)
                            d_ = DVE.tensor_tensor(out=acc[:, :], in0=acc[:, :], in1=p.ap[:, :], op=ALU.add)
                        p.consumed(d_)
                        chain[j % 2].add(d_)
                        if j == NP - 1:
                            chain[0].wait(DVE)
                            chain[1].wait(DVE)
                            d2 = DVE.tensor_tensor(out=accs[0][:, :], in0=accs[0][:, :], in1=accs[1][:, :], op=ALU.add)
                            fin = Evs("fin")
                            fin.add(d2)
                            fin.wait(DVE)
                            d3 = DVE.tensor_tensor(out=pacc.ap[:, :], in0=accs[0][:, 0:T], in1=accs[0][:, T:2 * T], op=ALU.add)
                            pacc.produced(d3)
                        return m
                    s_mm(0)
                    s_mm(1)
                    for j in range(NP):
                        if j + 2 < NP:
                            s_mm(j + 2)
                        m = pv_mm(j)
                        if j == 2 and pend_tail[0] is not None:
                            pend_tail[0]()
                            pend_tail[0] = None
                    ob.produced(m)
                    pacc.wait_ready(PE)
                    m = PE.matmul(db_.ap[:, :], lhsT=ones_f[:, :], rhs=pacc.ap[:, :], start=True, stop=True)
                    pacc.consumed(m)
                    db_.produced(m)
                    pend_tail[0] = (lambda ob=ob, db_=db_, dst=obT.ap()[hq * 128:(hq + 1) * 128, tsl], o_=ostb[it % 2]:
                                    attn_tail(ob, db_, dst, rden, o_))
                    it += 1
            pend_tail[0]()
            pend_tail[0] = None
            banks[7].acquire(PE)
            misc.inc(PE.matmul(banks[7].ap[:, 0:1], lhsT=ones[:, :], rhs=ones[:, 0:1], start=True, stop=True))
            for e in (SP, ACT, DVE):
                misc.wait(e)
                st.wait(e)
            st.wait(PE)

        if STOP == "mb":
            raise _Stop()
        with ExitStack() as e3:
            def sbl(name, shape, dt):
                return e3.enter_context(nc.sbuf_tensor(name, list(shape), dt))
            kwin = Cur([Buf(nc, f"kwin{i}", sbl(f"kwin{i}", [128, 3, 4096], BF16)) for i in range(2)])
            vwin = Cur([Buf(nc, f"vwin{i}", sbl(f"vwin{i}", [128, 32, 384], BF16)) for i in range(2)])
            qwin = Cur([Buf(nc, f"qwin{i}", sbl(f"qwin{i}", [128, 3, TOK], BF16)) for i in range(2)])
            etb = Buf(nc, "etb", sbl("etb", [128, ETOT], BF16))
            sx = [Buf(nc, (f"tmp{i}" if i < 3 else "lnt"), sbl(f"sx{i}", [128, T], F32)) for i in range(4)]
            ps = [Buf(nc, ("sq%d" % i if i < 2 else ("sil%d" % (i - 2) if i < 4 else "hid")), sbl(f"pa{i}", [128, T], BF16)) for i in range(5)]
            rden = Buf(nc, "rstd", sbl("rdena", [128, T], F32))
            osta = [Buf(nc, f"ost{i}", sbl(f"osta{i}", [128, T], BF16)) for i in range(2)]
            hal, hal2 = halo_evs[0]
            hal2.wait(SP)
            hal.wait(SP)
            qaTv = qaT.ap().rearrange("(h p) t -> h p t", p=128)
            hKpv = hKp.ap().rearrange("(h p) o t -> h p (o t)", p=128)
            hKnv = hKn.ap().rearrange("(h p) o t -> h p (o t)", p=128)
            hVpv = hVp.ap().rearrange("(b p) o c -> p b (o c)", p=128)
            hVnv = hVn.ap().rearrange("(b p) o c -> p b (o c)", p=128)
            it = 0
            lastd = [None]

            def load_slot(h):
                for b_ in (kwin, vwin, qwin):
                    b_.sel(h % 2)
                    b_.acquire(SP)
                for g in range(3):
                    hd = 4 * g + h
                    kwin.produced(SP.dma_start(out=kwin.ap[:, g, 1024:3072],
                                               in_=kaTc[hd // 2].ap()[(hd % 2) * 128:(hd % 2 + 1) * 128, :]), 16)
                    kwin.produced(SP.dma_start(out=kwin.ap[:, g, 0:1024], in_=hKpv[hd]), 16)
                    kwin.produced(SP.dma_start(out=kwin.ap[:, g, 3072:4096], in_=hKnv[hd]), 16)
                    for c in range(8):
                        vwin.produced(SP.dma_start(
                            out=vwin.ap[:, 8 + 2 * c:10 + 2 * c, g * 128:(g + 1) * 128],
                            in_=vaLc[c].ap()[:, hd * 128:(hd + 1) * 128].rearrange("(b p) c -> p b c", p=128)), 16)
                    vwin.produced(SP.dma_start(out=vwin.ap[:, 0:8, g * 128:(g + 1) * 128],
                                               in_=hVpv[:, :, hd * 128:(hd + 1) * 128]), 16)
                    vwin.produced(SP.dma_start(out=vwin.ap[:, 24:32, g * 128:(g + 1) * 128],
                                               in_=hVnv[:, :, hd * 128:(hd + 1) * 128]), 16)
                    qwin.produced(SP.dma_start(out=qwin.ap[:, g, :], in_=qaTv[hd]), 16)

            def compute_slot(h):
                nonlocal it
                for b_ in (kwin, vwin, qwin):
                    b_.sel(h % 2)
                for b_ in (kwin, vwin, qwin):
                    b_.wait_ready(PE)
                etb.wait_ready(DVE)
                for qt in range(NT):
                    tsl = slice(qt * T, (qt + 1) * T)
                    ob = banks[3 + it % 2]
                    db_ = banks[5 + it % 2]
                    ob.acquire(PE)
                    db_.acquire(PE)
                    work = []
                    for g in range(3):
                        for kb in range(4 * qt - DMAX[g], 4 * qt + 3 + DMAX[g] + 1):
                            work.append((g, kb))
                    nw = len(work)

                    def crange(i):
                        g, kb = work[i]
                        d0 = kb - 4 * qt
                        lo = max(0, d0 - DMAX[g])
                        hi = min(3, d0 + DMAX[g])
                        return lo * 128, (hi + 1) * 128

                    def s_mm(i):
                        g, kb = work[i]
                        p_ = kb + 8
                        ca, cb = crange(i)
                        sbk = banks[(0, 1, 2, 7)[i % 4]]
                        sbk.acquire(PE)
                        m = PE.matmul(sbk.ap[:, ca:cb], lhsT=kwin.ap[:, g, p_ * 128:(p_ + 1) * 128],
                                      rhs=qwin.ap[:, g, qt * T + ca:qt * T + cb], start=True, stop=True)
                        sbk.produced(m)
                        x_ = sx[i % 4]
                        sbk.wait_ready(ACT)
                        x_.acquire(ACT)
                        a = ACT.activation(out=x_.ap[:, ca:cb], in_=sbk.ap[:, ca:cb], func=AF.Exp, scale=SCALE,
                                           bias=vtab[:, p_:p_ + 1])
                        sbk.consumed(a)
                        x_.produced(a)
                        p = ps[i % 5]
                        x_.wait_ready(DVE)
                        p.acquire(DVE)
                        d0 = kb - 4 * qt
                        c0 = EOFF[g] + (DMAXP[g] - d0) * 128
                        d_ = DVE.tensor_tensor(out=p.ap[:, ca:cb], in0=x_.ap[:, ca:cb], in1=etb.ap[:, c0 + ca:c0 + cb], op=ALU.mult)
                        x_.consumed(d_)
                        p.produced(d_)
                        lastd[0] = d_

                    def pv_mm(i):
                        g, kb = work[i]
                        p_ = kb + 8
                        ca, cb = crange(i)
                        p = ps[i % 5]
                        p.wait_ready(PE)
                        PE.matmul(ob.ap[:, ca:cb], lhsT=vwin.ap[:, p_, g * 128:(g + 1) * 128], rhs=p.ap[:, ca:cb],
                                  start=(i == 0), stop=(i == nw - 1), skip_group_check=True)
                        m = PE.matmul(db_.ap[:, ca:cb], lhsT=ones[:, :], rhs=p.ap[:, ca:cb],
                                      start=(i == 0), stop=(i == nw - 1), skip_group_check=True)
                        p.consumed(m)
                        return m
                    s_mm(0)
                    s_mm(1)
                    s_mm(2)
                    for i in range(nw):
                        if i + 3 < nw:
                            s_mm(i + 3)
                        m = pv_mm(i)
                        if i == 2 and pend_tail[0] is not None:
                            pend_tail[0]()
                            pend_tail[0] = None
                    ob.produced(m)
                    db_.produced(m)
                    if qt == NT - 1:
                        for b_ in (kwin, vwin, qwin):
                            b_.consumed(m)
                        etb.consumed(lastd[0])
                    pend_tail[0] = (lambda ob=ob, db_=db_, dst=oaT.ap()[h * 128:(h + 1) * 128, tsl], o_=osta[it % 2]:
                                    attn_tail(ob, db_, dst, rden, o_))
                    it += 1
            def load_etb(h):
                etb.acquire(SP)
                etb.produced(SP.dma_start(out=etb.ap[:, :], in_=etab_d.ap()[h]), 16)
            load_slot(0)
            load_etb(0)
            load_slot(1)
            for h in range(4):
                compute_slot(h)
                if h + 1 < 4:
                    load_etb(h + 1)
                if h + 2 < 4:
                    load_slot(h + 2)
            pend_tail[0]()
            pend_tail[0] = None
            banks[7].acquire(PE)
            misc.inc(PE.matmul(banks[7].ap[:, 0:1], lhsT=ones[:, :], rhs=ones[:, 0:1], start=True, stop=True))
            for e in (SP, ACT, DVE):
                misc.wait(e)
                st.wait(e)
            st.wait(PE)

        if STOP == "ma":
            raise _Stop()
        run_token_phases(False)
        st.wait(SP)
        for e in (PE, ACT, DVE, GP):
            st.wait(e)
    return nc


def _tab16(v):
    return np.ascontiguousarray(np.asarray(v, np.float32).reshape(DC, 128).T)


def _const_tables():
    half = 64
    inv_freq = (10000.0 ** (-np.arange(0, half, 2, dtype=np.float32) / half)).astype(np.float32)
    tpos = np.arange(S)
    row = (tpos // 64).astype(np.float32)
    col = (tpos % 64).astype(np.float32)
    ang = np.concatenate([row[:, None] * inv_freq, col[:, None] * inv_freq], axis=-1).astype(np.float32)
    cos = np.cos(ang).astype(np.float32)
    sin = np.sin(ang).astype(np.float32)
    C = np.concatenate([cos, cos], axis=1).T
    Sn = np.concatenate([-sin, sin], axis=1).T
    slopes = (2.0 ** (-8.0 * np.arange(1, 13, dtype=np.float32) / 12.0)).astype(np.float32).reshape(3, 4)
    dil = (1, 4, 16)
    et = np.zeros((4, 128, ETOT), np.float32)
    for h in range(4):
        for g in range(3):
            k = np.arange(128)[:, None]
            c = np.arange(EW[g])[None, :]
            diff = 128 * DMAXP[g] + k - c
            ok = (diff % dil[g] == 0) & (np.abs(diff) <= 64 * dil[g])
            val = np.exp(-slopes[g, h] * np.abs(diff).astype(np.float32)).astype(np.float32)
            et[h, :, EOFF[g]:EOFF[g] + EW[g]] = np.where(ok, val, 0.0)
    return np.ascontiguousarray(C), np.ascontiguousarray(Sn), et.astype(ml_dtypes.bfloat16)


def _win_layout(w_in):
    qa = np.arange(0, 1536)
    ka = np.arange(1536, 3072)
    va = np.arange(3072, 4608)
    qb0 = 4608
    kb0 = 4608 + 1024
    vb0 = kb0 + 256
    ga0 = vb0 + 256
    gb0 = ga0 + 2048

    def swap(base):
        return np.concatenate([np.arange(base + 64, base + 128), np.arange(base, base + 64)])
    cols = [qa, ka]
    for h in range(8):
        cols.append(np.arange(qb0 + h * 128, qb0 + (h + 1) * 128))
        cols.append(swap(qb0 + h * 128))
    for h in range(2):
        cols.append(np.arange(kb0 + h * 128, kb0 + (h + 1) * 128))
        cols.append(swap(kb0 + h * 128))
    cols.append(np.arange(ga0, ga0 + 2048))
    cols.append(np.arange(gb0, gb0 + 2048))
    cols.append(va)
    cols.append(np.arange(vb0, vb0 + 256))
    idx = np.concatenate(cols)
    assert idx.shape[0] == WIN_COLS
    return np.ascontiguousarray(w_in[:, idx])


def _ada_cols(g):
    blocks = list(range(g * 12, g * 12 + 12)) + list(range(48 + g * 24, 48 + g * 24 + 24))
    return np.concatenate([np.arange(j * 128, (j + 1) * 128) for j in blocks])


_NC_CACHE = {}


def kernel(x, c, w_ada, b_ada, norm_ffn1, w1_ffn1, w3_ffn1, w2_ffn1, norm_mix, w_in,
           q_norm_a, k_norm_a, q_norm_b, k_norm_b, w_branch_a, w_branch_b, w_out,
           norm_ffn2, w1_ffn2, w3_ffn2, w2_ffn2, norm_final):
    f = lambda a: np.ascontiguousarray(np.asarray(a, dtype=np.float32))
    x = f(x); c = f(c)
    w_ada0 = f(w_ada)[0]; b_ada0 = f(b_ada)[0]
    C, Sn, et = _const_tables()
    gt = np.concatenate([_tab16(f(norm_ffn1)[0]), _tab16(f(norm_mix)[0]), _tab16(f(norm_ffn2)[0]),
                         _tab16(f(norm_final))], axis=1)
    gqa, gka, gqb, gkb = f(q_norm_a)[0], f(k_norm_a)[0], f(q_norm_b)[0], f(k_norm_b)[0]
    sw = lambda v: np.concatenate([v[64:], v[:64]])
    qk = np.stack([gqa, gka, gqb, gkb, sw(gqb), sw(gkb), gqa, gqa], axis=1)
    qk = np.ascontiguousarray(qk.astype(np.float32))
    winp = _win_layout(f(w_in)[0])
    shared = {
        "w1a": f(w1_ffn1)[0], "w3a": f(w3_ffn1)[0], "w2a": f(w2_ffn1)[0],
        "w1b": f(w1_ffn2)[0], "w3b": f(w3_ffn2)[0], "w2b": f(w2_ffn2)[0],
        "win": winp, "wpa": f(w_branch_a)[0], "wpb": f(w_branch_b)[0], "wout": f(w_out)[0],
        "gtab": np.ascontiguousarray(gt), "qkg": qk, "etab": et,
    }
    in_maps = []
    for r in range(NCORES):
        b, g = r // 4, r % 4
        t0 = g * TOK
        vt = np.zeros((128, 32), np.float32)
        for p in range(32):
            gb = g * 16 - 8 + p
            if gb < 0 or gb >= 64:
                vt[:, p] = NEG
        m = dict(shared)
        m.update({
            "xT": np.ascontiguousarray(x[b, t0:t0 + TOK, :].T),
            "cT": _tab16(c[b]),
            "wada": np.ascontiguousarray(w_ada0[:, _ada_cols(g)]),
            "bada": np.ascontiguousarray(b_ada0[_ada_cols(g)].reshape(36, 128).T),
            "ropeC": np.ascontiguousarray(C[:, t0:t0 + TOK]),
            "ropeS": np.ascontiguousarray(Sn[:, t0:t0 + TOK]),
            "vtab": vt,
        })
        in_maps.append(m)
    if "nc" not in _NC_CACHE:
        _NC_CACHE["nc"] = build_program()
    nc = _NC_CACHE["nc"]
    res = run_bass_kernel_spmd(nc, in_maps, core_ids=list(range(NCORES)))
    out = np.empty((2, S, D), np.float32)
    for r in range(NCORES):
        b, g = r // 4, r % 4
        out[b, g * TOK:(g + 1) * TOK, :] = res.results[r]["yT"].T
    return out


if __name__ == "__main__":
    import time
    t0 = time.time()
    nc = build_program()
    print("build ok", time.time() - t0)
```
